# Optimizing a Trainium2 kernel written in Bass

```python
import math
import jax, jax.numpy as jnp
from jax import lax
import numpy as np

D_MODEL = 1024
BATCH = 1
SEQ = 16384
DEPTH = 4

HEAD_DIM = 64
HGRN_WIDTH = D_MODEL // 4
HGRN_HEADS = HGRN_WIDTH // HEAD_DIM
HGRN_CHUNK = 64
FOURIER_WIDTH = D_MODEL // 4
FOURIER_GROUPS = FOURIER_WIDTH // HEAD_DIM
ATTN_WIDTH = D_MODEL // 2
ATTN_HEADS = ATTN_WIDTH // HEAD_DIM
MIX_WIDTH = HGRN_WIDTH + FOURIER_WIDTH + ATTN_WIDTH
DILATED_PATTERNS = ((128, 1), (512, 4), (2048, 16))
ROPE_THETA = 10000.0
D_FF = 4 * D_MODEL
RMS_EPS = 1e-6
NEG_BIG = -1e30
IN_SPLITS = (HGRN_WIDTH, HGRN_WIDTH, HGRN_WIDTH, HGRN_WIDTH, HGRN_WIDTH, FOURIER_WIDTH, ATTN_WIDTH, ATTN_WIDTH, ATTN_WIDTH)
IN_WIDTH = sum(IN_SPLITS)
SPLIT_POINTS = [sum(IN_SPLITS[:i + 1]) for i in range(len(IN_SPLITS) - 1)]

kernel_name = "hybrid_hgrn2_fnet_dilated_attn_encoder"


def rms_norm(x, w):
    xf = x.astype(jnp.float32)
    y = xf * lax.rsqrt(jnp.mean(xf * xf, axis=-1, keepdims=True) + RMS_EPS)
    return (y * w.astype(jnp.float32)).astype(x.dtype)


def hgrn2_chunk_scan(q, k, v, log_f):
    B, H, S, dk = q.shape
    dv = v.shape[-1]
    C = HGRN_CHUNK
    n = S // C

    def chunked(t):
        return jnp.moveaxis(t.reshape(B, H, n, C, t.shape[-1]), 2, 0)

    tri = jnp.tril(jnp.ones((C, C), dtype=bool))[:, :, None]

    def step(state, inp):
        qc, kc, vc, gc = inp
        b = jnp.cumsum(gc, axis=-2)
        inter = jnp.einsum('bhtk,bhkv->bhtv', qc * jnp.exp(b), state)
        diff = b[..., :, None, :] - b[..., None, :, :]
        pair = jnp.where(tri, jnp.exp(jnp.where(tri, diff, 0.0)), 0.0)
        scores = jnp.einsum('bhtk,bhsk,bhtsk->bhts', qc, kc, pair)
        intra = jnp.einsum('bhts,bhsv->bhtv', scores, vc)
        b_last = b[..., -1:, :]
        new_state = (jnp.exp(b_last[..., 0, :])[..., None] * state
                     + jnp.einsum('bhsk,bhsv->bhkv', kc * jnp.exp(b_last - b), vc))
        return new_state, inter + intra

    init = jnp.zeros((B, H, dk, dv), dtype=jnp.float32)
    _, out = lax.scan(step, init, (chunked(q), chunked(k), chunked(v), chunked(log_f)))
    return jnp.moveaxis(out, 0, 2).reshape(B, H, S, dv)


def hgrn2_mixer(q_in, i_in, zf, zb, g_in, lb):
    B, S, W = q_in.shape
    H, dh = HGRN_HEADS, HEAD_DIM
    f32 = jnp.float32
    z = jnp.stack([zf, zb[:, ::-1]], axis=1).astype(f32)
    lbb = lb.astype(f32)[None, :, None, :]
    f = lbb + (1.0 - lbb) * jax.nn.sigmoid(z)
    log_f = jnp.log(f)
    k = 1.0 - f
    q = jax.nn.silu(q_in.astype(f32))
    v = i_in.astype(f32)
    q2 = jnp.stack([q, q[:, ::-1]], axis=1)
    v2 = jnp.stack([v, v[:, ::-1]], axis=1)

    def heads(t):
        return t.reshape(B, 2, S, H, dh).transpose(0, 1, 3, 2, 4).reshape(B, 2 * H, S, dh)

    o = hgrn2_chunk_scan(heads(q2), heads(k), heads(v2), heads(log_f)).reshape(B, 2, H, S, dh)
    o = o[:, 0] + o[:, 1][:, :, ::-1]
    o = o * lax.rsqrt(jnp.mean(o * o, axis=-1, keepdims=True) + RMS_EPS)
    o = o.transpose(0, 2, 1, 3).reshape(B, S, W)
    return o * jax.nn.silu(g_in.astype(f32))


def fourier_mixer(u):
    B, S, W = u.shape
    ug = u.astype(jnp.float32).reshape(B, S, FOURIER_GROUPS, W // FOURIER_GROUPS)
    y = jnp.fft.fft2(ug, axes=(1, 3), norm='ortho').real
    return y.reshape(B, S, W)


def rotary(t, positions):
    hd = t.shape[-1]
    inv_freq = ROPE_THETA ** (-jnp.arange(0, hd, 2, dtype=jnp.float32) / hd)
    ang = positions.astype(jnp.float32)[:, None, :, None] * inv_freq
    cos, sin = jnp.cos(ang), jnp.sin(ang)
    t1, t2 = t[..., :hd // 2], t[..., hd // 2:]
    return jnp.concatenate([t1 * cos - t2 * sin, t2 * cos + t1 * sin], axis=-1)


def dilated_window_attention(q, k, v, window, dilation):
    B, H, S, hd = q.shape
    d = dilation
    R = window // (2 * dilation)
    BLK = R
    L = S // d
    nb = -(-L // BLK)
    Lp = nb * BLK

    def sub(t):
        return t.reshape(B, H, L, d, hd).transpose(0, 1, 3, 2, 4)

    qs = jnp.pad(sub(q), ((0, 0), (0, 0), (0, 0), (0, Lp - L), (0, 0))).reshape(B, H, d, nb, BLK, hd)

    def key_blocks(t):
        tp = jnp.pad(sub(t), ((0, 0), (0, 0), (0, 0), (BLK, Lp - L + BLK), (0, 0))).reshape(B, H, d, nb + 2, BLK, hd)
        return jnp.concatenate([tp[:, :, :, :-2], tp[:, :, :, 1:-1], tp[:, :, :, 2:]], axis=-2)

    k3 = key_blocks(k)
    v3 = key_blocks(v)
    qj = jnp.arange(nb)[:, None, None] * BLK + jnp.arange(BLK)[None, :, None]
    kj = (jnp.arange(nb)[:, None, None] - 1) * BLK + jnp.arange(3 * BLK)[None, None, :]
    valid = (kj >= 0) & (kj < L) & (jnp.abs(qj - kj) <= R)
    s = jnp.einsum('bhrnqc,bhrnkc->bhrnqk', qs, k3)
    s = jnp.where(valid, s, NEG_BIG)
    lse = jax.nn.logsumexp(s, axis=-1)
    p = jnp.exp(s - lse[..., None])
    o = jnp.einsum('bhrnqk,bhrnkc->bhrnqc', p, v3)
    o = o.reshape(B, H, d, Lp, hd)[:, :, :, :L].transpose(0, 1, 3, 2, 4).reshape(B, H, S, hd)
    lse = lse.reshape(B, H, d, Lp)[..., :L].transpose(0, 1, 3, 2).reshape(B, H, S)
    return o, lse


def dilated_attention_mixer(cq, ck, cv, positions):
    B, S, W = cq.shape
    f32 = jnp.float32

    def heads(t):
        return t.astype(f32).reshape(B, S, ATTN_HEADS, HEAD_DIM).transpose(0, 2, 1, 3)

    q = rotary(heads(cq), positions) * (HEAD_DIM ** -0.5)
    k = rotary(heads(ck), positions)
    v = heads(cv)
    outs = []
    lses = []
    for window, dilation in DILATED_PATTERNS:
        o_p, lse_p = dilated_window_attention(q, k, v, window, dilation)
        outs.append(o_p)
        lses.append(lse_p)
    weights = jax.nn.softmax(jnp.stack(lses, axis=0), axis=0)
    o = jnp.sum(weights[..., None] * jnp.stack(outs, axis=0), axis=0)
    return o.transpose(0, 2, 1, 3).reshape(B, S, W)


def setup_inputs(seed: int = 0) -> dict:
    key = jax.random.key(seed)
    ks = jax.random.split(key, 10)
    f32 = jnp.float32
    x = jax.random.normal(ks[0], (BATCH, SEQ, D_MODEL), f32)
    positions = jnp.broadcast_to(jnp.arange(SEQ, dtype=jnp.int32)[None, :], (BATCH, SEQ))
    attn_norm_w = 1.0 + 0.02 * jax.random.normal(ks[1], (DEPTH, D_MODEL), f32)
    w_in = jax.random.normal(ks[2], (DEPTH, D_MODEL, IN_WIDTH), f32) * D_MODEL ** -0.5
    hgrn_lower_bounds = 0.5 * jax.random.normal(ks[3], (DEPTH, 2, HGRN_WIDTH), f32)
    w_out = jax.random.normal(ks[4], (DEPTH, MIX_WIDTH, D_MODEL), f32) * MIX_WIDTH ** -0.5
    mlp_norm_w = 1.0 + 0.02 * jax.random.normal(ks[5], (DEPTH, D_MODEL), f32)
    w_up = jax.random.normal(ks[6], (DEPTH, D_MODEL, D_FF), f32) * D_MODEL ** -0.5
    w_down = jax.random.normal(ks[7], (DEPTH, D_FF, D_MODEL), f32) * D_FF ** -0.5
    final_norm_w = 1.0 + 0.02 * jax.random.normal(ks[8], (D_MODEL,), f32)
    return {"x": x, "positions": positions, "attn_norm_w": attn_norm_w, "w_in": w_in,
            "hgrn_lower_bounds": hgrn_lower_bounds, "w_out": w_out, "mlp_norm_w": mlp_norm_w,
            "w_up": w_up, "w_down": w_down, "final_norm_w": final_norm_w}


def reference(x, positions, attn_norm_w, w_in, hgrn_lower_bounds, w_out, mlp_norm_w, w_up, w_down, final_norm_w):
    p_lb = jax.nn.softmax(hgrn_lower_bounds.astype(jnp.float32), axis=0)
    lb_all = jnp.cumsum(p_lb, axis=0) - p_lb[0:1]
    for layer in range(DEPTH):
        h = rms_norm(x, attn_norm_w[layer])
        proj = h @ w_in[layer]
        a_q, a_i, a_zf, a_zb, a_g, b_u, c_q, c_k, c_v = jnp.split(proj, SPLIT_POINTS, axis=-1)
        o_a = hgrn2_mixer(a_q, a_i, a_zf, a_zb, a_g, lb_all[layer]).astype(x.dtype)
        o_b = fourier_mixer(b_u).astype(x.dtype)
        o_c = dilated_attention_mixer(c_q, c_k, c_v, positions).astype(x.dtype)
        mixed = jnp.concatenate([o_a, o_b, o_c], axis=-1)
        x = x + mixed @ w_out[layer]
        h = rms_norm(x, mlp_norm_w[layer])
        x = x + jnp.square(jax.nn.relu(h @ w_up[layer])) @ w_down[layer]
    return rms_norm(x, final_norm_w)
```

```python
import numpy as np
import ml_dtypes
from contextlib import ExitStack
import concourse.bass as bass
import concourse.mybir as mybir
from concourse.bass_utils import run_bass_kernel_spmd

F32 = mybir.dt.float32
BF16 = mybir.dt.bfloat16
I32 = mybir.dt.int32
AF = mybir.ActivationFunctionType
ALU = mybir.AluOpType
NPBF = ml_dtypes.bfloat16

S_LEN = 16384
D = 1024
NCORE = 8
TOK = S_LEN // NCORE
EPS = 1e-6


class Buf:
    __slots__ = ("name", "w", "r", "excl")

    def __init__(self, name="", excl=False):
        self.name = name
        self.w = []
        self.r = {}
        self.excl = excl


class Sched:
    ENGS = ("pe", "act", "dve", "pool", "sp")

    def __init__(self, nc, n_dma_sems=32, strict_same=("act", "dve", "pool")):
        self.nc = nc
        self.prog = {e: [] for e in self.ENGS}
        self.cnt = {e: 0 for e in self.ENGS}
        self.known = {e: {} for e in self.ENGS}
        self.strict_same = set(strict_same)
        self.n_dma_sems = n_dma_sems
        self.dma_val = [0] * n_dma_sems
        self.q_range = {"sp": (0, n_dma_sems // 2), "pool": (n_dma_sems // 2, n_dma_sems * 3 // 4),
                        "act": (n_dma_sems * 3 // 4, n_dma_sems)}
        self.dma_rr = {q: r[0] for q, r in self.q_range.items()}
        self.out_events = []

    def _need(self, eng, ev):
        kind, key, val = ev
        if kind == "eng" and key == eng and eng not in self.strict_same:
            return
        k = (kind, key)
        if self.known[eng].get(k, 0) >= val:
            return
        self.known[eng][k] = val
        self.prog[eng].append(("wait", k, val))

    def _deps(self, eng, reads, writes, parts):
        for b in reads:
            for ev in b.w:
                self._need(eng, ev)
            if b.excl:
                for k, ev in b.r.items():
                    if k != eng:
                        self._need(eng, ev)
        for b in writes:
            for ev in b.w:
                self._need(eng, ev)
            for ev in b.r.values():
                self._need(eng, ev)
        for b in parts:
            for ev in b.r.values():
                self._need(eng, ev)

    def _mark(self, rkey, ev, reads, writes, parts):
        for b in reads:
            b.r[rkey] = ev
        for b in writes:
            b.w = [ev]
            b.r = {}
        for b in parts:
            b.w.append(ev)

    def op(self, eng, fn, reads=(), writes=(), parts=(), inc=True):
        self._deps(eng, reads, writes, parts)
        if inc:
            self.cnt[eng] += 1
            ev = ("eng", eng, self.cnt[eng])
        else:
            ev = ("eng", eng, self.cnt[eng] + 1)
        self.prog[eng].append(("op", fn, inc))
        self._mark(eng, ev, reads, writes, parts)
        return ev

    def dma(self, q, fn, reads=(), writes=(), parts=(), is_output=False):
        self._deps(q, reads, writes, parts)
        i = self.dma_rr[q]
        lo, hi = self.q_range[q]
        self.dma_rr[q] = lo + (i + 1 - lo) % (hi - lo)
        if self.dma_val[i] > 0:
            self._need(q, ("dma", i, self.dma_val[i]))
        self.dma_val[i] += 16
        ev = ("dma", i, self.dma_val[i])
        self.prog[q].append(("dma", fn, i))
        self._mark(("dma", i), ev, reads, writes, parts)
        if is_output:
            self.out_events.append(ev)
        return ev

    def finish(self, q="sp"):
        for ev in reversed(self.out_events):
            self._need(q, ev)
        self.out_events = []

    def emit(self):
        nc = self.nc
        with ExitStack() as st:
            esem = {e: st.enter_context(nc.semaphore("s_" + e)) for e in self.ENGS}
            dsem = [st.enter_context(nc.semaphore("d%d" % i)) for i in range(self.n_dma_sems)]
            block = st.enter_context(nc.Block())

            def run(eng_name):
                def body(eng):
                    for item in self.prog[eng_name]:
                        if item[0] == "wait":
                            (kind, key), val = item[1], item[2]
                            eng.wait_ge(esem[key] if kind == "eng" else dsem[key], val)
                        elif item[0] == "op":
                            ins = item[1](eng)
                            if item[2]:
                                ins.then_inc(esem[eng_name], 1)
                        else:
                            item[1](eng).then_inc(dsem[item[2]], 16)
                return body

            block.tensor(run("pe"))
            block.scalar(run("act"))
            block.vector(run("dve"))
            block.gpsimd(run("pool"))
            block.sync(run("sp"))


class Ctx:
    def __init__(self):
        self.nc = bass.Bass("TRN2", target_bir_lowering=False)
        self.S = Sched(self.nc)
        self.st = ExitStack()
        self.psum = []
        self.pb = []
        self.ps_rr = 0

    def sb(self, name, shape, dt):
        return self.st.enter_context(self.nc.sbuf_tensor(name, list(shape), dt))

    def din(self, name, shape, dt):
        return self.nc.dram_tensor(name, list(shape), dt, kind="ExternalInput").ap()

    def dout(self, name, shape, dt):
        return self.nc.dram_tensor(name, list(shape), dt, kind="ExternalOutput").ap()

    def dscratch(self, name, shape, dt):
        return self.nc.dram_tensor(name, list(shape), dt).ap()

    def init_psum(self, n=8):
        for i in range(n):
            self.psum.append(self.st.enter_context(self.nc.psum_tensor("ps%d" % i, [128, 512], F32)))
            self.pb.append(Buf("ps%d" % i, excl=True))

    def next_psum(self):
        i = self.ps_rr
        self.ps_rr = (self.ps_rr + 1) % len(self.psum)
        return self.psum[i], self.pb[i]

    def done(self):
        self.S.finish("sp")
        self.S.emit()
        self.st.close()
        return self.nc


def build_dense(do_tail, do_proj, do_final):
    C = Ctx()
    nc, S = C.nc, C.S
    T, TB = TOK, 512
    NB = T // TB
    xT = C.din("xT", [D, T], F32)
    if do_tail:
        oafT = C.din("oafT", [256, T], F32)
        oabT = C.din("oabT", [256, T], F32)
        gT = C.din("gT", [256, T], BF16)
        obT = C.din("obT", [256, T], BF16)
        ocT = C.din("ocT", [512, T], BF16)
        w_out = C.din("w_out", [1024, 1024], F32)
        w_up = C.din("w_up", [1024, 4096], F32)
        w_down = C.din("w_down", [4096, 1024], F32)
        mlp_nw = C.din("mlp_nw", [128, 8], F32)
        ws_out = C.dscratch("ws_out", [128, 8, 1024], BF16)
        ws_up = C.dscratch("ws_up", [128, 8, 4096], BF16)
        ws_down = C.dscratch("ws_down", [128, 32, 1024], BF16)
        if not do_final:
            xoT = C.dout("xoT", [D, T], F32)
    if do_proj:
        w_in = C.din("w_in", [1024, 3072], F32)
        attn_nw = C.din("attn_nw", [128, 8], F32)
        ws_in = C.dscratch("ws_in", [128, 8, 3072], BF16)
        projT = C.dout("projT", [3072, T], BF16)
        zT = C.dout("zT", [512, T], F32)
    if do_final:
        fin_nw = C.din("fin_nw", [128, 8], F32)
        yT = C.dout("yT", [D, T], F32)

    C.init_psum(8)
    ones_t = C.sb("ones_t", [128, 128], BF16); b_ones = Buf()
    blk_t = C.sb("blk_t", [128, 128], BF16); b_blk = Buf()
    S.op("pool", lambda e: e.memset(ones_t[:], 1.0 / 1024.0), writes=[b_ones])
    S.op("pool", lambda e: e.memset(blk_t[:], 0.0), writes=[b_blk])
    S.op("pool", lambda e: e.memset(blk_t[0:64, 0:64], 1.0 / 64.0), writes=[b_blk])
    S.op("pool", lambda e: e.memset(blk_t[64:128, 64:128], 1.0 / 64.0), writes=[b_blk])
    nw_tiles = {}
    for nm, ap_ in (("mlp", mlp_nw if do_tail else None), ("attn", attn_nw if do_proj else None),
                    ("fin", fin_nw if do_final else None)):
        if ap_ is None:
            continue
        t = C.sb("nw_" + nm, [128, 8], F32); b = Buf()
        S.dma("sp", lambda e, t=t, a=ap_: e.dma_start(out=t[:], in_=a[:, :]), writes=[b])
        nw_tiles[nm] = (t, b)

    stg = [C.sb("stg%d" % i, [128, 2048], F32) for i in range(2)]
    stg16 = [C.sb("stg16_%d" % i, [128, 2048], BF16) for i in range(2)]
    b_stg = [Buf() for _ in range(2)]
    b_stg16 = [Buf() for _ in range(2)]
    conv_i = [0]
    conv_engs = ("dve", "pool", "act")

    def convert_weight(W, Ws, K, N):
        bW = Buf()
        for kc in range(K // 128):
            for n0 in range(0, N, 2048):
                nn = min(2048, N - n0)
                i = conv_i[0] % 2
                eng = conv_engs[conv_i[0] % 3]
                conv_i[0] += 1
                S.dma("sp", lambda e, i=i, kc=kc, n0=n0, nn=nn: e.dma_start(
                    out=stg[i][:, :nn], in_=W[kc * 128:(kc + 1) * 128, n0:n0 + nn]), writes=[b_stg[i]])
                if eng == "act":
                    S.op("act", lambda e, i=i, nn=nn: e.activation(stg16[i][:, :nn], stg[i][:, :nn], AF.Copy),
                         reads=[b_stg[i]], writes=[b_stg16[i]])
                else:
                    S.op(eng, lambda e, i=i, nn=nn: e.tensor_copy(stg16[i][:, :nn], stg[i][:, :nn]),
                         reads=[b_stg[i]], writes=[b_stg16[i]])
                S.dma("pool", lambda e, i=i, kc=kc, n0=n0, nn=nn: e.dma_start(
                    out=Ws[:, kc, n0:n0 + nn], in_=stg16[i][:, :nn]), reads=[b_stg16[i]], parts=[bW])
        return bW

    wl = []
    if do_tail:
        b_wout = convert_weight(w_out, ws_out, 1024, 1024)
        b_wup = convert_weight(w_up, ws_up, 1024, 4096)
        b_wdown = convert_weight(w_down, ws_down, 4096, 1024)
    if do_proj:
        b_win = convert_weight(w_in, ws_in, 1024, 3072)

    xb = C.sb("xb", [128, 8, TB], F32); bx = [Buf() for _ in range(8)]
    hT = C.sb("hT", [128, 8, TB], BF16); bh = [Buf() for _ in range(8)]
    sq = C.sb("sq", [128, 8, TB], BF16); bsq = Buf()
    rstd = C.sb("rstd", [128, TB], F32); brstd = Buf()
    sd = C.sb("sd", [128, TB], F32); bsd = Buf()
    pan = [C.sb("pan%d" % i, [128, 32, 512], BF16) for i in range(2)]
    bpan = [Buf() for _ in range(2)]
    if do_tail:
        mixT = C.sb("mixT", [128, 8, TB], BF16); bmix = [Buf() for _ in range(8)]
        actT = C.sb("actT", [128, 32, TB], BF16); bact = [Buf() for _ in range(32)]
        oa = C.sb("oa", [128, 2, TB], F32); boa = Buf()
        oa2 = C.sb("oa2", [128, 2, TB], F32); boa2 = Buf()
        gg = C.sb("gg", [128, 2, TB], BF16); bgg = Buf()
        sqa = C.sb("sqa", [128, 2, TB], BF16); bsqa = Buf()
        sg = C.sb("sg", [128, TB], F32); bsg = Buf()
        tmpa = C.sb("tmpa", [128, TB], F32); btmpa = Buf()
        usq = [C.sb("usq%d" % i, [128, TB], F32) for i in range(2)]; busq = [Buf() for _ in range(2)]
    if do_proj:
        ost = [C.sb("ost%d" % i, [128, TB], BF16) for i in range(3)]; bost = [Buf() for _ in range(3)]
        ostf = [C.sb("ostf%d" % i, [128, TB], F32) for i in range(2)]; bostf = [Buf() for _ in range(2)]
    if do_final:
        yst = [C.sb("yst%d" % i, [128, TB], F32) for i in range(2)]; byst = [Buf() for _ in range(2)]

    pan_i = [0]
    cnts = {"usq": 0, "ost": 0, "ostf": 0, "yst": 0}

    def rmsnorm(nw, out_fn):
        nwt, bnw = nw
        S.op("act", lambda e: e.activation(sq[:].rearrange("p a t -> p (a t)"), xb[:].rearrange("p a t -> p (a t)"),
                                           AF.Square), reads=bx, writes=[bsq])
        ps, bps = C.next_psum()
        for kc in range(8):
            S.op("pe", lambda e, kc=kc, ps=ps: e.matmul(ps[:, :TB], ones_t[:], sq[:, kc, :], start=(kc == 0),
                                                      stop=(kc == 7)),
                 reads=[b_ones, bsq], writes=[bps] if kc == 0 else (), parts=[bps] if kc > 0 else (), inc=(kc == 7))
        S.op("act", lambda e, ps=ps: e.activation(sd[:], ps[:, :TB], AF.Sqrt, bias=EPS), reads=[bps], writes=[bsd])
        S.op("dve", lambda e: e.reciprocal(rstd[:], sd[:]), reads=[bsd], writes=[brstd])
        for kc in range(8):
            out_fn(kc, nwt[:, kc:kc + 1], bnw)

    def norm_to_h(nw):
        def out_fn(kc, sc, bnw):
            S.op("dve", lambda e, kc=kc, sc=sc: e.scalar_tensor_tensor(hT[:, kc, :], xb[:, kc, :], sc, rstd[:],
                                                                      ALU.mult, ALU.mult),
                 reads=[bx[kc], brstd, bnw], writes=[bh[kc]])
        rmsnorm(nw, out_fn)

    def matmul_phase(Ws, bW, KC, N, in_tile, in_bufs, evac):
        npan = N // 512
        jobs = list(range(npan))

        def load(p):
            i = pan_i[0] % 2
            pan_i[0] += 1
            S.dma("sp", lambda e, i=i, p=p: e.dma_start(out=pan[i][:, :KC, :], in_=Ws[:, :, p * 512:(p + 1) * 512]),
                  reads=[bW], writes=[bpan[i]])
            return i
        cur = load(0)
        for p in jobs:
            nxt = load(p + 1) if p + 1 < npan else None
            for jj in range(4):
                j = p * 4 + jj
                ps, bps = C.next_psum()
                for kc in range(KC):
                    S.op("pe", lambda e, ps=ps, cur=cur, kc=kc, jj=jj: e.matmul(
                        ps[:, :TB], pan[cur][:, kc, jj * 128:(jj + 1) * 128], in_tile[:, kc, :],
                        start=(kc == 0), stop=(kc == KC - 1)),
                        reads=[bpan[cur], in_bufs[kc]], writes=[bps] if kc == 0 else (),
                        parts=[bps] if kc > 0 else (), inc=(kc == KC - 1))
                evac(j, ps, bps)
            cur = nxt

    for tb in range(NB):
        tsl = slice(tb * TB, (tb + 1) * TB)
        S.dma("sp", lambda e, tsl=tsl: e.dma_start(out=xb[:], in_=xT.rearrange("(kc p) t -> p kc t", p=128)[:, :, tsl]),
              writes=bx)
        if do_tail:
            S.dma("sp", lambda e, tsl=tsl: e.dma_start(out=oa[:], in_=oafT.rearrange("(kc p) t -> p kc t", p=128)[:, :, tsl]),
                  writes=[boa])
            S.dma("sp", lambda e, tsl=tsl: e.dma_start(out=oa2[:], in_=oabT.rearrange("(kc p) t -> p kc t", p=128)[:, :, tsl]),
                  writes=[boa2])
            S.op("dve", lambda e: e.tensor_tensor(oa[:], oa[:], oa2[:], ALU.add), reads=[boa, boa2], writes=[boa])
            S.dma("sp", lambda e, tsl=tsl: e.dma_start(out=gg[:], in_=gT.rearrange("(kc p) t -> p kc t", p=128)[:, :, tsl]),
                  writes=[bgg])
            S.dma("sp", lambda e, tsl=tsl: e.dma_start(out=mixT[:, 2:4, :],
                                                       in_=obT.rearrange("(kc p) t -> p kc t", p=128)[:, :, tsl]),
                  writes=bmix[2:4])
            S.dma("sp", lambda e, tsl=tsl: e.dma_start(out=mixT[:, 4:8, :],
                                                       in_=ocT.rearrange("(kc p) t -> p kc t", p=128)[:, :, tsl]),
                  writes=bmix[4:8])
            S.op("act", lambda e: e.activation(sqa[:].rearrange("p a t -> p (a t)"), oa[:].rearrange("p a t -> p (a t)"),
                                               AF.Square), reads=[boa], writes=[bsqa])
            for c in range(2):
                ps, bps = C.next_psum()
                S.op("pe", lambda e, ps=ps, c=c: e.matmul(ps[:, :TB], blk_t[:], sqa[:, c, :], start=True, stop=True),
                     reads=[b_blk, bsqa], writes=[bps])
                S.op("act", lambda e, ps=ps: e.activation(sd[:], ps[:, :TB], AF.Sqrt, bias=EPS), reads=[bps], writes=[bsd])
                S.op("dve", lambda e: e.reciprocal(rstd[:], sd[:]), reads=[bsd], writes=[brstd])
                S.op("act", lambda e, c=c: e.activation(sg[:], gg[:, c, :], AF.Silu), reads=[bgg], writes=[bsg])
                S.op("dve", lambda e, c=c: e.tensor_tensor(tmpa[:], oa[:, c, :], rstd[:], ALU.mult),
                     reads=[boa, brstd], writes=[btmpa])
                S.op("dve", lambda e, c=c: e.tensor_tensor(mixT[:, c, :], tmpa[:], sg[:], ALU.mult),
                     reads=[btmpa, bsg], writes=[bmix[c]])

            def evac_res(j, ps, bps):
                S.op("dve", lambda e, j=j, ps=ps: e.tensor_tensor(xb[:, j, :], xb[:, j, :], ps[:, :TB], ALU.add),
                     reads=[bps, bx[j]], writes=[bx[j]])

            def evac_up(j, ps, bps):
                i = cnts["usq"] % 2
                cnts["usq"] += 1
                S.op("act", lambda e, i=i, ps=ps: e.activation(usq[i][:], ps[:, :TB], AF.Square), reads=[bps],
                     writes=[busq[i]])
                S.op("dve", lambda e, i=i, ps=ps, j=j: e.scalar_tensor_tensor(actT[:, j, :], ps[:, :TB], 0.0, usq[i][:],
                                                                            ALU.is_gt, ALU.mult),
                     reads=[bps, busq[i]], writes=[bact[j]])

            matmul_phase(ws_out, b_wout, 8, 1024, mixT, bmix, evac_res)
            norm_to_h(nw_tiles["mlp"])
            matmul_phase(ws_up, b_wup, 8, 4096, hT, bh, evac_up)
            matmul_phase(ws_down, b_wdown, 32, 1024, actT, bact, evac_res)
            if not do_final:
                S.dma("pool", lambda e, tsl=tsl: e.dma_start(out=xoT.rearrange("(kc p) t -> p kc t", p=128)[:, :, tsl],
                                                           in_=xb[:]), reads=bx, is_output=True)
        if do_final:
            def out_fn(kc, sc, bnw):
                i = cnts["yst"] % 2
                cnts["yst"] += 1
                S.op("dve", lambda e, kc=kc, sc=sc, i=i: e.scalar_tensor_tensor(yst[i][:], xb[:, kc, :], sc, rstd[:],
                                                                              ALU.mult, ALU.mult),
                     reads=[bx[kc], brstd, bnw], writes=[byst[i]])
                S.dma("pool", lambda e, kc=kc, i=i, tsl=tsl: e.dma_start(out=yT[kc * 128:(kc + 1) * 128, tsl], in_=yst[i][:]),
                      reads=[byst[i]], is_output=True)
            rmsnorm(nw_tiles["fin"], out_fn)
        if do_proj:
            norm_to_h(nw_tiles["attn"])

            def evac_proj(j, ps, bps):
                i = cnts["ost"] % 3
                cnts["ost"] += 1
                if 4 <= j < 8:
                    k = cnts["ostf"] % 2
                    cnts["ostf"] += 1
                    S.op("act", lambda e, k=k, ps=ps: e.activation(ostf[k][:], ps[:, :TB], AF.Copy), reads=[bps],
                         writes=[bostf[k]])
                    S.op("dve", lambda e, k=k, i=i: e.tensor_copy(ost[i][:], ostf[k][:]), reads=[bostf[k]],
                         writes=[bost[i]])
                    S.dma("pool", lambda e, k=k, j=j, tsl=tsl: e.dma_start(out=zT[(j - 4) * 128:(j - 3) * 128, tsl],
                                                                         in_=ostf[k][:]),
                          reads=[bostf[k]], is_output=True)
                else:
                    S.op("act", lambda e, i=i, ps=ps: e.activation(ost[i][:], ps[:, :TB], AF.Copy), reads=[bps],
                         writes=[bost[i]])
                S.dma("pool", lambda e, i=i, j=j, tsl=tsl: e.dma_start(out=projT[j * 128:(j + 1) * 128, tsl], in_=ost[i][:]),
                      reads=[bost[i]], is_output=True)
            matmul_phase(ws_in, b_win, 8, 3072, hT, bh, evac_proj)
    return C.done()


PATTERN_D = (1, 4, 16)
QPAD = 2048
KPAD = 1024
SEG = 4096
TWO_PI = 6.283185307179586
CW1 = 6.28125
CW2 = TWO_PI - CW1
MAGIC = 12582912.0
PI_SAFE = 3.1415925


def attn_tile_base():
    base, b = {}, 0
    for d in PATTERN_D:
        base[d] = b
        b += d * (S_LEN // d // 128 + 1)
    return base, b


def build_attn():
    C = Ctx()
    nc, S = C.nc, C.S
    base, NT = attn_tile_base()
    qT = C.din("qT", [64, S_LEN], BF16)
    kT = C.din("kT", [64, S_LEN], BF16)
    vaug = C.din("vaug", [128, NT, 65], BF16)
    posb = C.din("posb", [64, S_LEN], I32)
    invf = C.din("invf", [64, 1], F32)
    psw_d = C.din("psw", [64, 64], BF16)
    mask_d = C.din("maskab", [128, 256], BF16)
    ident_d = C.din("ident", [128, 128], BF16)
    sel_d = C.din("sel", [65, 64], F32)
    oT = C.dout("oT", [64, S_LEN], BF16)
    C.init_psum(8)

    qTp = C.sb("qTp", [64, QPAD + S_LEN + QPAD], BF16); bq = Buf()
    kTp = C.sb("kTp", [64, KPAD + S_LEN + KPAD], BF16); bk = Buf()
    V = C.sb("V", [128, NT, 65], BF16); bV = Buf()
    acc = C.sb("acc", [65, SEG], F32); bacc = Buf()
    invf_t = C.sb("invf_t", [64, 1], F32); binvf = Buf()
    psw = C.sb("psw_t", [64, 64], BF16); bpsw = Buf()
    maskab = C.sb("mask_t", [128, 256], BF16); bmask = Buf()
    ident = C.sb("ident_t", [128, 128], BF16); bident = Buf()
    sel = C.sb("sel_t", [65, 64], F32); bsel = Buf()
    for t, b, a in ((invf_t, binvf, invf), (psw, bpsw, psw_d), (maskab, bmask, mask_d), (ident, bident, ident_d),
                    (sel, bsel, sel_d)):
        S.dma("sp", lambda e, t=t, a=a: e.dma_start(out=t[:], in_=a[:, :]), writes=[b])
    for i0 in range(0, NT, 81):
        S.dma("sp", lambda e, i0=i0: e.dma_start(out=V[:, i0:i0 + 81, :], in_=vaug[:, i0:i0 + 81, :]), parts=[bV])
    S.op("pool", lambda e: e.memset(V[:, :, 64:65], 1.0), reads=[bV], parts=[bV])
    for d in PATTERN_D:
        nt = S_LEN // d // 128 + 1
        S.op("pool", lambda e, d=d, nt=nt: e.memset(V[0:64, base[d]:base[d] + d * nt:nt, 64:65], 0.0), parts=[bV])
        S.op("pool", lambda e, d=d, nt=nt: e.memset(V[64:128, base[d] + nt - 1:base[d] + d * nt:nt, 64:65], 0.0), parts=[bV])
    S.op("pool", lambda e: e.memset(qTp[:, 0:QPAD], 0.0), parts=[bq])
    S.op("pool", lambda e: e.memset(qTp[:, QPAD + S_LEN:], 0.0), parts=[bq])
    S.op("pool", lambda e: e.memset(kTp[:, 0:KPAD], 0.0), parts=[bk])
    S.op("pool", lambda e: e.memset(kTp[:, KPAD + S_LEN:], 0.0), parts=[bk])

    CH = 1024
    pos_i = C.sb("pos_i", [64, CH], I32); bpos = Buf()
    ang = C.sb("ang", [64, CH], F32); bang = Buf()
    nn_ = C.sb("nn", [64, CH], F32); bnn = Buf()
    yy = C.sb("yy", [64, CH], F32); byy = Buf()
    sin_t = C.sb("sin_t", [64, CH], F32); bsin = Buf()
    cos_t = C.sb("cos_t", [64, CH], F32); bcos = Buf()
    raw = [C.sb("raw%d" % i, [64, CH], BF16) for i in range(2)]; braw = [Buf() for _ in range(2)]
    t1 = C.sb("t1", [64, CH], F32); bt1 = Buf()
    t2 = C.sb("t2", [64, CH], F32); bt2 = Buf()
    for ch in range(S_LEN // CH):
        csl = slice(ch * CH, (ch + 1) * CH)
        S.dma("sp", lambda e, csl=csl: e.dma_start(out=pos_i[:], in_=posb[:, csl]), writes=[bpos])
        S.op("dve", lambda e: e.tensor_copy(ang[:], pos_i[:]), reads=[bpos], writes=[bang])
        S.op("dve", lambda e: e.tensor_scalar(ang[:], ang[:], invf_t[:, 0:1], None, ALU.mult), reads=[bang, binvf],
             writes=[bang])
        S.op("dve", lambda e: e.tensor_scalar(nn_[:], ang[:], 1.0 / TWO_PI, MAGIC, ALU.mult, ALU.add), reads=[bang],
             writes=[bnn])
        S.op("dve", lambda e: e.tensor_scalar(nn_[:], nn_[:], -MAGIC, None, ALU.add), reads=[bnn], writes=[bnn])
        S.op("dve", lambda e: e.scalar_tensor_tensor(yy[:], nn_[:], -CW1, ang[:], ALU.mult, ALU.add), reads=[bnn, bang],
             writes=[byy])
        S.op("dve", lambda e: e.scalar_tensor_tensor(yy[:], nn_[:], -CW2, yy[:], ALU.mult, ALU.add), reads=[bnn, byy],
             writes=[byy])
        S.op("dve", lambda e: e.tensor_scalar(yy[:], yy[:], PI_SAFE, -PI_SAFE, ALU.min, ALU.max), reads=[byy],
             writes=[byy])
        S.op("act", lambda e: e.activation(sin_t[:], yy[:], AF.Sin), reads=[byy], writes=[bsin])
        S.op("dve", lambda e: e.scalar_tensor_tensor(nn_[:], yy[:], -1.0, yy[:], ALU.mult, ALU.min), reads=[byy],
             writes=[bnn])
        S.op("dve", lambda e: e.tensor_scalar(yy[:], nn_[:], 1.5707963, None, ALU.add), reads=[bnn], writes=[byy])
        S.op("act", lambda e: e.activation(cos_t[:], yy[:], AF.Sin), reads=[byy], writes=[bcos])
        for which, (src, dstt, bdst, pad) in enumerate(((qT, qTp, bq, QPAD), (kT, kTp, bk, KPAD))):
            rw, brw = raw[which], braw[which]
            S.dma("sp", lambda e, rw=rw, src=src, csl=csl: e.dma_start(out=rw[:], in_=src[:, csl]), writes=[brw])
            pss = []
            for b4 in range(CH // 512):
                ps, bps = C.next_psum()
                S.op("pe", lambda e, ps=ps, rw=rw, b4=b4: e.matmul(ps[0:64, :], psw[:], rw[:, b4 * 512:(b4 + 1) * 512],
                                                                 start=True, stop=True),
                     reads=[bpsw, brw], writes=[bps])
                pss.append((ps, bps))
            S.op("dve", lambda e, rw=rw: e.tensor_tensor(t1[:], rw[:], cos_t[:], ALU.mult), reads=[brw, bcos],
                 writes=[bt1])
            for b4 in range(CH // 512):
                ps, bps = pss[b4]
                S.op("dve", lambda e, ps=ps, b4=b4: e.tensor_tensor(t2[:, b4 * 512:(b4 + 1) * 512], ps[0:64, :],
                                                                  sin_t[:, b4 * 512:(b4 + 1) * 512], ALU.mult),
                     reads=[bps, bsin], writes=[bt2] if b4 == 0 else (), parts=[bt2] if b4 else ())
            S.op("pool", lambda e, dstt=dstt, pad=pad, ch=ch: e.tensor_tensor(
                dstt[:, pad + ch * CH:pad + (ch + 1) * CH], t1[:], t2[:], ALU.add), reads=[bt1, bt2], parts=[bdst])

    NP_ = 4
    Pt = [C.sb("P%d" % i, [128, 512], BF16) for i in range(NP_)]; bP = [Buf() for _ in range(NP_)]
    p_rr = [0]
    rec = C.sb("rec", [64, 512], F32); brec = Buf()
    ost = [C.sb("ost%d" % i, [64, SEG], BF16) for i in range(2)]; bost = [Buf() for _ in range(2)]
    for seg in range(S_LEN // SEG):
        for d in PATTERN_D:
            L = S_LEN // d
            nt = L // 128 + 1
            nq = SEG // d // 128
            m0 = nq * seg
            for r in range(d):
                def kcols(mp):
                    st_ = KPAD + r + d * (128 * mp - 64)
                    return slice(st_, st_ + 127 * d + 1, d)

                def qcols(mp):
                    st_ = QPAD + r + d * 128 * (mp - 1)
                    return slice(st_, st_ + 255 * d + 1, d)
                npair = (nq + 2) // 2
                ptiles = {}
                out_ps = None
                for u in range(npair):
                    ps, bps = C.next_psum()
                    slot = p_rr[0] % NP_
                    p_rr[0] += 1
                    ntile = 0
                    for s_ in range(2):
                        rel = 2 * u + s_
                        if rel > nq:
                            break
                        mp = m0 + rel
                        S.op("pe", lambda e, ps=ps, s_=s_, kc=kcols(mp), qc=qcols(mp): e.matmul(
                            ps[:, s_ * 256:(s_ + 1) * 256], kTp[:, kc], qTp[:, qc], start=True, stop=False),
                            reads=[bk, bq], writes=[bps] if s_ == 0 else (), parts=[bps] if s_ else (), inc=False)
                        S.op("pe", lambda e, ps=ps, s_=s_: e.matmul(ps[:, s_ * 256:(s_ + 1) * 256], ident[:], maskab[:],
                                                                   start=False, stop=True),
                             reads=[bident, bmask], parts=[bps], inc=True)
                        ptiles[rel] = (slot, s_)
                        ntile += 1
                    w = ntile * 256
                    S.op("act", lambda e, ps=ps, slot=slot, w=w: e.activation(Pt[slot][:, :w], ps[:, :w], AF.Exp, scale=0.125),
                         reads=[bps], writes=[bP[slot]])
                    for q in (2 * u - 1, 2 * u):
                        if q < 0 or q >= nq or (q + 1) not in ptiles:
                            continue
                        wq = q % 4
                        if wq == 0:
                            out_ps = C.next_psum()
                        ops, bops = out_ps
                        sa, ha = ptiles[q]
                        sb_, hb = ptiles[q + 1]
                        ia = base[d] + r * nt + m0 + q
                        S.op("pe", lambda e, ops=ops, wq=wq, ia=ia, sa=sa, ha=ha: e.matmul(
                            ops[0:65, wq * 128:(wq + 1) * 128], V[:, ia, :], Pt[sa][:, ha * 256 + 128:ha * 256 + 256],
                            start=True, stop=False),
                            reads=[bV, bP[sa]], writes=[bops] if wq == 0 else (), parts=[bops] if wq else (), inc=False)
                        S.op("pe", lambda e, ops=ops, wq=wq, ia=ia, sb_=sb_, hb=hb: e.matmul(
                            ops[0:65, wq * 128:(wq + 1) * 128], V[:, ia + 1, :], Pt[sb_][:, hb * 256:hb * 256 + 128],
                            start=False, stop=True),
                            reads=[bV, bP[sb_]], parts=[bops], inc=True)
                        if wq == 3 or q == nq - 1:
                            nqt = wq + 1
                            q0 = q - wq
                            t0 = r + d * 128 * q0
                            asl = slice(t0, t0 + (nqt * 128 - 1) * d + 1, d)
                            if d == 1:
                                S.op("dve", lambda e, ops=ops, asl=asl, nqt=nqt: e.tensor_copy(acc[:, asl], ops[0:65, :nqt * 128]),
                                     reads=[bops], parts=[bacc])
                            else:
                                S.op("dve", lambda e, ops=ops, asl=asl, nqt=nqt: e.tensor_tensor(
                                    acc[:, asl], acc[:, asl], ops[0:65, :nqt * 128], ALU.add),
                                    reads=[bops, bacc], writes=[bacc])
        oi = seg % 2
        for b8 in range(SEG // 512):
            ps, bps = C.next_psum()
            S.op("pe", lambda e, ps=ps, b8=b8: e.matmul(ps[0:64, :], sel[:], acc[:, b8 * 512:(b8 + 1) * 512], start=True,
                                                      stop=True), reads=[bsel, bacc], writes=[bps])
            S.op("dve", lambda e, ps=ps: e.reciprocal(rec[:], ps[0:64, :]), reads=[bps], writes=[brec])
            S.op("dve", lambda e, b8=b8, oi=oi: e.tensor_tensor(ost[oi][:, b8 * 512:(b8 + 1) * 512],
                                                              acc[0:64, b8 * 512:(b8 + 1) * 512], rec[:], ALU.mult),
                 reads=[bacc, brec], writes=[bost[oi]] if b8 == 0 else (), parts=[bost[oi]] if b8 else ())
        S.dma("pool", lambda e, oi=oi, seg=seg: e.dma_start(out=oT[:, seg * SEG:(seg + 1) * SEG], in_=ost[oi][:]),
              reads=[bost[oi]], is_output=True)
    return C.done()


def attn_consts():
    inv = (np.float32(10000.0) ** (-(np.arange(0, 64, 2, dtype=np.float32) / np.float32(64.0)))).astype(np.float32)
    invf = np.concatenate([-inv, inv]).reshape(64, 1).astype(np.float32)
    psw = np.zeros((64, 64), np.float32)
    for c in range(64):
        psw[(c + 32) % 64, c] = 1.0
    p = np.arange(128)[:, None]
    c = np.arange(128)[None, :]
    NEG = -30000.0
    maskab = np.concatenate([np.where(c >= p, 0.0, NEG), np.where(c <= p, 0.0, NEG)], axis=1).astype(np.float32)
    sel = np.zeros((65, 64), np.float32)
    sel[64, :] = 1.0
    return {"invf": invf, "psw": psw.astype(NPBF), "maskab": maskab.astype(NPBF),
            "ident": np.eye(128, dtype=np.float32).astype(NPBF), "sel": sel}


def attn_v_layout(v):
    base, NT = attn_tile_base()
    out = np.zeros((128, NT, 65), dtype=v.dtype)
    p = np.arange(128)
    for d in PATTERN_D:
        L = S_LEN // d
        nt = L // 128 + 1
        for r in range(d):
            for mp in range(nt):
                j = 128 * mp - 64 + p
                ok = (j >= 0) & (j < L)
                tok = r + d * j[ok]
                out[p[ok], base[d] + r * nt + mp, :64] = v[tok]
    return out


def build_fft():
    C = Ctx()
    nc, S = C.nc, C.S
    uT = C.din("uT", [64, S_LEN], BF16)
    cs64_d = C.din("cs64", [64, 128], BF16)
    r1a_d = C.din("r1a", [128, 128], BF16)
    r1b_d = C.din("r1b", [128, 128], BF16)
    c128_d = C.din("c128", [128, 128], BF16)
    s128_d = C.din("s128", [128, 128], BF16)
    tr_d = C.din("tw_r", [128, 64], F32)
    ti_d = C.din("tw_i", [128, 64], F32)
    yh = C.dout("yh", [128, 64, 64], BF16)
    C.init_psum(8)
    u = C.sb("u", [64, S_LEN], BF16); bu = Buf()
    consts = {}
    for nm, a, shp, dt in (("cs64", cs64_d, [64, 128], BF16), ("r1a", r1a_d, [128, 128], BF16),
                           ("r1b", r1b_d, [128, 128], BF16), ("c128", c128_d, [128, 128], BF16),
                           ("s128", s128_d, [128, 128], BF16), ("tr", tr_d, [128, 64], F32), ("ti", ti_d, [128, 64], F32)):
        t = C.sb("k_" + nm, shp, dt); b = Buf()
        S.dma("sp", lambda e, t=t, a=a: e.dma_start(out=t[:], in_=a[:, :]), writes=[b])
        consts[nm] = (t, b)
    for i in range(4):
        S.dma("sp", lambda e, i=i: e.dma_start(out=u[:, i * 4096:(i + 1) * 4096], in_=uT[:, i * 4096:(i + 1) * 4096]),
              parts=[bu])
    Vall = C.sb("Vall", [128, 128, 2, 64], BF16); bVall = Buf()
    Pp = C.sb("Pp", [128, 2, 64, 64], BF16); bPp = Buf()
    ysb = C.sb("ysb", [128, 64, 64], BF16); bys = Buf()
    tmp = [C.sb("ftmp%d" % i, [128, 4, 64], F32) for i in range(4)]; btmp = [Buf() for _ in range(4)]
    cs64, bcs = consts["cs64"]
    for g4 in range(32):
        ps, bps = C.next_psum()
        for i in range(4):
            s1 = g4 * 4 + i
            S.op("pe", lambda e, ps=ps, i=i, s1=s1: e.matmul(ps[:, i * 128:(i + 1) * 128],
                                                           u[:, s1:s1 + 127 * 128 + 1:128], cs64[:], start=True, stop=True),
                 reads=[bu, bcs], writes=[bps] if i == 0 else (), parts=[bps] if i else (), inc=(i == 3))
        dst = Vall[:, g4 * 4:(g4 + 1) * 4, :, :].rearrange("p a r c -> p (a r c)")
        if g4 % 2 == 0:
            S.op("act", lambda e, ps=ps, dst=dst: e.activation(dst, ps[:, :], AF.Copy), reads=[bps], parts=[bVall])
        else:
            S.op("dve", lambda e, ps=ps, dst=dst: e.tensor_copy(dst, ps[:, :]), reads=[bps], parts=[bVall])
    r1a, br1a = consts["r1a"]
    r1b, br1b = consts["r1b"]
    tr, btr = consts["tr"]
    ti, bti = consts["ti"]
    trb = tr[:, None, :].to_broadcast([128, 4, 64])
    tib = ti[:, None, :].to_broadcast([128, 4, 64])
    for g4 in range(16):
        ps, bps = C.next_psum()
        for i in range(4):
            cp = g4 * 4 + i
            S.op("pe", lambda e, ps=ps, i=i, cp=cp: e.matmul(ps[:, i * 128:(i + 1) * 128], Vall[:, :, 0, cp], r1a[:],
                                                           start=True, stop=False),
                 reads=[bVall, br1a], writes=[bps] if i == 0 else (), parts=[bps] if i else (), inc=False)
            S.op("pe", lambda e, ps=ps, i=i, cp=cp: e.matmul(ps[:, i * 128:(i + 1) * 128], Vall[:, :, 1, cp], r1b[:],
                                                           start=False, stop=True),
                 reads=[bVall, br1b], parts=[bps], inc=(i == 3))
        pv = ps[:, :].rearrange("p (c r k) -> p c r k", c=4, r=2)
        pr, pi = pv[:, :, 0, :], pv[:, :, 1, :]
        S.op("dve", lambda e, pr=pr: e.tensor_tensor(tmp[0][:], pr, trb, ALU.mult), reads=[bps, btr], writes=[btmp[0]])
        S.op("dve", lambda e, pi=pi: e.tensor_tensor(tmp[1][:], pi, tib, ALU.mult), reads=[bps, bti], writes=[btmp[1]])
        S.op("dve", lambda e, pr=pr: e.tensor_tensor(tmp[2][:], pr, tib, ALU.mult), reads=[bps, bti], writes=[btmp[2]])
        S.op("dve", lambda e, pi=pi: e.tensor_tensor(tmp[3][:], pi, trb, ALU.mult), reads=[bps, btr], writes=[btmp[3]])
        csl = slice(g4 * 4, (g4 + 1) * 4)
        S.op("pool", lambda e, csl=csl: e.tensor_tensor(Pp[:, 0, csl, :], tmp[0][:], tmp[1][:], ALU.subtract),
             reads=[btmp[0], btmp[1]], parts=[bPp])
        S.op("pool", lambda e, csl=csl: e.tensor_tensor(Pp[:, 1, csl, :], tmp[2][:], tmp[3][:], ALU.add),
             reads=[btmp[2], btmp[3]], parts=[bPp])
    c128, bc128 = consts["c128"]
    s128, bs128 = consts["s128"]
    for b8 in range(8):
        ps, bps = C.next_psum()
        csl = slice(b8 * 8, (b8 + 1) * 8)
        S.op("pe", lambda e, ps=ps, csl=csl: e.matmul(ps[:, :], c128[:], Pp[:, 0, csl, :].rearrange("p c k -> p (c k)"),
                                                    start=True, stop=False), reads=[bPp, bc128], writes=[bps], inc=False)
        S.op("pe", lambda e, ps=ps, csl=csl: e.matmul(ps[:, :], s128[:], Pp[:, 1, csl, :].rearrange("p c k -> p (c k)"),
                                                    start=False, stop=True), reads=[bPp, bs128], parts=[bps])
        S.op("act", lambda e, ps=ps, csl=csl: e.activation(ysb[:, :, csl].rearrange("p k c -> p c k"),
                                                        ps[:, :].rearrange("p (c k) -> p c k", c=8), AF.Copy,
                                                        scale=1.0 / 1024.0), reads=[bps], parts=[bys])
    S.dma("pool", lambda e: e.dma_start(out=yh[:, :, :], in_=ysb[:]), reads=[bys], is_output=True)
    return C.done()


def fft_consts(hh):
    f64 = np.float64
    c = np.arange(64)
    a64 = 2 * np.pi * np.outer(c, c) / 64
    cs64 = np.concatenate([np.cos(a64), -np.sin(a64)], axis=1)
    n = np.arange(128)
    a128 = 2 * np.pi * np.outer(n, n) / 128
    C128, S128 = np.cos(a128), np.sin(a128)
    k2 = np.arange(64) + 64 * hh
    Ch, Sh = C128[:, k2], S128[:, k2]
    r1a = np.concatenate([Ch, -Sh], axis=1)
    r1b = np.concatenate([Sh, Ch], axis=1)
    th = 2 * np.pi * np.outer(n, k2) / S_LEN
    return {"cs64": cs64.astype(np.float32).astype(NPBF), "r1a": r1a.astype(np.float32).astype(NPBF),
            "r1b": r1b.astype(np.float32).astype(NPBF), "c128": C128.astype(np.float32).astype(NPBF),
            "s128": S128.astype(np.float32).astype(NPBF), "tw_r": np.cos(th).astype(np.float32),
            "tw_i": (-np.sin(th)).astype(np.float32)}


def build_hgrn():
    C = Ctx()
    nc, S = C.nc, C.S
    T = S_LEN
    TBK = 1024
    NBK = T // TBK
    NCH = T // 64
    NPAIR = T // 128
    din = {}
    for dr in ("f", "b"):
        din["q" + dr] = C.din("qT_" + dr, [64, T], BF16)
        din["z" + dr] = C.din("zT_" + dr, [64, T], F32)
        din["v" + dr] = C.din("vtok_" + dr, [128, NPAIR, 32], BF16)
        din["o" + dr] = C.dout("oT_" + dr, [32, T], F32)
    lbraw_d = C.din("lbraw", [64, 2, 4], F32)
    lbmask_d = C.din("lbmask", [64, 2, 4], F32)
    maskT_d = C.din("maskT", [128, 128], BF16)
    identf_d = C.din("identf", [64, 64], F32)
    C.init_psum(8)
    lbraw = C.sb("lbraw_t", [64, 2, 4], F32); blbraw = Buf()
    lbmask = C.sb("lbmask_t", [64, 2, 4], F32); blbmask = Buf()
    maskT = C.sb("maskT_t", [128, 128], BF16); bmaskT = Buf()
    identf = C.sb("identf_t", [64, 64], F32); bidentf = Buf()
    for t, b, a in ((lbraw, blbraw, lbraw_d), (lbmask, blbmask, lbmask_d)):
        S.dma("sp", lambda e, t=t, a=a: e.dma_start(out=t[:], in_=a[:, :, :]), writes=[b])
    for t, b, a in ((maskT, bmaskT, maskT_d), (identf, bidentf, identf_d)):
        S.dma("sp", lambda e, t=t, a=a: e.dma_start(out=t[:], in_=a[:, :]), writes=[b])
    lbe = C.sb("lbe", [64, 2, 4], F32); blbe = Buf()
    lbs = C.sb("lbs", [64, 2], F32); blbs = Buf()
    lbn = C.sb("lbn", [64, 2], F32); blbn = Buf()
    lb = C.sb("lb", [64, 2], F32); blb = Buf()
    oml = C.sb("oml", [64, 2], F32); boml = Buf()
    S.op("act", lambda e: e.activation(lbe[:], lbraw[:], AF.Exp), reads=[blbraw], writes=[blbe])
    S.op("dve", lambda e: e.reduce_sum(lbs[:], lbe[:], mybir.AxisListType.X), reads=[blbe], writes=[blbs])
    S.op("dve", lambda e: e.tensor_tensor(lbe[:], lbe[:], lbmask[:], ALU.mult), reads=[blbe, blbmask], writes=[blbe])
    S.op("dve", lambda e: e.reduce_sum(lbn[:], lbe[:], mybir.AxisListType.X), reads=[blbe], writes=[blbn])
    S.op("dve", lambda e: e.reciprocal(lbs[:], lbs[:]), reads=[blbs], writes=[blbs])
    S.op("dve", lambda e: e.tensor_tensor(lb[:], lbn[:], lbs[:], ALU.mult), reads=[blbn, blbs], writes=[blb])
    S.op("dve", lambda e: e.tensor_scalar(oml[:], lb[:], -1.0, 1.0, ALU.mult, ALU.add), reads=[blb], writes=[boml])
    rmask = C.sb("rmask", [64, TBK], F32); brmask = Buf()
    S.op("pool", lambda e: e.memset(rmask[:], 1.0), writes=[brmask])
    S.op("pool", lambda e: e.memset(rmask[:, 0:TBK:64], 0.0), writes=[brmask])
    Qt = C.sb("Qt", [64, T], BF16); bQt = Buf()
    Kt = C.sb("Kt", [64, T], BF16); bKt = Buf()
    Ktok = C.sb("Ktok", [128, NPAIR, 64], BF16); bKtok = Buf()
    Vtok = C.sb("Vtok", [128, NPAIR, 32], BF16); bVtok = Buf()
    U = C.sb("U", [64, NCH, 32], F32); bU = Buf()
    Sd = C.sb("Sd", [64, NCH, 32], BF16); bSd = Buf()
    Dd = C.sb("Dd", [64, NCH], F32); bDd = Buf()
    a1 = C.sb("a1", [64, TBK], F32); ba1 = Buf()
    a2 = C.sb("a2", [64, TBK], F32); ba2 = Buf()
    a3 = C.sb("a3", [64, TBK], F32); ba3 = Buf()
    a4 = C.sb("a4", [64, TBK], F32); ba4 = Buf()
    kf = C.sb("kf", [64, TBK], F32); bkf = Buf()
    qraw = C.sb("qraw", [64, TBK], BF16); bqraw = Buf()
    qs = C.sb("qs", [64, TBK], F32); bqs = Buf()
    Am = [C.sb("Am%d" % i, [128, 512], BF16) for i in range(2)]; bAm = [Buf() for _ in range(2)]
    osb = [C.sb("osb%d" % i, [32, 512], F32) for i in range(2)]; bosb = [Buf() for _ in range(2)]
    nck = TBK // 64
    am_i = [0]
    for di, dr in enumerate(("f", "b")):
        qd, zd, vd, od = din["q" + dr], din["z" + dr], din["v" + dr], din["o" + dr]
        lbc, omlc = lb[:, di:di + 1], oml[:, di:di + 1]
        S.dma("sp", lambda e, vd=vd: e.dma_start(out=Vtok[:], in_=vd[:, :, :]), writes=[bVtok])
        for bk in range(NBK):
            tsl = slice(bk * TBK, (bk + 1) * TBK)
            S.dma("sp", lambda e, zd=zd, tsl=tsl: e.dma_start(out=a1[:], in_=zd[:, tsl]), writes=[ba1])
            S.dma("sp", lambda e, qd=qd, tsl=tsl: e.dma_start(out=qraw[:], in_=qd[:, tsl]), writes=[bqraw])
            S.op("act", lambda e: e.activation(a1[:], a1[:], AF.Sigmoid), reads=[ba1], writes=[ba1])
            S.op("dve", lambda e, omlc=omlc, lbc=lbc: e.tensor_scalar(a1[:], a1[:], omlc, lbc, ALU.mult, ALU.add),
                 reads=[ba1, blb, boml], writes=[ba1])
            S.op("act", lambda e: e.activation(a2[:], a1[:], AF.Ln), reads=[ba1], writes=[ba2])
            S.op("pool", lambda e: e.tensor_scalar(a3[:], a1[:], -1.0, 1.0, ALU.mult, ALU.add), reads=[ba1],
                 writes=[ba3])
            S.op("dve", lambda e: e.tensor_tensor_scan(a4[:], rmask[:], a2[:], 0.0, ALU.mult, ALU.add),
                 reads=[brmask, ba2], writes=[ba4])
            blast = a4[:, 63:TBK:64]
            S.op("act", lambda e, bk=bk, blast=blast: e.activation(Dd[:, bk * nck:(bk + 1) * nck], blast, AF.Exp),
                 reads=[ba4], parts=[bDd])
            S.op("dve", lambda e, blast=blast: e.tensor_tensor(
                a2[:].rearrange("p (c t) -> p c t", t=64), a4[:].rearrange("p (c t) -> p c t", t=64),
                blast[:, :, None].to_broadcast([64, nck, 64]), ALU.subtract), reads=[ba4], writes=[ba2])
            S.op("act", lambda e: e.activation(a1[:], a2[:], AF.Exp), reads=[ba2], writes=[ba1])
            S.op("act", lambda e: e.activation(a4[:], a2[:], AF.Exp, scale=-1.0), reads=[ba2], writes=[ba4])
            S.op("act", lambda e: e.activation(qs[:], qraw[:], AF.Silu), reads=[bqraw], writes=[bqs])
            S.op("dve", lambda e, tsl=tsl: e.tensor_tensor(Qt[:, tsl], qs[:], a1[:], ALU.mult), reads=[bqs, ba1],
                 parts=[bQt])
            S.op("dve", lambda e: e.tensor_tensor(kf[:], a3[:], a4[:], ALU.mult), reads=[ba3, ba4], writes=[bkf])
            S.op("pool", lambda e, tsl=tsl: e.tensor_copy(Kt[:, tsl], kf[:]), reads=[bkf], parts=[bKt])
            ps, bps = C.next_psum()
            for i in range(TBK // 128):
                S.op("pe", lambda e, ps=ps, i=i: e.transpose(ps[:, i * 64:(i + 1) * 64], kf[:, i * 128:(i + 1) * 128],
                                                            identf[:]),
                     reads=[bkf, bidentf], writes=[bps] if i == 0 else (), parts=[bps] if i else (),
                     inc=(i == TBK // 128 - 1))
            pr0 = bk * (TBK // 128)
            S.op("act", lambda e, ps=ps, pr0=pr0: e.activation(
                Ktok[:, pr0:pr0 + TBK // 128, :].rearrange("p a k -> p (a k)"), ps[:, :TBK // 2], AF.Copy),
                reads=[bps], parts=[bKtok])
        for g in range(NCH // 32):
            banks = [C.next_psum(), C.next_psum()]
            for i in range(16):
                for half in range(2):
                    ps, bps = banks[half]
                    pr = g * 16 + i
                    psl = slice(half * 64, half * 64 + 64)
                    S.op("pe", lambda e, ps=ps, i=i, pr=pr, psl=psl: e.matmul(ps[0:64, i * 32:(i + 1) * 32], Ktok[psl, pr, :],
                                                                            Vtok[psl, pr, :], start=True, stop=True),
                         reads=[bKtok, bVtok], writes=[bps] if i == 0 else (), parts=[bps] if i else (), inc=(i == 15))
            for half in range(2):
                ps, bps = banks[half]
                S.op("dve", lambda e, ps=ps, g=g, half=half: e.tensor_copy(
                    U[:, g * 32 + half:g * 32 + 32:2, :], ps[0:64, :].rearrange("p (c v) -> p c v", v=32)),
                    reads=[bps], parts=[bU])
        S.op("pool", lambda e: e.memset(Sd[:, 0, :], 0.0), reads=[bSd], parts=[bSd])
        for v in range(32):
            S.op("dve", lambda e, v=v: e.tensor_tensor_scan(Sd[:, 1:NCH, v], U[:, 0:NCH - 1, v], Dd[:, 1:NCH], 0.0,
                                                           ALU.add, ALU.mult),
                 reads=[bU, bDd], parts=[bSd])
        for g in range(NPAIR // 4):
            ps, bps = C.next_psum()
            for i in range(4):
                pr = g * 4 + i
                tsl = slice(pr * 128, (pr + 1) * 128)
                S.op("pe", lambda e, ps=ps, i=i, tsl=tsl: e.matmul(ps[:, i * 128:(i + 1) * 128], Kt[:, tsl], Qt[:, tsl],
                                                                 start=True, stop=True),
                     reads=[bKt, bQt], writes=[bps] if i == 0 else (), parts=[bps] if i else (), inc=(i == 3))
            ai = am_i[0] % 2
            am_i[0] += 1
            S.op("dve", lambda e, ps=ps, ai=ai: e.tensor_tensor(
                Am[ai][:].rearrange("p (a t) -> p a t", a=4), ps[:, :].rearrange("p (a t) -> p a t", a=4),
                maskT[:, None, :].to_broadcast([128, 4, 128]), ALU.mult), reads=[bps, bmaskT], writes=[bAm[ai]])
            po, bpo = C.next_psum()
            for i in range(4):
                pr = g * 4 + i
                S.op("pe", lambda e, po=po, i=i, pr=pr, ai=ai: e.matmul(po[0:32, i * 128:(i + 1) * 128], Vtok[:, pr, :],
                                                                      Am[ai][:, i * 128:(i + 1) * 128], start=True,
                                                                      stop=False),
                     reads=[bVtok, bAm[ai]], writes=[bpo] if i == 0 else (), parts=[bpo] if i else (), inc=False)
                for h in range(2):
                    c = pr * 2 + h
                    tq = slice(pr * 128 + h * 64, pr * 128 + h * 64 + 64)
                    S.op("pe", lambda e, po=po, i=i, h=h, c=c, tq=tq: e.matmul(
                        po[0:32, i * 128 + h * 64:i * 128 + h * 64 + 64], Sd[:, c, :], Qt[:, tq], start=False,
                        stop=(h == 1)), reads=[bSd, bQt], parts=[bpo], inc=(i == 3 and h == 1))
            S.op("act", lambda e, po=po, ai=ai: e.activation(osb[ai][:], po[0:32, :], AF.Copy), reads=[bpo],
                 writes=[bosb[ai]])
            S.dma("pool", lambda e, od=od, g=g, ai=ai: e.dma_start(out=od[:, g * 512:(g + 1) * 512], in_=osb[ai][:]),
                  reads=[bosb[ai]], is_output=True)
    return C.done()


def hgrn_consts(layer):
    s = np.arange(128)[:, None]
    t = np.arange(128)[None, :]
    maskT = ((s // 64 == t // 64) & (s <= t)).astype(np.float32)
    lbmask = np.zeros((64, 2, 4), np.float32)
    lbmask[:, :, 1:layer + 1] = 1.0
    return {"maskT": maskT.astype(NPBF), "identf": np.eye(64, dtype=np.float32), "lbmask": lbmask}


_PROGS = {}


def _prog(key, fn):
    if key not in _PROGS:
        _PROGS[key] = fn()
    return _PROGS[key]


def _run(nc, in_maps):
    return run_bass_kernel_spmd(nc, in_maps, core_ids=list(range(NCORE))).results


def _nw_layout(w):
    return np.ascontiguousarray(np.asarray(w, np.float32).reshape(8, 128).T)


def kernel(x, positions, attn_norm_w, w_in, hgrn_lower_bounds, w_out, mlp_norm_w, w_up, w_down, final_norm_w):
    x = np.asarray(x, np.float32)[0]
    pos = np.asarray(positions)[0].astype(np.int32)
    depth = w_in.shape[0]
    T = TOK
    cat = np.concatenate
    xT_sh = [np.ascontiguousarray(x[c * T:(c + 1) * T].T) for c in range(NCORE)]
    acon = attn_consts()
    posb = np.ascontiguousarray(np.broadcast_to(pos[None, :], (64, S_LEN)))
    fcon = [fft_consts(hh) for hh in range(2)]
    nc = _prog("proj", lambda: build_dense(False, True, False))
    res = _run(nc, [{"xT": xT_sh[c], "w_in": np.asarray(w_in[0], np.float32), "attn_nw": _nw_layout(attn_norm_w[0])}
                    for c in range(NCORE)])
    out = None
    for layer in range(depth):
        projT = cat([r["projT"] for r in res], axis=1)
        zT = cat([r["zT"] for r in res], axis=1)
        if layer > 0:
            xT_sh = [r["xoT"] for r in res]
        nc = _prog("attn", build_attn)
        ins = []
        for h in range(8):
            v = np.ascontiguousarray(projT[2560 + 64 * h:2560 + 64 * h + 64].T)
            ins.append({"qT": np.ascontiguousarray(projT[1536 + 64 * h:1536 + 64 * h + 64]),
                        "kT": np.ascontiguousarray(projT[2048 + 64 * h:2048 + 64 * h + 64]),
                        "vaug": attn_v_layout(v), "posb": posb, **acon})
        ra = _run(nc, ins)
        ocT = cat([r["oT"] for r in ra], axis=0)
        nc = _prog("fft", build_fft)
        ins = []
        for c in range(8):
            g, hh = c // 2, c % 2
            ins.append({"uT": np.ascontiguousarray(projT[1280 + 64 * g:1280 + 64 * g + 64]), **fcon[hh]})
        rf = _run(nc, ins)
        ob = np.zeros((S_LEN, 256), dtype=projT.dtype)
        for c in range(8):
            g, hh = c // 2, c % 2
            yh = rf[c]["yh"]
            ob.reshape(128, 2, 64, 256)[:, hh, :, 64 * g:64 * g + 64] = yh
        obT = np.ascontiguousarray(ob.T)
        nc = _prog("hgrn", build_hgrn)
        hcon = hgrn_consts(layer)
        ins = []
        lbr = np.asarray(hgrn_lower_bounds, np.float32)
        for c in range(8):
            h, vh = c // 2, c % 2
            q = projT[64 * h:64 * h + 64]
            iv = projT[256 + 64 * h + 32 * vh:256 + 64 * h + 32 * vh + 32]
            zf = zT[64 * h:64 * h + 64]
            zb = zT[256 + 64 * h:256 + 64 * h + 64]

            def vtok(vT):
                return np.ascontiguousarray(vT.T.reshape(S_LEN // 128, 128, 32).transpose(1, 0, 2))
            ins.append({"qT_f": np.ascontiguousarray(q), "zT_f": np.ascontiguousarray(zf), "vtok_f": vtok(iv),
                        "qT_b": np.ascontiguousarray(q[:, ::-1]), "zT_b": np.ascontiguousarray(zb[:, ::-1]),
                        "vtok_b": vtok(iv[:, ::-1]),
                        "lbraw": np.ascontiguousarray(lbr[:, :, 64 * h:64 * h + 64].transpose(2, 1, 0)), **hcon})
        rh = _run(nc, ins)
        oafT = cat([r["oT_f"] for r in rh], axis=0)
        oabT = np.ascontiguousarray(cat([r["oT_b"] for r in rh], axis=0)[:, ::-1])
        gT = projT[1024:1280]
        last = layer == depth - 1
        nc = _prog("tail_final" if last else "tail_proj", lambda: build_dense(True, not last, last))
        ins = []
        for c in range(NCORE):
            sl = slice(c * T, (c + 1) * T)
            d = {"xT": xT_sh[c], "oafT": np.ascontiguousarray(oafT[:, sl]), "oabT": np.ascontiguousarray(oabT[:, sl]),
                 "gT": np.ascontiguousarray(gT[:, sl]), "obT": np.ascontiguousarray(obT[:, sl]),
                 "ocT": np.ascontiguousarray(ocT[:, sl]), "w_out": np.asarray(w_out[layer], np.float32),
                 "w_up": np.asarray(w_up[layer], np.float32), "w_down": np.asarray(w_down[layer], np.float32),
                 "mlp_nw": _nw_layout(mlp_norm_w[layer])}
            if last:
                d["fin_nw"] = _nw_layout(final_norm_w)
            else:
                d["w_in"] = np.asarray(w_in[layer + 1], np.float32)
                d["attn_nw"] = _nw_layout(attn_norm_w[layer + 1])
            ins.append(d)
        res = _run(nc, ins)
        if last:
            out = cat([r["yT"].T for r in res], axis=0)
    return np.ascontiguousarray(out, dtype=np.float32)[None]
```

```python
import numpy as np
import ml_dtypes
from contextlib import ExitStack
import concourse.bass as bass
import concourse.mybir as mybir
from concourse.bass_utils import run_bass_kernel_spmd

F32 = mybir.dt.float32
BF16 = mybir.dt.bfloat16
I32 = mybir.dt.int32
AF = mybir.ActivationFunctionType
ALU = mybir.AluOpType
NPBF = ml_dtypes.bfloat16

S_LEN = 16384
D = 1024
NCORE = 8
TOK = S_LEN // NCORE
EPS = 1e-6


class Buf:
    __slots__ = ("name", "w", "r", "excl")

    def __init__(self, name="", excl=False):
        self.name = name
        self.w = []
        self.r = {}
        self.excl = excl


class Sched:
    ENGS = ("pe", "act", "dve", "pool", "sp")

    def __init__(self, nc, n_dma_sems=32, strict_same=("act", "dve", "pool")):
        self.nc = nc
        self.prog = {e: [] for e in self.ENGS}
        self.cnt = {e: 0 for e in self.ENGS}
        self.known = {e: {} for e in self.ENGS}
        self.strict_same = set(strict_same)
        self.n_dma_sems = n_dma_sems
        self.dma_val = [0] * n_dma_sems
        self.q_range = {"sp": (0, n_dma_sems // 2), "pool": (n_dma_sems // 2, n_dma_sems * 3 // 4),
                        "act": (n_dma_sems * 3 // 4, n_dma_sems)}
        self.dma_rr = {q: r[0] for q, r in self.q_range.items()}
        self.out_events = []

    def _need(self, eng, ev):
        kind, key, val = ev
        if kind == "eng" and key == eng and eng not in self.strict_same:
            return
        k = (kind, key)
        if self.known[eng].get(k, 0) >= val:
            return
        self.known[eng][k] = val
        self.prog[eng].append(("wait", k, val))

    def _deps(self, eng, reads, writes, parts):
        for b in reads:
            for ev in b.w:
                self._need(eng, ev)
            if b.excl:
                for k, ev in b.r.items():
                    if k != eng:
                        self._need(eng, ev)
        for b in writes:
            for ev in b.w:
                self._need(eng, ev)
            for ev in b.r.values():
                self._need(eng, ev)
        for b in parts:
            for ev in b.r.values():
                self._need(eng, ev)

    def _mark(self, rkey, ev, reads, writes, parts):
        for b in reads:
            b.r[rkey] = ev
        for b in writes:
            b.w = [ev]
            b.r = {}
        for b in parts:
            b.w.append(ev)

    def op(self, eng, fn, reads=(), writes=(), parts=(), inc=True):
        self._deps(eng, reads, writes, parts)
        if inc:
            self.cnt[eng] += 1
            ev = ("eng", eng, self.cnt[eng])
        else:
            ev = ("eng", eng, self.cnt[eng] + 1)
        self.prog[eng].append(("op", fn, inc))
        self._mark(eng, ev, reads, writes, parts)
        return ev

    def dma(self, q, fn, reads=(), writes=(), parts=(), is_output=False):
        self._deps(q, reads, writes, parts)
        i = self.dma_rr[q]
        lo, hi = self.q_range[q]
        self.dma_rr[q] = lo + (i + 1 - lo) % (hi - lo)
        if self.dma_val[i] > 0:
            self._need(q, ("dma", i, self.dma_val[i]))
        self.dma_val[i] += 16
        ev = ("dma", i, self.dma_val[i])
        self.prog[q].append(("dma", fn, i))
        self._mark(("dma", i), ev, reads, writes, parts)
        if is_output:
            self.out_events.append(ev)
        return ev

    def finish(self, q="sp"):
        for ev in reversed(self.out_events):
            self._need(q, ev)
        self.out_events = []

    def emit(self):
        nc = self.nc
        with ExitStack() as st:
            esem = {e: st.enter_context(nc.semaphore("s_" + e)) for e in self.ENGS}
            dsem = [st.enter_context(nc.semaphore("d%d" % i)) for i in range(self.n_dma_sems)]
            block = st.enter_context(nc.Block())

            def run(eng_name):
                def body(eng):
                    for item in self.prog[eng_name]:
                        if item[0] == "wait":
                            (kind, key), val = item[1], item[2]
                            eng.wait_ge(esem[key] if kind == "eng" else dsem[key], val)
                        elif item[0] == "op":
                            ins = item[1](eng)
                            if item[2]:
                                ins.then_inc(esem[eng_name], 1)
                        else:
                            item[1](eng).then_inc(dsem[item[2]], 16)
                return body

            block.tensor(run("pe"))
            block.scalar(run("act"))
            block.vector(run("dve"))
            block.gpsimd(run("pool"))
            block.sync(run("sp"))


class Ctx:
    def __init__(self):
        self.nc = bass.Bass("TRN2", target_bir_lowering=False)
        self.S = Sched(self.nc)
        self.st = ExitStack()
        self.psum = []
        self.pb = []
        self.ps_rr = 0

    def sb(self, name, shape, dt):
        return self.st.enter_context(self.nc.sbuf_tensor(name, list(shape), dt))

    def din(self, name, shape, dt):
        return self.nc.dram_tensor(name, list(shape), dt, kind="ExternalInput").ap()

    def dout(self, name, shape, dt):
        return self.nc.dram_tensor(name, list(shape), dt, kind="ExternalOutput").ap()

    def dscratch(self, name, shape, dt):
        return self.nc.dram_tensor(name, list(shape), dt).ap()

    def init_psum(self, n=8):
        for i in range(n):
            self.psum.append(self.st.enter_context(self.nc.psum_tensor("ps%d" % i, [128, 512], F32)))
            self.pb.append(Buf("ps%d" % i, excl=True))

    def next_psum(self):
        i = self.ps_rr
        self.ps_rr = (self.ps_rr + 1) % len(self.psum)
        return self.psum[i], self.pb[i]

    def done(self):
        self.S.finish("sp")
        self.S.emit()
        self.st.close()
        return self.nc


def build_dense(do_tail, do_proj, do_final):
    C = Ctx()
    nc, S = C.nc, C.S
    T, TB = TOK, 512
    NB = T // TB
    xT = C.din("xT", [D, T], F32)
    if do_tail:
        oafT = C.din("oafT", [256, T], F32)
        oabT = C.din("oabT", [256, T], F32)
        gT = C.din("gT", [256, T], BF16)
        obT = C.din("obT", [256, T], BF16)
        ocT = C.din("ocT", [512, T], BF16)
        w_out = C.din("w_out", [1024, 1024], F32)
        w_up = C.din("w_up", [1024, 4096], F32)
        w_down = C.din("w_down", [4096, 1024], F32)
        mlp_nw = C.din("mlp_nw", [128, 8], F32)
        ws_out = C.dscratch("ws_out", [2, 128, 8, 512], BF16)
        ws_up = C.dscratch("ws_up", [8, 128, 8, 512], BF16)
        ws_down = C.dscratch("ws_down", [2, 128, 32, 512], BF16)
        if not do_final:
            xoT = C.dout("xoT", [D, T], F32)
    if do_proj:
        w_in = C.din("w_in", [1024, 3072], F32)
        attn_nw = C.din("attn_nw", [128, 8], F32)
        ws_in = C.dscratch("ws_in", [6, 128, 8, 512], BF16)
        projT = C.dout("projT", [3072, T], BF16)
        zT = C.dout("zT", [512, T], F32)
    if do_final:
        fin_nw = C.din("fin_nw", [128, 8], F32)
        yT = C.dout("yT", [D, T], F32)

    C.init_psum(8)
    ones_t = C.sb("ones_t", [128, 128], BF16); b_ones = Buf()
    blk_t = C.sb("blk_t", [128, 128], BF16); b_blk = Buf()
    S.op("pool", lambda e: e.memset(ones_t[:], 1.0 / 1024.0), writes=[b_ones])
    S.op("pool", lambda e: e.memset(blk_t[:], 0.0), writes=[b_blk])
    S.op("pool", lambda e: e.memset(blk_t[0:64, 0:64], 1.0 / 64.0), writes=[b_blk])
    S.op("pool", lambda e: e.memset(blk_t[64:128, 64:128], 1.0 / 64.0), writes=[b_blk])
    nw_tiles = {}
    for nm, ap_ in (("mlp", mlp_nw if do_tail else None), ("attn", attn_nw if do_proj else None),
                    ("fin", fin_nw if do_final else None)):
        if ap_ is None:
            continue
        t = C.sb("nw_" + nm, [128, 8], F32); b = Buf()
        S.dma("sp", lambda e, t=t, a=ap_: e.dma_start(out=t[:], in_=a[:, :]), writes=[b])
        nw_tiles[nm] = (t, b)

    stg = [C.sb("stg%d" % i, [128, 4, 512], F32) for i in range(2)]
    stg16 = [C.sb("stg16_%d" % i, [128, 4, 512], BF16) for i in range(2)]
    b_stg = [Buf() for _ in range(2)]
    b_stg16 = [Buf() for _ in range(2)]
    conv_i = [0]
    conv_engs = ("dve", "pool", "act")
    conv_plan = []
    conv_done = [0]

    def plan_weight(W, Ws, K, N):
        KC = K // 128
        bWp = [Buf() for _ in range(N // 512)]
        Wv = W.rearrange("(kc p) n -> p kc n", p=128)
        for p in range(N // 512):
            conv_plan.append((Wv, Ws, KC, p, bWp[p]))
        return bWp

    def convert_upto(n):
        while conv_done[0] < min(n, len(conv_plan)):
            Wv, Ws, KC, p, bp = conv_plan[conv_done[0]]
            conv_done[0] += 1
            for q in range(KC // 4):
                i = conv_i[0] % 2
                eng = conv_engs[conv_i[0] % 3]
                conv_i[0] += 1
                S.dma("sp", lambda e, i=i, p=p, q=q, Wv=Wv: e.dma_start(
                    out=stg[i][:], in_=Wv[:, q * 4:(q + 1) * 4, p * 512:(p + 1) * 512]), writes=[b_stg[i]])
                src = stg[i][:].rearrange("p a n -> p (a n)")
                dst = stg16[i][:].rearrange("p a n -> p (a n)")
                if eng == "act":
                    S.op("act", lambda e, src=src, dst=dst: e.activation(dst, src, AF.Copy), reads=[b_stg[i]],
                         writes=[b_stg16[i]])
                else:
                    S.op(eng, lambda e, src=src, dst=dst: e.tensor_copy(dst, src), reads=[b_stg[i]], writes=[b_stg16[i]])
                S.dma("pool", lambda e, i=i, p=p, q=q, Ws=Ws: e.dma_start(out=Ws[p, :, q * 4:(q + 1) * 4, :], in_=stg16[i][:]),
                      reads=[b_stg16[i]], parts=[bp])

    if do_tail:
        b_wout = plan_weight(w_out, ws_out, 1024, 1024)
        b_wup = plan_weight(w_up, ws_up, 1024, 4096)
        b_wdown = plan_weight(w_down, ws_down, 4096, 1024)
    if do_proj:
        b_win = plan_weight(w_in, ws_in, 1024, 3072)
    panel_seq = [0]

    xb = C.sb("xb", [128, 8, TB], F32); bx = [Buf() for _ in range(8)]
    hT = C.sb("hT", [128, 8, TB], BF16); bh = [Buf() for _ in range(8)]
    sq = C.sb("sq", [128, 8, TB], BF16); bsq = Buf()
    rstd = C.sb("rstd", [128, TB], F32); brstd = Buf()
    sd = C.sb("sd", [128, TB], F32); bsd = Buf()
    pan = [C.sb("pan%d" % i, [128, 32, 512], BF16) for i in range(2)]
    bpan = [Buf() for _ in range(2)]
    if do_tail:
        mixT = C.sb("mixT", [128, 8, TB], BF16); bmix = [Buf() for _ in range(8)]
        actT = C.sb("actT", [128, 32, TB], BF16); bact = [Buf() for _ in range(32)]
        oa = C.sb("oa", [128, 2, TB], F32); boa = Buf()
        oa2 = C.sb("oa2", [128, 2, TB], F32); boa2 = Buf()
        gg = C.sb("gg", [128, 2, TB], BF16); bgg = Buf()
        sqa = C.sb("sqa", [128, 2, TB], BF16); bsqa = Buf()
        sg = C.sb("sg", [128, TB], F32); bsg = Buf()
        tmpa = C.sb("tmpa", [128, TB], F32); btmpa = Buf()
        usq = [C.sb("usq%d" % i, [128, TB], F32) for i in range(2)]; busq = [Buf() for _ in range(2)]
    if do_proj:
        ost = [C.sb("ost%d" % i, [128, TB], BF16) for i in range(3)]; bost = [Buf() for _ in range(3)]
        ostf = [C.sb("ostf%d" % i, [128, TB], F32) for i in range(2)]; bostf = [Buf() for _ in range(2)]
    if do_final:
        yst = [C.sb("yst%d" % i, [128, TB], F32) for i in range(2)]; byst = [Buf() for _ in range(2)]

    pan_i = [0]
    cnts = {"usq": 0, "ost": 0, "ostf": 0, "yst": 0}

    def rmsnorm(nw, out_fn):
        nwt, bnw = nw
        S.op("act", lambda e: e.activation(sq[:].rearrange("p a t -> p (a t)"), xb[:].rearrange("p a t -> p (a t)"),
                                           AF.Square), reads=bx, writes=[bsq])
        ps, bps = C.next_psum()
        for kc in range(8):
            S.op("pe", lambda e, kc=kc, ps=ps: e.matmul(ps[:, :TB], ones_t[:], sq[:, kc, :], start=(kc == 0),
                                                      stop=(kc == 7)),
                 reads=[b_ones, bsq], writes=[bps] if kc == 0 else (), parts=[bps] if kc > 0 else (), inc=(kc == 7))
        S.op("act", lambda e, ps=ps: e.activation(sd[:], ps[:, :TB], AF.Sqrt, bias=EPS), reads=[bps], writes=[bsd])
        S.op("dve", lambda e: e.reciprocal(rstd[:], sd[:]), reads=[bsd], writes=[brstd])
        for kc in range(8):
            out_fn(kc, nwt[:, kc:kc + 1], bnw)

    def norm_to_h(nw):
        def out_fn(kc, sc, bnw):
            S.op("dve", lambda e, kc=kc, sc=sc: e.scalar_tensor_tensor(hT[:, kc, :], xb[:, kc, :], sc, rstd[:],
                                                                      ALU.mult, ALU.mult),
                 reads=[bx[kc], brstd, bnw], writes=[bh[kc]])
        rmsnorm(nw, out_fn)

    def matmul_phase(Ws, bW, KC, N, in_tile, in_bufs, evac):
        npan = N // 512
        jobs = list(range(npan))

        def load(p):
            panel_seq[0] += 1
            convert_upto(panel_seq[0] + 3)
            i = pan_i[0] % 2
            pan_i[0] += 1
            S.dma("sp", lambda e, i=i, p=p: e.dma_start(out=pan[i][:, :KC, :], in_=Ws[p, :, :, :]),
                  reads=[bW[p]], writes=[bpan[i]])
            return i
        cur = load(0)
        for p in jobs:
            nxt = load(p + 1) if p + 1 < npan else None
            for jj in range(4):
                j = p * 4 + jj
                ps, bps = C.next_psum()
                for kc in range(KC):
                    S.op("pe", lambda e, ps=ps, cur=cur, kc=kc, jj=jj: e.matmul(
                        ps[:, :TB], pan[cur][:, kc, jj * 128:(jj + 1) * 128], in_tile[:, kc, :],
                        start=(kc == 0), stop=(kc == KC - 1)),
                        reads=[bpan[cur], in_bufs[kc]], writes=[bps] if kc == 0 else (),
                        parts=[bps] if kc > 0 else (), inc=(kc == KC - 1))
                evac(j, ps, bps)
            cur = nxt

    for tb in range(NB):
        tsl = slice(tb * TB, (tb + 1) * TB)
        S.dma("sp", lambda e, tsl=tsl: e.dma_start(out=xb[:], in_=xT.rearrange("(kc p) t -> p kc t", p=128)[:, :, tsl]),
              writes=bx)
        if do_tail:
            S.dma("sp", lambda e, tsl=tsl: e.dma_start(out=oa[:], in_=oafT.rearrange("(kc p) t -> p kc t", p=128)[:, :, tsl]),
                  writes=[boa])
            S.dma("sp", lambda e, tsl=tsl: e.dma_start(out=oa2[:], in_=oabT.rearrange("(kc p) t -> p kc t", p=128)[:, :, tsl]),
                  writes=[boa2])
            S.op("dve", lambda e: e.tensor_tensor(oa[:], oa[:], oa2[:], ALU.add), reads=[boa, boa2], writes=[boa])
            S.dma("sp", lambda e, tsl=tsl: e.dma_start(out=gg[:], in_=gT.rearrange("(kc p) t -> p kc t", p=128)[:, :, tsl]),
                  writes=[bgg])
            S.dma("sp", lambda e, tsl=tsl: e.dma_start(out=mixT[:, 2:4, :],
                                                       in_=obT.rearrange("(kc p) t -> p kc t", p=128)[:, :, tsl]),
                  writes=bmix[2:4])
            S.dma("sp", lambda e, tsl=tsl: e.dma_start(out=mixT[:, 4:8, :],
                                                       in_=ocT.rearrange("(kc p) t -> p kc t", p=128)[:, :, tsl]),
                  writes=bmix[4:8])
            S.op("act", lambda e: e.activation(sqa[:].rearrange("p a t -> p (a t)"), oa[:].rearrange("p a t -> p (a t)"),
                                               AF.Square), reads=[boa], writes=[bsqa])
            for c in range(2):
                ps, bps = C.next_psum()
                S.op("pe", lambda e, ps=ps, c=c: e.matmul(ps[:, :TB], blk_t[:], sqa[:, c, :], start=True, stop=True),
                     reads=[b_blk, bsqa], writes=[bps])
                S.op("act", lambda e, ps=ps: e.activation(sd[:], ps[:, :TB], AF.Sqrt, bias=EPS), reads=[bps], writes=[bsd])
                S.op("dve", lambda e: e.reciprocal(rstd[:], sd[:]), reads=[bsd], writes=[brstd])
                S.op("act", lambda e, c=c: e.activation(sg[:], gg[:, c, :], AF.Silu), reads=[bgg], writes=[bsg])
                S.op("dve", lambda e, c=c: e.tensor_tensor(tmpa[:], oa[:, c, :], rstd[:], ALU.mult),
                     reads=[boa, brstd], writes=[btmpa])
                S.op("dve", lambda e, c=c: e.tensor_tensor(mixT[:, c, :], tmpa[:], sg[:], ALU.mult),
                     reads=[btmpa, bsg], writes=[bmix[c]])

            def evac_res(j, ps, bps):
                S.op("dve", lambda e, j=j, ps=ps: e.tensor_tensor(xb[:, j, :], xb[:, j, :], ps[:, :TB], ALU.add),
                     reads=[bps, bx[j]], writes=[bx[j]])

            def evac_up(j, ps, bps):
                i = cnts["usq"] % 2
                cnts["usq"] += 1
                S.op("act", lambda e, i=i, ps=ps: e.activation(usq[i][:], ps[:, :TB], AF.Square), reads=[bps],
                     writes=[busq[i]])
                S.op("dve", lambda e, i=i, ps=ps, j=j: e.scalar_tensor_tensor(actT[:, j, :], ps[:, :TB], 0.0, usq[i][:],
                                                                            ALU.is_gt, ALU.mult),
                     reads=[bps, busq[i]], writes=[bact[j]])

            matmul_phase(ws_out, b_wout, 8, 1024, mixT, bmix, evac_res)
            norm_to_h(nw_tiles["mlp"])
            matmul_phase(ws_up, b_wup, 8, 4096, hT, bh, evac_up)
            matmul_phase(ws_down, b_wdown, 32, 1024, actT, bact, evac_res)
            if not do_final:
                S.dma("act", lambda e, tsl=tsl: e.dma_start(out=xoT.rearrange("(kc p) t -> p kc t", p=128)[:, :, tsl],
                                                           in_=xb[:]), reads=bx, is_output=True)
        if do_final:
            def out_fn(kc, sc, bnw):
                i = cnts["yst"] % 2
                cnts["yst"] += 1
                S.op("dve", lambda e, kc=kc, sc=sc, i=i: e.scalar_tensor_tensor(yst[i][:], xb[:, kc, :], sc, rstd[:],
                                                                              ALU.mult, ALU.mult),
                     reads=[bx[kc], brstd, bnw], writes=[byst[i]])
                S.dma("act", lambda e, kc=kc, i=i, tsl=tsl: e.dma_start(out=yT[kc * 128:(kc + 1) * 128, tsl], in_=yst[i][:]),
                      reads=[byst[i]], is_output=True)
            rmsnorm(nw_tiles["fin"], out_fn)
        if do_proj:
            norm_to_h(nw_tiles["attn"])

            def evac_proj(j, ps, bps):
                i = cnts["ost"] % 3
                cnts["ost"] += 1
                if 4 <= j < 8:
                    k = cnts["ostf"] % 2
                    cnts["ostf"] += 1
                    S.op("act", lambda e, k=k, ps=ps: e.activation(ostf[k][:], ps[:, :TB], AF.Copy), reads=[bps],
                         writes=[bostf[k]])
                    S.op("dve", lambda e, k=k, i=i: e.tensor_copy(ost[i][:], ostf[k][:]), reads=[bostf[k]],
                         writes=[bost[i]])
                    S.dma("act", lambda e, k=k, j=j, tsl=tsl: e.dma_start(out=zT[(j - 4) * 128:(j - 3) * 128, tsl],
                                                                         in_=ostf[k][:]),
                          reads=[bostf[k]], is_output=True)
                else:
                    S.op("act", lambda e, i=i, ps=ps: e.activation(ost[i][:], ps[:, :TB], AF.Copy), reads=[bps],
                         writes=[bost[i]])
                S.dma("act", lambda e, i=i, j=j, tsl=tsl: e.dma_start(out=projT[j * 128:(j + 1) * 128, tsl], in_=ost[i][:]),
                      reads=[bost[i]], is_output=True)
            matmul_phase(ws_in, b_win, 8, 3072, hT, bh, evac_proj)
    return C.done()


PATTERN_D = (1, 4, 16)
QPAD = 2048
KPAD = 1024
SEG = 4096
TWO_PI = 6.283185307179586
CW1 = 6.28125
CW2 = TWO_PI - CW1
MAGIC = 12582912.0
PI_SAFE = 3.1415925


def attn_tile_base():
    base, b = {}, 0
    for d in PATTERN_D:
        base[d] = b
        b += d * (S_LEN // d // 128 + 1)
    return base, b


def build_attn():
    C = Ctx()
    nc, S = C.nc, C.S
    base, NT = attn_tile_base()
    qT = C.din("qT", [64, S_LEN], BF16)
    kT = C.din("kT", [64, S_LEN], BF16)
    vaug = C.din("vaug", [128, NT, 65], BF16)
    posb = C.din("posb", [64, S_LEN], I32)
    invf = C.din("invf", [64, 1], F32)
    psw_d = C.din("psw", [64, 64], BF16)
    mask_d = C.din("maskab", [128, 256], BF16)
    ident_d = C.din("ident", [128, 128], BF16)
    sel_d = C.din("sel", [65, 64], F32)
    oT = C.dout("oT", [64, S_LEN], BF16)
    C.init_psum(8)

    qTp = C.sb("qTp", [64, QPAD + S_LEN + QPAD], BF16); bq = Buf()
    kTp = C.sb("kTp", [64, KPAD + S_LEN + KPAD], BF16); bk = Buf()
    V = C.sb("V", [128, NT, 65], BF16); bV = Buf()
    acc = C.sb("acc", [65, SEG], F32); bacc = Buf()
    invf_t = C.sb("invf_t", [64, 1], F32); binvf = Buf()
    psw = C.sb("psw_t", [64, 64], BF16); bpsw = Buf()
    maskab = C.sb("mask_t", [128, 256], BF16); bmask = Buf()
    ident = C.sb("ident_t", [128, 128], BF16); bident = Buf()
    sel = C.sb("sel_t", [65, 64], F32); bsel = Buf()
    for t, b, a in ((invf_t, binvf, invf), (psw, bpsw, psw_d), (maskab, bmask, mask_d), (ident, bident, ident_d),
                    (sel, bsel, sel_d)):
        S.dma("sp", lambda e, t=t, a=a: e.dma_start(out=t[:], in_=a[:, :]), writes=[b])
    for i0 in range(0, NT, 81):
        S.dma("sp", lambda e, i0=i0: e.dma_start(out=V[:, i0:i0 + 81, :], in_=vaug[:, i0:i0 + 81, :]), parts=[bV])
    bVo = Buf()
    S.op("pool", lambda e: e.memset(V[:, :, 64:65], 1.0), reads=[bV], writes=[bVo])
    for d in PATTERN_D:
        nt = S_LEN // d // 128 + 1
        S.op("pool", lambda e, d=d, nt=nt: e.memset(V[0:64, base[d]:base[d] + d * nt:nt, 64:65], 0.0), writes=[bVo])
        S.op("pool", lambda e, d=d, nt=nt: e.memset(V[64:128, base[d] + nt - 1:base[d] + d * nt:nt, 64:65], 0.0), writes=[bVo])
    S.op("pool", lambda e: e.memset(qTp[:, 0:QPAD], 0.0), parts=[bq])
    S.op("pool", lambda e: e.memset(qTp[:, QPAD + S_LEN:], 0.0), parts=[bq])
    S.op("pool", lambda e: e.memset(kTp[:, 0:KPAD], 0.0), parts=[bk])
    S.op("pool", lambda e: e.memset(kTp[:, KPAD + S_LEN:], 0.0), parts=[bk])

    CH = 1024
    pos_i = C.sb("pos_i", [64, CH], I32); bpos = Buf()
    ang = C.sb("ang", [64, CH], F32); bang = Buf()
    nn_ = C.sb("nn", [64, CH], F32); bnn = Buf()
    yy = C.sb("yy", [64, CH], F32); byy = Buf()
    sin_t = C.sb("sin_t", [64, CH], F32); bsin = Buf()
    cos_t = C.sb("cos_t", [64, CH], F32); bcos = Buf()
    raw = [C.sb("raw%d" % i, [64, CH], BF16) for i in range(2)]; braw = [Buf() for _ in range(2)]
    t1 = C.sb("t1", [64, CH], F32); bt1 = Buf()
    t2 = C.sb("t2", [64, CH], F32); bt2 = Buf()
    for ch in range(S_LEN // CH):
        csl = slice(ch * CH, (ch + 1) * CH)
        S.dma("sp", lambda e, csl=csl: e.dma_start(out=pos_i[:], in_=posb[:, csl]), writes=[bpos])
        S.op("dve", lambda e: e.tensor_copy(ang[:], pos_i[:]), reads=[bpos], writes=[bang])
        S.op("dve", lambda e: e.tensor_scalar(ang[:], ang[:], invf_t[:, 0:1], None, ALU.mult), reads=[bang, binvf],
             writes=[bang])
        S.op("dve", lambda e: e.tensor_scalar(nn_[:], ang[:], 1.0 / TWO_PI, MAGIC, ALU.mult, ALU.add), reads=[bang],
             writes=[bnn])
        S.op("dve", lambda e: e.tensor_scalar(nn_[:], nn_[:], -MAGIC, None, ALU.add), reads=[bnn], writes=[bnn])
        S.op("dve", lambda e: e.scalar_tensor_tensor(yy[:], nn_[:], -CW1, ang[:], ALU.mult, ALU.add), reads=[bnn, bang],
             writes=[byy])
        S.op("dve", lambda e: e.scalar_tensor_tensor(yy[:], nn_[:], -CW2, yy[:], ALU.mult, ALU.add), reads=[bnn, byy],
             writes=[byy])
        S.op("dve", lambda e: e.tensor_scalar(yy[:], yy[:], PI_SAFE, -PI_SAFE, ALU.min, ALU.max), reads=[byy],
             writes=[byy])
        S.op("act", lambda e: e.activation(sin_t[:], yy[:], AF.Sin), reads=[byy], writes=[bsin])
        S.op("dve", lambda e: e.scalar_tensor_tensor(nn_[:], yy[:], -1.0, yy[:], ALU.mult, ALU.min), reads=[byy],
             writes=[bnn])
        S.op("dve", lambda e: e.tensor_scalar(yy[:], nn_[:], 1.5707963, None, ALU.add), reads=[bnn], writes=[byy])
        S.op("act", lambda e: e.activation(cos_t[:], yy[:], AF.Sin), reads=[byy], writes=[bcos])
        for which, (src, dstt, bdst, pad) in enumerate(((qT, qTp, bq, QPAD), (kT, kTp, bk, KPAD))):
            rw, brw = raw[which], braw[which]
            S.dma("sp", lambda e, rw=rw, src=src, csl=csl: e.dma_start(out=rw[:], in_=src[:, csl]), writes=[brw])
            pss = []
            for b4 in range(CH // 512):
                ps, bps = C.next_psum()
                S.op("pe", lambda e, ps=ps, rw=rw, b4=b4: e.matmul(ps[0:64, :], psw[:], rw[:, b4 * 512:(b4 + 1) * 512],
                                                                 start=True, stop=True),
                     reads=[bpsw, brw], writes=[bps])
                pss.append((ps, bps))
            S.op("dve", lambda e, rw=rw: e.tensor_tensor(t1[:], rw[:], cos_t[:], ALU.mult), reads=[brw, bcos],
                 writes=[bt1])
            for b4 in range(CH // 512):
                ps, bps = pss[b4]
                S.op("dve", lambda e, ps=ps, b4=b4: e.tensor_tensor(t2[:, b4 * 512:(b4 + 1) * 512], ps[0:64, :],
                                                                  sin_t[:, b4 * 512:(b4 + 1) * 512], ALU.mult),
                     reads=[bps, bsin], writes=[bt2] if b4 == 0 else (), parts=[bt2] if b4 else ())
            S.op("pool", lambda e, dstt=dstt, pad=pad, ch=ch: e.tensor_tensor(
                dstt[:, pad + ch * CH:pad + (ch + 1) * CH], t1[:], t2[:], ALU.add), reads=[bt1, bt2], parts=[bdst])

    NP_ = 4
    Pt = [C.sb("P%d" % i, [128, 512], BF16) for i in range(NP_)]; bP = [Buf() for _ in range(NP_)]
    p_rr = [0]
    rec = C.sb("rec", [64, 512], F32); brec = Buf()
    ost = [C.sb("ost%d" % i, [64, SEG], BF16) for i in range(2)]; bost = [Buf() for _ in range(2)]
    for seg in range(S_LEN // SEG):
        for d in PATTERN_D:
            L = S_LEN // d
            nt = L // 128 + 1
            nq = SEG // d // 128
            m0 = nq * seg
            for r in range(d):
                def kcols(mp):
                    st_ = KPAD + r + d * (128 * mp - 64)
                    return slice(st_, st_ + 127 * d + 1, d)

                def qcols(mp):
                    st_ = QPAD + r + d * 128 * (mp - 1)
                    return slice(st_, st_ + 255 * d + 1, d)
                npair = (nq + 2) // 2
                ptiles = {}
                out_ps = None
                for u in range(npair):
                    ps, bps = C.next_psum()
                    slot = p_rr[0] % NP_
                    p_rr[0] += 1
                    ntile = 0
                    for s_ in range(2):
                        rel = 2 * u + s_
                        if rel > nq:
                            break
                        mp = m0 + rel
                        S.op("pe", lambda e, ps=ps, s_=s_, kc=kcols(mp), qc=qcols(mp): e.matmul(
                            ps[:, s_ * 256:(s_ + 1) * 256], kTp[:, kc], qTp[:, qc], start=True, stop=False),
                            reads=[bk, bq], writes=[bps] if s_ == 0 else (), parts=[bps] if s_ else (), inc=False)
                        S.op("pe", lambda e, ps=ps, s_=s_: e.matmul(ps[:, s_ * 256:(s_ + 1) * 256], ident[:], maskab[:],
                                                                   start=False, stop=True),
                             reads=[bident, bmask], parts=[bps], inc=True)
                        ptiles[rel] = (slot, s_)
                        ntile += 1
                    w = ntile * 256
                    S.op("act", lambda e, ps=ps, slot=slot, w=w: e.activation(Pt[slot][:, :w], ps[:, :w], AF.Exp, scale=0.125),
                         reads=[bps], writes=[bP[slot]])
                    for q in (2 * u - 1, 2 * u):
                        if q < 0 or q >= nq or (q + 1) not in ptiles:
                            continue
                        wq = q % 4
                        if wq == 0:
                            out_ps = C.next_psum()
                        ops, bops = out_ps
                        sa, ha = ptiles[q]
                        sb_, hb = ptiles[q + 1]
                        ia = base[d] + r * nt + m0 + q
                        S.op("pe", lambda e, ops=ops, wq=wq, ia=ia, sa=sa, ha=ha: e.matmul(
                            ops[0:65, wq * 128:(wq + 1) * 128], V[:, ia, :], Pt[sa][:, ha * 256 + 128:ha * 256 + 256],
                            start=True, stop=False),
                            reads=[bV, bVo, bP[sa]], writes=[bops] if wq == 0 else (), parts=[bops] if wq else (), inc=False)
                        S.op("pe", lambda e, ops=ops, wq=wq, ia=ia, sb_=sb_, hb=hb: e.matmul(
                            ops[0:65, wq * 128:(wq + 1) * 128], V[:, ia + 1, :], Pt[sb_][:, hb * 256:hb * 256 + 128],
                            start=False, stop=True),
                            reads=[bV, bVo, bP[sb_]], parts=[bops], inc=True)
                        if wq == 3 or q == nq - 1:
                            nqt = wq + 1
                            q0 = q - wq
                            t0 = r + d * 128 * q0
                            asl = slice(t0, t0 + (nqt * 128 - 1) * d + 1, d)
                            if d == 1:
                                S.op("dve", lambda e, ops=ops, asl=asl, nqt=nqt: e.tensor_copy(acc[:, asl], ops[0:65, :nqt * 128]),
                                     reads=[bops], parts=[bacc])
                            else:
                                S.op("dve", lambda e, ops=ops, asl=asl, nqt=nqt: e.tensor_tensor(
                                    acc[:, asl], acc[:, asl], ops[0:65, :nqt * 128], ALU.add),
                                    reads=[bops, bacc], writes=[bacc])
        oi = seg % 2
        for b8 in range(SEG // 512):
            ps, bps = C.next_psum()
            S.op("pe", lambda e, ps=ps, b8=b8: e.matmul(ps[0:64, :], sel[:], acc[:, b8 * 512:(b8 + 1) * 512], start=True,
                                                      stop=True), reads=[bsel, bacc], writes=[bps])
            S.op("dve", lambda e, ps=ps: e.reciprocal(rec[:], ps[0:64, :]), reads=[bps], writes=[brec])
            S.op("dve", lambda e, b8=b8, oi=oi: e.tensor_tensor(ost[oi][:, b8 * 512:(b8 + 1) * 512],
                                                              acc[0:64, b8 * 512:(b8 + 1) * 512], rec[:], ALU.mult),
                 reads=[bacc, brec], writes=[bost[oi]] if b8 == 0 else (), parts=[bost[oi]] if b8 else ())
        S.dma("pool", lambda e, oi=oi, seg=seg: e.dma_start(out=oT[:, seg * SEG:(seg + 1) * SEG], in_=ost[oi][:]),
              reads=[bost[oi]], is_output=True)
    return C.done()


def attn_consts():
    inv = (np.float32(10000.0) ** (-(np.arange(0, 64, 2, dtype=np.float32) / np.float32(64.0)))).astype(np.float32)
    invf = np.concatenate([-inv, inv]).reshape(64, 1).astype(np.float32)
    psw = np.zeros((64, 64), np.float32)
    for c in range(64):
        psw[(c + 32) % 64, c] = 1.0
    p = np.arange(128)[:, None]
    c = np.arange(128)[None, :]
    NEG = -30000.0
    maskab = np.concatenate([np.where(c >= p, 0.0, NEG), np.where(c <= p, 0.0, NEG)], axis=1).astype(np.float32)
    sel = np.zeros((65, 64), np.float32)
    sel[64, :] = 1.0
    return {"invf": invf, "psw": psw.astype(NPBF), "maskab": maskab.astype(NPBF),
            "ident": np.eye(128, dtype=np.float32).astype(NPBF), "sel": sel}


def attn_v_layout(v):
    base, NT = attn_tile_base()
    out = np.zeros((128, NT, 65), dtype=v.dtype)
    p = np.arange(128)
    for d in PATTERN_D:
        L = S_LEN // d
        nt = L // 128 + 1
        for r in range(d):
            for mp in range(nt):
                j = 128 * mp - 64 + p
                ok = (j >= 0) & (j < L)
                tok = r + d * j[ok]
                out[p[ok], base[d] + r * nt + mp, :64] = v[tok]
    return out


def build_fft():
    C = Ctx()
    nc, S = C.nc, C.S
    uT = C.din("uT", [64, S_LEN], BF16)
    cs64_d = C.din("cs64", [64, 128], BF16)
    r1a_d = C.din("r1a", [128, 128], BF16)
    r1b_d = C.din("r1b", [128, 128], BF16)
    c128_d = C.din("c128", [128, 128], BF16)
    s128_d = C.din("s128", [128, 128], BF16)
    tr_d = C.din("tw_r", [128, 64], F32)
    ti_d = C.din("tw_i", [128, 64], F32)
    yh = C.dout("yh", [128, 64, 64], BF16)
    C.init_psum(8)
    u = C.sb("u", [64, S_LEN], BF16); bu = Buf()
    consts = {}
    for nm, a, shp, dt in (("cs64", cs64_d, [64, 128], BF16), ("r1a", r1a_d, [128, 128], BF16),
                           ("r1b", r1b_d, [128, 128], BF16), ("c128", c128_d, [128, 128], BF16),
                           ("s128", s128_d, [128, 128], BF16), ("tr", tr_d, [128, 64], F32), ("ti", ti_d, [128, 64], F32)):
        t = C.sb("k_" + nm, shp, dt); b = Buf()
        S.dma("sp", lambda e, t=t, a=a: e.dma_start(out=t[:], in_=a[:, :]), writes=[b])
        consts[nm] = (t, b)
    for i in range(4):
        S.dma("sp", lambda e, i=i: e.dma_start(out=u[:, i * 4096:(i + 1) * 4096], in_=uT[:, i * 4096:(i + 1) * 4096]),
              parts=[bu])
    Vall = C.sb("Vall", [128, 128, 2, 64], BF16); bVall = Buf()
    Pp = C.sb("Pp", [128, 2, 64, 64], BF16); bPp = Buf()
    ysb = C.sb("ysb", [128, 64, 64], BF16); bys = Buf()
    tmp = [C.sb("ftmp%d" % i, [128, 4, 64], F32) for i in range(4)]; btmp = [Buf() for _ in range(4)]
    cs64, bcs = consts["cs64"]
    for g4 in range(32):
        ps, bps = C.next_psum()
        for i in range(4):
            s1 = g4 * 4 + i
            S.op("pe", lambda e, ps=ps, i=i, s1=s1: e.matmul(ps[:, i * 128:(i + 1) * 128],
                                                           u[:, s1:s1 + 127 * 128 + 1:128], cs64[:], start=True, stop=True),
                 reads=[bu, bcs], writes=[bps] if i == 0 else (), parts=[bps] if i else (), inc=(i == 3))
        dst = Vall[:, g4 * 4:(g4 + 1) * 4, :, :].rearrange("p a r c -> p (a r c)")
        if g4 % 2 == 0:
            S.op("act", lambda e, ps=ps, dst=dst: e.activation(dst, ps[:, :], AF.Copy), reads=[bps], parts=[bVall])
        else:
            S.op("dve", lambda e, ps=ps, dst=dst: e.tensor_copy(dst, ps[:, :]), reads=[bps], parts=[bVall])
    r1a, br1a = consts["r1a"]
    r1b, br1b = consts["r1b"]
    tr, btr = consts["tr"]
    ti, bti = consts["ti"]
    trb = tr[:, None, :].to_broadcast([128, 4, 64])
    tib = ti[:, None, :].to_broadcast([128, 4, 64])
    for g4 in range(16):
        ps, bps = C.next_psum()
        for i in range(4):
            cp = g4 * 4 + i
            S.op("pe", lambda e, ps=ps, i=i, cp=cp: e.matmul(ps[:, i * 128:(i + 1) * 128], Vall[:, :, 0, cp], r1a[:],
                                                           start=True, stop=False),
                 reads=[bVall, br1a], writes=[bps] if i == 0 else (), parts=[bps] if i else (), inc=False)
            S.op("pe", lambda e, ps=ps, i=i, cp=cp: e.matmul(ps[:, i * 128:(i + 1) * 128], Vall[:, :, 1, cp], r1b[:],
                                                           start=False, stop=True),
                 reads=[bVall, br1b], parts=[bps], inc=(i == 3))
        pv = ps[:, :].rearrange("p (c r k) -> p c r k", c=4, r=2)
        pr, pi = pv[:, :, 0, :], pv[:, :, 1, :]
        S.op("dve", lambda e, pr=pr: e.tensor_tensor(tmp[0][:], pr, trb, ALU.mult), reads=[bps, btr], writes=[btmp[0]])
        S.op("dve", lambda e, pi=pi: e.tensor_tensor(tmp[1][:], pi, tib, ALU.mult), reads=[bps, bti], writes=[btmp[1]])
        S.op("dve", lambda e, pr=pr: e.tensor_tensor(tmp[2][:], pr, tib, ALU.mult), reads=[bps, bti], writes=[btmp[2]])
        S.op("dve", lambda e, pi=pi: e.tensor_tensor(tmp[3][:], pi, trb, ALU.mult), reads=[bps, btr], writes=[btmp[3]])
        csl = slice(g4 * 4, (g4 + 1) * 4)
        S.op("pool", lambda e, csl=csl: e.tensor_tensor(Pp[:, 0, csl, :], tmp[0][:], tmp[1][:], ALU.subtract),
             reads=[btmp[0], btmp[1]], parts=[bPp])
        S.op("pool", lambda e, csl=csl: e.tensor_tensor(Pp[:, 1, csl, :], tmp[2][:], tmp[3][:], ALU.add),
             reads=[btmp[2], btmp[3]], parts=[bPp])
    c128, bc128 = consts["c128"]
    s128, bs128 = consts["s128"]
    for b8 in range(8):
        ps, bps = C.next_psum()
        csl = slice(b8 * 8, (b8 + 1) * 8)
        S.op("pe", lambda e, ps=ps, csl=csl: e.matmul(ps[:, :], c128[:], Pp[:, 0, csl, :].rearrange("p c k -> p (c k)"),
                                                    start=True, stop=False), reads=[bPp, bc128], writes=[bps], inc=False)
        S.op("pe", lambda e, ps=ps, csl=csl: e.matmul(ps[:, :], s128[:], Pp[:, 1, csl, :].rearrange("p c k -> p (c k)"),
                                                    start=False, stop=True), reads=[bPp, bs128], parts=[bps])
        S.op("act", lambda e, ps=ps, csl=csl: e.activation(ysb[:, :, csl].rearrange("p k c -> p c k"),
                                                        ps[:, :].rearrange("p (c k) -> p c k", c=8), AF.Copy,
                                                        scale=1.0 / 1024.0), reads=[bps], parts=[bys])
    S.dma("pool", lambda e: e.dma_start(out=yh[:, :, :], in_=ysb[:]), reads=[bys], is_output=True)
    return C.done()


def fft_consts(hh):
    f64 = np.float64
    c = np.arange(64)
    a64 = 2 * np.pi * np.outer(c, c) / 64
    cs64 = np.concatenate([np.cos(a64), -np.sin(a64)], axis=1)
    n = np.arange(128)
    a128 = 2 * np.pi * np.outer(n, n) / 128
    C128, S128 = np.cos(a128), np.sin(a128)
    k2 = np.arange(64) + 64 * hh
    Ch, Sh = C128[:, k2], S128[:, k2]
    r1a = np.concatenate([Ch, -Sh], axis=1)
    r1b = np.concatenate([Sh, Ch], axis=1)
    th = 2 * np.pi * np.outer(n, k2) / S_LEN
    return {"cs64": cs64.astype(np.float32).astype(NPBF), "r1a": r1a.astype(np.float32).astype(NPBF),
            "r1b": r1b.astype(np.float32).astype(NPBF), "c128": C128.astype(np.float32).astype(NPBF),
            "s128": S128.astype(np.float32).astype(NPBF), "tw_r": np.cos(th).astype(np.float32),
            "tw_i": (-np.sin(th)).astype(np.float32)}


def build_hgrn():
    C = Ctx()
    nc, S = C.nc, C.S
    T = S_LEN
    TBK = 1024
    NBK = T // TBK
    NCH = T // 64
    NPAIR = T // 128
    din = {}
    for dr in ("f", "b"):
        din["q" + dr] = C.din("qT_" + dr, [64, T], BF16)
        din["z" + dr] = C.din("zT_" + dr, [64, T], F32)
        din["v" + dr] = C.din("vtok_" + dr, [128, NPAIR, 32], BF16)
        din["o" + dr] = C.dout("oT_" + dr, [32, T], F32)
    lbraw_d = C.din("lbraw", [64, 2, 4], F32)
    lbmask_d = C.din("lbmask", [64, 2, 4], F32)
    maskT_d = C.din("maskT", [128, 128], BF16)
    identf_d = C.din("identf", [64, 64], F32)
    C.init_psum(8)
    lbraw = C.sb("lbraw_t", [64, 2, 4], F32); blbraw = Buf()
    lbmask = C.sb("lbmask_t", [64, 2, 4], F32); blbmask = Buf()
    maskT = C.sb("maskT_t", [128, 128], BF16); bmaskT = Buf()
    identf = C.sb("identf_t", [64, 64], F32); bidentf = Buf()
    for t, b, a in ((lbraw, blbraw, lbraw_d), (lbmask, blbmask, lbmask_d)):
        S.dma("sp", lambda e, t=t, a=a: e.dma_start(out=t[:], in_=a[:, :, :]), writes=[b])
    for t, b, a in ((maskT, bmaskT, maskT_d), (identf, bidentf, identf_d)):
        S.dma("sp", lambda e, t=t, a=a: e.dma_start(out=t[:], in_=a[:, :]), writes=[b])
    lbe = C.sb("lbe", [64, 2, 4], F32); blbe = Buf()
    lbs = C.sb("lbs", [64, 2], F32); blbs = Buf()
    lbn = C.sb("lbn", [64, 2], F32); blbn = Buf()
    lb = C.sb("lb", [64, 2], F32); blb = Buf()
    oml = C.sb("oml", [64, 2], F32); boml = Buf()
    S.op("act", lambda e: e.activation(lbe[:], lbraw[:], AF.Exp), reads=[blbraw], writes=[blbe])
    S.op("dve", lambda e: e.reduce_sum(lbs[:], lbe[:], mybir.AxisListType.X), reads=[blbe], writes=[blbs])
    S.op("dve", lambda e: e.tensor_tensor(lbe[:], lbe[:], lbmask[:], ALU.mult), reads=[blbe, blbmask], writes=[blbe])
    S.op("dve", lambda e: e.reduce_sum(lbn[:], lbe[:], mybir.AxisListType.X), reads=[blbe], writes=[blbn])
    S.op("dve", lambda e: e.reciprocal(lbs[:], lbs[:]), reads=[blbs], writes=[blbs])
    S.op("dve", lambda e: e.tensor_tensor(lb[:], lbn[:], lbs[:], ALU.mult), reads=[blbn, blbs], writes=[blb])
    S.op("dve", lambda e: e.tensor_scalar(oml[:], lb[:], -1.0, 1.0, ALU.mult, ALU.add), reads=[blb], writes=[boml])
    rmask = C.sb("rmask", [64, TBK], F32); brmask = Buf()
    S.op("pool", lambda e: e.memset(rmask[:], 1.0), writes=[brmask])
    S.op("pool", lambda e: e.memset(rmask[:, 0:TBK:64], 0.0), writes=[brmask])
    Qt = C.sb("Qt", [64, T], BF16); bQt = Buf()
    Kt = C.sb("Kt", [64, T], BF16); bKt = Buf()
    Ktok = C.sb("Ktok", [128, NPAIR, 64], BF16); bKtok = Buf()
    Vtok = C.sb("Vtok", [128, NPAIR, 32], BF16); bVtok = Buf()
    U = C.sb("U", [64, NCH, 32], F32); bU = Buf()
    Sd = C.sb("Sd", [64, NCH, 32], BF16); bSd = Buf()
    Dd = C.sb("Dd", [64, NCH], F32); bDd = Buf()
    a1 = C.sb("a1", [64, TBK], F32); ba1 = Buf()
    a2 = C.sb("a2", [64, TBK], F32); ba2 = Buf()
    a3 = C.sb("a3", [64, TBK], F32); ba3 = Buf()
    a4 = C.sb("a4", [64, TBK], F32); ba4 = Buf()
    kf = C.sb("kf", [64, TBK], F32); bkf = Buf()
    qraw = C.sb("qraw", [64, TBK], BF16); bqraw = Buf()
    qs = C.sb("qs", [64, TBK], F32); bqs = Buf()
    Am = [C.sb("Am%d" % i, [128, 512], BF16) for i in range(2)]; bAm = [Buf() for _ in range(2)]
    osb = [C.sb("osb%d" % i, [32, 512], F32) for i in range(2)]; bosb = [Buf() for _ in range(2)]
    nck = TBK // 64
    am_i = [0]
    for di, dr in enumerate(("f", "b")):
        qd, zd, vd, od = din["q" + dr], din["z" + dr], din["v" + dr], din["o" + dr]
        lbc, omlc = lb[:, di:di + 1], oml[:, di:di + 1]
        S.dma("sp", lambda e, vd=vd: e.dma_start(out=Vtok[:], in_=vd[:, :, :]), writes=[bVtok])
        for bk in range(NBK):
            tsl = slice(bk * TBK, (bk + 1) * TBK)
            S.dma("sp", lambda e, zd=zd, tsl=tsl: e.dma_start(out=a1[:], in_=zd[:, tsl]), writes=[ba1])
            S.dma("sp", lambda e, qd=qd, tsl=tsl: e.dma_start(out=qraw[:], in_=qd[:, tsl]), writes=[bqraw])
            S.op("act", lambda e: e.activation(a1[:], a1[:], AF.Sigmoid), reads=[ba1], writes=[ba1])
            S.op("dve", lambda e, omlc=omlc, lbc=lbc: e.tensor_scalar(a1[:], a1[:], omlc, lbc, ALU.mult, ALU.add),
                 reads=[ba1, blb, boml], writes=[ba1])
            S.op("act", lambda e: e.activation(a2[:], a1[:], AF.Ln), reads=[ba1], writes=[ba2])
            S.op("pool", lambda e: e.tensor_scalar(a3[:], a1[:], -1.0, 1.0, ALU.mult, ALU.add), reads=[ba1],
                 writes=[ba3])
            S.op("dve", lambda e: e.tensor_tensor_scan(a4[:], rmask[:], a2[:], 0.0, ALU.mult, ALU.add),
                 reads=[brmask, ba2], writes=[ba4])
            blast = a4[:, 63:TBK:64]
            S.op("act", lambda e, bk=bk, blast=blast: e.activation(Dd[:, bk * nck:(bk + 1) * nck], blast, AF.Exp),
                 reads=[ba4], parts=[bDd])
            S.op("dve", lambda e, blast=blast: e.tensor_tensor(
                a2[:].rearrange("p (c t) -> p c t", t=64), a4[:].rearrange("p (c t) -> p c t", t=64),
                blast[:, :, None].to_broadcast([64, nck, 64]), ALU.subtract), reads=[ba4], writes=[ba2])
            S.op("act", lambda e: e.activation(a1[:], a2[:], AF.Exp), reads=[ba2], writes=[ba1])
            S.op("act", lambda e: e.activation(a4[:], a2[:], AF.Exp, scale=-1.0), reads=[ba2], writes=[ba4])
            S.op("act", lambda e: e.activation(qs[:], qraw[:], AF.Silu), reads=[bqraw], writes=[bqs])
            S.op("dve", lambda e, tsl=tsl: e.tensor_tensor(Qt[:, tsl], qs[:], a1[:], ALU.mult), reads=[bqs, ba1],
                 parts=[bQt])
            S.op("dve", lambda e: e.tensor_tensor(kf[:], a3[:], a4[:], ALU.mult), reads=[ba3, ba4], writes=[bkf])
            S.op("pool", lambda e, tsl=tsl: e.tensor_copy(Kt[:, tsl], kf[:]), reads=[bkf], parts=[bKt])
            ps, bps = C.next_psum()
            for i in range(TBK // 128):
                S.op("pe", lambda e, ps=ps, i=i: e.transpose(ps[:, i * 64:(i + 1) * 64], kf[:, i * 128:(i + 1) * 128],
                                                            identf[:]),
                     reads=[bkf, bidentf], writes=[bps] if i == 0 else (), parts=[bps] if i else (),
                     inc=(i == TBK // 128 - 1))
            pr0 = bk * (TBK // 128)
            S.op("act", lambda e, ps=ps, pr0=pr0: e.activation(
                Ktok[:, pr0:pr0 + TBK // 128, :].rearrange("p a k -> p (a k)"), ps[:, :TBK // 2], AF.Copy),
                reads=[bps], parts=[bKtok])
        for g in range(NCH // 32):
            banks = [C.next_psum(), C.next_psum()]
            for i in range(16):
                for half in range(2):
                    ps, bps = banks[half]
                    pr = g * 16 + i
                    psl = slice(half * 64, half * 64 + 64)
                    S.op("pe", lambda e, ps=ps, i=i, pr=pr, psl=psl: e.matmul(ps[0:64, i * 32:(i + 1) * 32], Ktok[psl, pr, :],
                                                                            Vtok[psl, pr, :], start=True, stop=True),
                         reads=[bKtok, bVtok], writes=[bps] if i == 0 else (), parts=[bps] if i else (), inc=(i == 15))
            for half in range(2):
                ps, bps = banks[half]
                S.op("dve", lambda e, ps=ps, g=g, half=half: e.tensor_copy(
                    U[:, g * 32 + half:g * 32 + 32:2, :], ps[0:64, :].rearrange("p (c v) -> p c v", v=32)),
                    reads=[bps], parts=[bU])
        S.op("pool", lambda e: e.memset(Sd[:, 0, :], 0.0), reads=[bSd], parts=[bSd])
        for v in range(32):
            S.op("dve", lambda e, v=v: e.tensor_tensor_scan(Sd[:, 1:NCH, v], U[:, 0:NCH - 1, v], Dd[:, 1:NCH], 0.0,
                                                           ALU.add, ALU.mult),
                 reads=[bU, bDd], parts=[bSd])
        for g in range(NPAIR // 4):
            ps, bps = C.next_psum()
            for i in range(4):
                pr = g * 4 + i
                tsl = slice(pr * 128, (pr + 1) * 128)
                S.op("pe", lambda e, ps=ps, i=i, tsl=tsl: e.matmul(ps[:, i * 128:(i + 1) * 128], Kt[:, tsl], Qt[:, tsl],
                                                                 start=True, stop=True),
                     reads=[bKt, bQt], writes=[bps] if i == 0 else (), parts=[bps] if i else (), inc=(i == 3))
            ai = am_i[0] % 2
            am_i[0] += 1
            S.op("dve", lambda e, ps=ps, ai=ai: e.tensor_tensor(
                Am[ai][:].rearrange("p (a t) -> p a t", a=4), ps[:, :].rearrange("p (a t) -> p a t", a=4),
                maskT[:, None, :].to_broadcast([128, 4, 128]), ALU.mult), reads=[bps, bmaskT], writes=[bAm[ai]])
            po, bpo = C.next_psum()
            for i in range(4):
                pr = g * 4 + i
                S.op("pe", lambda e, po=po, i=i, pr=pr, ai=ai: e.matmul(po[0:32, i * 128:(i + 1) * 128], Vtok[:, pr, :],
                                                                      Am[ai][:, i * 128:(i + 1) * 128], start=True,
                                                                      stop=False),
                     reads=[bVtok, bAm[ai]], writes=[bpo] if i == 0 else (), parts=[bpo] if i else (), inc=False)
                for h in range(2):
                    c = pr * 2 + h
                    tq = slice(pr * 128 + h * 64, pr * 128 + h * 64 + 64)
                    S.op("pe", lambda e, po=po, i=i, h=h, c=c, tq=tq: e.matmul(
                        po[0:32, i * 128 + h * 64:i * 128 + h * 64 + 64], Sd[:, c, :], Qt[:, tq], start=False,
                        stop=(h == 1)), reads=[bSd, bQt], parts=[bpo], inc=(i == 3 and h == 1))
            S.op("act", lambda e, po=po, ai=ai: e.activation(osb[ai][:], po[0:32, :], AF.Copy), reads=[bpo],
                 writes=[bosb[ai]])
            S.dma("pool", lambda e, od=od, g=g, ai=ai: e.dma_start(out=od[:, g * 512:(g + 1) * 512], in_=osb[ai][:]),
                  reads=[bosb[ai]], is_output=True)
    return C.done()


def hgrn_consts(layer):
    s = np.arange(128)[:, None]
    t = np.arange(128)[None, :]
    maskT = ((s // 64 == t // 64) & (s <= t)).astype(np.float32)
    lbmask = np.zeros((64, 2, 4), np.float32)
    lbmask[:, :, 1:layer + 1] = 1.0
    return {"maskT": maskT.astype(NPBF), "identf": np.eye(64, dtype=np.float32), "lbmask": lbmask}


_PROGS = {}


def _prog(key, fn):
    if key not in _PROGS:
        _PROGS[key] = fn()
    return _PROGS[key]


def _run(nc, in_maps):
    return run_bass_kernel_spmd(nc, in_maps, core_ids=list(range(NCORE))).results


def _nw_layout(w):
    return np.ascontiguousarray(np.asarray(w, np.float32).reshape(8, 128).T)


def kernel(x, positions, attn_norm_w, w_in, hgrn_lower_bounds, w_out, mlp_norm_w, w_up, w_down, final_norm_w):
    x = np.asarray(x, np.float32)[0]
    pos = np.asarray(positions)[0].astype(np.int32)
    depth = w_in.shape[0]
    T = TOK
    cat = np.concatenate
    xT_sh = [np.ascontiguousarray(x[c * T:(c + 1) * T].T) for c in range(NCORE)]
    acon = attn_consts()
    posb = np.ascontiguousarray(np.broadcast_to(pos[None, :], (64, S_LEN)))
    fcon = [fft_consts(hh) for hh in range(2)]
    nc = _prog("proj", lambda: build_dense(False, True, False))
    res = _run(nc, [{"xT": xT_sh[c], "w_in": np.asarray(w_in[0], np.float32), "attn_nw": _nw_layout(attn_norm_w[0])}
                    for c in range(NCORE)])
    out = None
    for layer in range(depth):
        projT = cat([r["projT"] for r in res], axis=1)
        zT = cat([r["zT"] for r in res], axis=1)
        if layer > 0:
            xT_sh = [r["xoT"] for r in res]
        nc = _prog("attn", build_attn)
        ins = []
        for h in range(8):
            v = np.ascontiguousarray(projT[2560 + 64 * h:2560 + 64 * h + 64].T)
            ins.append({"qT": np.ascontiguousarray(projT[1536 + 64 * h:1536 + 64 * h + 64]),
                        "kT": np.ascontiguousarray(projT[2048 + 64 * h:2048 + 64 * h + 64]),
                        "vaug": attn_v_layout(v), "posb": posb, **acon})
        ra = _run(nc, ins)
        ocT = cat([r["oT"] for r in ra], axis=0)
        nc = _prog("fft", build_fft)
        ins = []
        for c in range(8):
            g, hh = c // 2, c % 2
            ins.append({"uT": np.ascontiguousarray(projT[1280 + 64 * g:1280 + 64 * g + 64]), **fcon[hh]})
        rf = _run(nc, ins)
        ob = np.zeros((S_LEN, 256), dtype=projT.dtype)
        for c in range(8):
            g, hh = c // 2, c % 2
            yh = rf[c]["yh"]
            ob.reshape(128, 2, 64, 256)[:, hh, :, 64 * g:64 * g + 64] = yh
        obT = np.ascontiguousarray(ob.T)
        nc = _prog("hgrn", build_hgrn)
        hcon = hgrn_consts(layer)
        ins = []
        lbr = np.asarray(hgrn_lower_bounds, np.float32)
        for c in range(8):
            h, vh = c // 2, c % 2
            q = projT[64 * h:64 * h + 64]
            iv = projT[256 + 64 * h + 32 * vh:256 + 64 * h + 32 * vh + 32]
            zf = zT[64 * h:64 * h + 64]
            zb = zT[256 + 64 * h:256 + 64 * h + 64]

            def vtok(vT):
                return np.ascontiguousarray(vT.T.reshape(S_LEN // 128, 128, 32).transpose(1, 0, 2))
            ins.append({"qT_f": np.ascontiguousarray(q), "zT_f": np.ascontiguousarray(zf), "vtok_f": vtok(iv),
                        "qT_b": np.ascontiguousarray(q[:, ::-1]), "zT_b": np.ascontiguousarray(zb[:, ::-1]),
                        "vtok_b": vtok(iv[:, ::-1]),
                        "lbraw": np.ascontiguousarray(lbr[:, :, 64 * h:64 * h + 64].transpose(2, 1, 0)), **hcon})
        rh = _run(nc, ins)
        oafT = cat([r["oT_f"] for r in rh], axis=0)
        oabT = np.ascontiguousarray(cat([r["oT_b"] for r in rh], axis=0)[:, ::-1])
        gT = projT[1024:1280]
        last = layer == depth - 1
        nc = _prog("tail_final" if last else "tail_proj", lambda: build_dense(True, not last, last))
        ins = []
        for c in range(NCORE):
            sl = slice(c * T, (c + 1) * T)
            d = {"xT": xT_sh[c], "oafT": np.ascontiguousarray(oafT[:, sl]), "oabT": np.ascontiguousarray(oabT[:, sl]),
                 "gT": np.ascontiguousarray(gT[:, sl]), "obT": np.ascontiguousarray(obT[:, sl]),
                 "ocT": np.ascontiguousarray(ocT[:, sl]), "w_out": np.asarray(w_out[layer], np.float32),
                 "w_up": np.asarray(w_up[layer], np.float32), "w_down": np.asarray(w_down[layer], np.float32),
                 "mlp_nw": _nw_layout(mlp_norm_w[layer])}
            if last:
                d["fin_nw"] = _nw_layout(final_norm_w)
            else:
                d["w_in"] = np.asarray(w_in[layer + 1], np.float32)
                d["attn_nw"] = _nw_layout(attn_norm_w[layer + 1])
            ins.append(d)
        res = _run(nc, ins)
        if last:
            out = cat([r["yT"].T for r in res], axis=0)
    return np.ascontiguousarray(out, dtype=np.float32)[None]
```

```python
import numpy as np
import ml_dtypes
from contextlib import ExitStack
import concourse.bass as bass
import concourse.mybir as mybir
from concourse.bass_utils import run_bass_kernel_spmd

F32 = mybir.dt.float32
BF16 = mybir.dt.bfloat16
I32 = mybir.dt.int32
AF = mybir.ActivationFunctionType
ALU = mybir.AluOpType
NPBF = ml_dtypes.bfloat16

S_LEN = 16384
D = 1024
NCORE = 8
TOK = S_LEN // NCORE
EPS = 1e-6


class Buf:
    __slots__ = ("name", "w", "r", "excl")

    def __init__(self, name="", excl=False):
        self.name = name
        self.w = []
        self.r = {}
        self.excl = excl


class Sched:
    ENGS = ("pe", "act", "dve", "pool", "sp")

    def __init__(self, nc, n_dma_sems=32, strict_same=("act", "dve", "pool")):
        self.nc = nc
        self.prog = {e: [] for e in self.ENGS}
        self.cnt = {e: 0 for e in self.ENGS}
        self.known = {e: {} for e in self.ENGS}
        self.strict_same = set(strict_same)
        self.n_dma_sems = n_dma_sems
        self.dma_val = [0] * n_dma_sems
        self.q_range = {"sp": (0, n_dma_sems // 2), "pool": (n_dma_sems // 2, n_dma_sems * 3 // 4),
                        "act": (n_dma_sems * 3 // 4, n_dma_sems)}
        self.dma_rr = {q: r[0] for q, r in self.q_range.items()}
        self.out_events = []

    def _need(self, eng, ev):
        kind, key, val = ev
        if kind == "eng" and key == eng and eng not in self.strict_same:
            return
        k = (kind, key)
        if self.known[eng].get(k, 0) >= val:
            return
        self.known[eng][k] = val
        self.prog[eng].append(("wait", k, val))

    def _deps(self, eng, reads, writes, parts):
        for b in reads:
            for ev in b.w:
                self._need(eng, ev)
            if b.excl:
                for k, ev in b.r.items():
                    if k != eng:
                        self._need(eng, ev)
        for b in writes:
            for ev in b.w:
                self._need(eng, ev)
            for ev in b.r.values():
                self._need(eng, ev)
        for b in parts:
            for ev in b.r.values():
                self._need(eng, ev)

    def _mark(self, rkey, ev, reads, writes, parts):
        for b in reads:
            b.r[rkey] = ev
        for b in writes:
            b.w = [ev]
            b.r = {}
        for b in parts:
            b.w.append(ev)

    def op(self, eng, fn, reads=(), writes=(), parts=(), inc=True):
        self._deps(eng, reads, writes, parts)
        if inc:
            self.cnt[eng] += 1
            ev = ("eng", eng, self.cnt[eng])
        else:
            ev = ("eng", eng, self.cnt[eng] + 1)
        self.prog[eng].append(("op", fn, inc))
        self._mark(eng, ev, reads, writes, parts)
        return ev

    def dma(self, q, fn, reads=(), writes=(), parts=(), is_output=False):
        self._deps(q, reads, writes, parts)
        i = self.dma_rr[q]
        lo, hi = self.q_range[q]
        self.dma_rr[q] = lo + (i + 1 - lo) % (hi - lo)
        if self.dma_val[i] > 0:
            self._need(q, ("dma", i, self.dma_val[i]))
        self.dma_val[i] += 16
        ev = ("dma", i, self.dma_val[i])
        self.prog[q].append(("dma", fn, i))
        self._mark(("dma", i), ev, reads, writes, parts)
        if is_output:
            self.out_events.append(ev)
        return ev

    def finish(self, q="sp"):
        for ev in reversed(self.out_events):
            self._need(q, ev)
        self.out_events = []

    def emit(self):
        nc = self.nc
        with ExitStack() as st:
            esem = {e: st.enter_context(nc.semaphore("s_" + e)) for e in self.ENGS}
            dsem = [st.enter_context(nc.semaphore("d%d" % i)) for i in range(self.n_dma_sems)]
            block = st.enter_context(nc.Block())

            def run(eng_name):
                def body(eng):
                    for item in self.prog[eng_name]:
                        if item[0] == "wait":
                            (kind, key), val = item[1], item[2]
                            eng.wait_ge(esem[key] if kind == "eng" else dsem[key], val)
                        elif item[0] == "op":
                            ins = item[1](eng)
                            if item[2]:
                                ins.then_inc(esem[eng_name], 1)
                        else:
                            item[1](eng).then_inc(dsem[item[2]], 16)
                return body

            block.tensor(run("pe"))
            block.scalar(run("act"))
            block.vector(run("dve"))
            block.gpsimd(run("pool"))
            block.sync(run("sp"))


class Ctx:
    def __init__(self):
        self.nc = bass.Bass("TRN2", target_bir_lowering=False)
        self.S = Sched(self.nc)
        self.st = ExitStack()
        self.psum = []
        self.pb = []
        self.ps_rr = 0

    def sb(self, name, shape, dt):
        return self.st.enter_context(self.nc.sbuf_tensor(name, list(shape), dt))

    def din(self, name, shape, dt):
        return self.nc.dram_tensor(name, list(shape), dt, kind="ExternalInput").ap()

    def dout(self, name, shape, dt):
        return self.nc.dram_tensor(name, list(shape), dt, kind="ExternalOutput").ap()

    def dscratch(self, name, shape, dt):
        return self.nc.dram_tensor(name, list(shape), dt).ap()

    def init_psum(self, n=8):
        for i in range(n):
            self.psum.append(self.st.enter_context(self.nc.psum_tensor("ps%d" % i, [128, 512], F32)))
            self.pb.append(Buf("ps%d" % i, excl=True))

    def next_psum(self):
        i = self.ps_rr
        self.ps_rr = (self.ps_rr + 1) % len(self.psum)
        return self.psum[i], self.pb[i]

    def done(self):
        self.S.finish("sp")
        self.S.emit()
        self.st.close()
        return self.nc


def build_dense(do_tail, do_proj, do_final):
    C = Ctx()
    nc, S = C.nc, C.S
    T, TB = TOK, 512
    NB = T // TB
    xT = C.din("xT", [D, T], F32)
    if do_tail:
        oafT = C.din("oafT", [256, T], F32)
        oabT = C.din("oabT", [256, T], F32)
        gT = C.din("gT", [256, T], BF16)
        obT = C.din("obT", [256, T], BF16)
        ocT = C.din("ocT", [512, T], BF16)
        w_out = C.din("w_out", [1024, 1024], F32)
        w_up = C.din("w_up", [1024, 4096], F32)
        w_down = C.din("w_down", [4096, 1024], F32)
        mlp_nw = C.din("mlp_nw", [128, 8], F32)
        ws_out = C.dscratch("ws_out", [2, 128, 8, 512], BF16)
        ws_up = C.dscratch("ws_up", [8, 128, 8, 512], BF16)
        ws_down = C.dscratch("ws_down", [2, 128, 32, 512], BF16)
        if not do_final:
            xoT = C.dout("xoT", [D, T], F32)
    if do_proj:
        w_in = C.din("w_in", [1024, 3072], F32)
        attn_nw = C.din("attn_nw", [128, 8], F32)
        ws_in = C.dscratch("ws_in", [6, 128, 8, 512], BF16)
        projT = C.dout("projT", [3072, T], BF16)
        zT = C.dout("zT", [512, T], F32)
    if do_final:
        fin_nw = C.din("fin_nw", [128, 8], F32)
        yT = C.dout("yT", [D, T], F32)

    C.init_psum(8)
    ones_t = C.sb("ones_t", [128, 128], BF16); b_ones = Buf()
    blk_t = C.sb("blk_t", [128, 128], BF16); b_blk = Buf()
    S.op("pool", lambda e: e.memset(ones_t[:], 1.0 / 1024.0), writes=[b_ones])
    S.op("pool", lambda e: e.memset(blk_t[:], 0.0), writes=[b_blk])
    S.op("pool", lambda e: e.memset(blk_t[0:64, 0:64], 1.0 / 64.0), writes=[b_blk])
    S.op("pool", lambda e: e.memset(blk_t[64:128, 64:128], 1.0 / 64.0), writes=[b_blk])
    nw_tiles = {}
    for nm, ap_ in (("mlp", mlp_nw if do_tail else None), ("attn", attn_nw if do_proj else None),
                    ("fin", fin_nw if do_final else None)):
        if ap_ is None:
            continue
        t = C.sb("nw_" + nm, [128, 8], F32); b = Buf()
        S.dma("sp", lambda e, t=t, a=ap_: e.dma_start(out=t[:], in_=a[:, :]), writes=[b])
        nw_tiles[nm] = (t, b)

    stg = [C.sb("stg%d" % i, [128, 4, 512], F32) for i in range(2)]
    stg16 = [C.sb("stg16_%d" % i, [128, 4, 512], BF16) for i in range(2)]
    b_stg = [Buf() for _ in range(2)]
    b_stg16 = [Buf() for _ in range(2)]
    conv_i = [0]
    conv_engs = ("dve", "pool", "act")
    conv_plan = []
    conv_done = [0]

    def plan_weight(W, Ws, K, N):
        KC = K // 128
        bWp = [Buf() for _ in range(N // 512)]
        Wv = W.rearrange("(kc p) n -> p kc n", p=128)
        for p in range(N // 512):
            conv_plan.append((Wv, Ws, KC, p, bWp[p]))
        return bWp

    def convert_upto(n):
        while conv_done[0] < min(n, len(conv_plan)):
            Wv, Ws, KC, p, bp = conv_plan[conv_done[0]]
            conv_done[0] += 1
            for q in range(KC // 4):
                i = conv_i[0] % 2
                eng = conv_engs[conv_i[0] % 3]
                conv_i[0] += 1
                S.dma("sp", lambda e, i=i, p=p, q=q, Wv=Wv: e.dma_start(
                    out=stg[i][:], in_=Wv[:, q * 4:(q + 1) * 4, p * 512:(p + 1) * 512]), writes=[b_stg[i]])
                src = stg[i][:].rearrange("p a n -> p (a n)")
                dst = stg16[i][:].rearrange("p a n -> p (a n)")
                if eng == "act":
                    S.op("act", lambda e, src=src, dst=dst: e.activation(dst, src, AF.Copy), reads=[b_stg[i]],
                         writes=[b_stg16[i]])
                else:
                    S.op(eng, lambda e, src=src, dst=dst: e.tensor_copy(dst, src), reads=[b_stg[i]], writes=[b_stg16[i]])
                S.dma("pool", lambda e, i=i, p=p, q=q, Ws=Ws: e.dma_start(out=Ws[p, :, q * 4:(q + 1) * 4, :], in_=stg16[i][:]),
                      reads=[b_stg16[i]], parts=[bp])

    if do_tail:
        b_wout = plan_weight(w_out, ws_out, 1024, 1024)
        b_wup = plan_weight(w_up, ws_up, 1024, 4096)
        b_wdown = plan_weight(w_down, ws_down, 4096, 1024)
    if do_proj:
        b_win = plan_weight(w_in, ws_in, 1024, 3072)
    panel_seq = [0]

    xb = C.sb("xb", [128, 8, TB], F32); bx = [Buf() for _ in range(8)]
    hT = C.sb("hT", [128, 8, TB], BF16); bh = [Buf() for _ in range(8)]
    sq = C.sb("sq", [128, 8, TB], BF16); bsq = Buf()
    rstd = C.sb("rstd", [128, TB], F32); brstd = Buf()
    sd = C.sb("sd", [128, TB], F32); bsd = Buf()
    pan = [C.sb("pan%d" % i, [128, 32, 512], BF16) for i in range(2)]
    bpan = [Buf() for _ in range(2)]
    if do_tail:
        mixT = C.sb("mixT", [128, 8, TB], BF16); bmix = [Buf() for _ in range(8)]
        actT = C.sb("actT", [128, 32, TB], BF16); bact = [Buf() for _ in range(32)]
        oa = C.sb("oa", [128, 2, TB], F32); boa = Buf()
        oa2 = C.sb("oa2", [128, 2, TB], F32); boa2 = Buf()
        gg = C.sb("gg", [128, 2, TB], BF16); bgg = Buf()
        sqa = C.sb("sqa", [128, 2, TB], BF16); bsqa = Buf()
        sg = C.sb("sg", [128, TB], F32); bsg = Buf()
        tmpa = C.sb("tmpa", [128, TB], F32); btmpa = Buf()
        usq = [C.sb("usq%d" % i, [128, TB], F32) for i in range(2)]; busq = [Buf() for _ in range(2)]
    if do_proj:
        ost = [C.sb("ost%d" % i, [128, TB], BF16) for i in range(3)]; bost = [Buf() for _ in range(3)]
        ostf = [C.sb("ostf%d" % i, [128, TB], F32) for i in range(2)]; bostf = [Buf() for _ in range(2)]
    if do_final:
        yst = [C.sb("yst%d" % i, [128, TB], F32) for i in range(2)]; byst = [Buf() for _ in range(2)]

    pan_i = [0]
    cnts = {"usq": 0, "ost": 0, "ostf": 0, "yst": 0}

    def rmsnorm(nw, out_fn):
        nwt, bnw = nw
        S.op("act", lambda e: e.activation(sq[:].rearrange("p a t -> p (a t)"), xb[:].rearrange("p a t -> p (a t)"),
                                           AF.Square), reads=bx, writes=[bsq])
        ps, bps = C.next_psum()
        for kc in range(8):
            S.op("pe", lambda e, kc=kc, ps=ps: e.matmul(ps[:, :TB], ones_t[:], sq[:, kc, :], start=(kc == 0),
                                                      stop=(kc == 7)),
                 reads=[b_ones, bsq], writes=[bps] if kc == 0 else (), parts=[bps] if kc > 0 else (), inc=(kc == 7))
        S.op("act", lambda e, ps=ps: e.activation(sd[:], ps[:, :TB], AF.Sqrt, bias=EPS), reads=[bps], writes=[bsd])
        S.op("dve", lambda e: e.reciprocal(rstd[:], sd[:]), reads=[bsd], writes=[brstd])
        for kc in range(8):
            out_fn(kc, nwt[:, kc:kc + 1], bnw)

    def norm_to_h(nw):
        def out_fn(kc, sc, bnw):
            S.op("dve", lambda e, kc=kc, sc=sc: e.scalar_tensor_tensor(hT[:, kc, :], xb[:, kc, :], sc, rstd[:],
                                                                      ALU.mult, ALU.mult),
                 reads=[bx[kc], brstd, bnw], writes=[bh[kc]])
        rmsnorm(nw, out_fn)

    def matmul_phase(Ws, bW, KC, N, in_tile, in_bufs, evac):
        npan = N // 512
        jobs = list(range(npan))

        def load(p):
            panel_seq[0] += 1
            convert_upto(panel_seq[0] + 3)
            i = pan_i[0] % 2
            pan_i[0] += 1
            S.dma("sp", lambda e, i=i, p=p: e.dma_start(out=pan[i][:, :KC, :], in_=Ws[p, :, :, :]),
                  reads=[bW[p]], writes=[bpan[i]])
            return i
        cur = load(0)
        for p in jobs:
            nxt = load(p + 1) if p + 1 < npan else None
            for jj in range(4):
                j = p * 4 + jj
                ps, bps = C.next_psum()
                for kc in range(KC):
                    S.op("pe", lambda e, ps=ps, cur=cur, kc=kc, jj=jj: e.matmul(
                        ps[:, :TB], pan[cur][:, kc, jj * 128:(jj + 1) * 128], in_tile[:, kc, :],
                        start=(kc == 0), stop=(kc == KC - 1)),
                        reads=[bpan[cur], in_bufs[kc]], writes=[bps] if kc == 0 else (),
                        parts=[bps] if kc > 0 else (), inc=(kc == KC - 1))
                evac(j, ps, bps)
            cur = nxt

    for tb in range(NB):
        tsl = slice(tb * TB, (tb + 1) * TB)
        S.dma("sp", lambda e, tsl=tsl: e.dma_start(out=xb[:], in_=xT.rearrange("(kc p) t -> p kc t", p=128)[:, :, tsl]),
              writes=bx)
        if do_tail:
            S.dma("sp", lambda e, tsl=tsl: e.dma_start(out=oa[:], in_=oafT.rearrange("(kc p) t -> p kc t", p=128)[:, :, tsl]),
                  writes=[boa])
            S.dma("sp", lambda e, tsl=tsl: e.dma_start(out=oa2[:], in_=oabT.rearrange("(kc p) t -> p kc t", p=128)[:, :, tsl]),
                  writes=[boa2])
            S.op("dve", lambda e: e.tensor_tensor(oa[:], oa[:], oa2[:], ALU.add), reads=[boa, boa2], writes=[boa])
            S.dma("sp", lambda e, tsl=tsl: e.dma_start(out=gg[:], in_=gT.rearrange("(kc p) t -> p kc t", p=128)[:, :, tsl]),
                  writes=[bgg])
            S.dma("sp", lambda e, tsl=tsl: e.dma_start(out=mixT[:, 2:4, :],
                                                       in_=obT.rearrange("(kc p) t -> p kc t", p=128)[:, :, tsl]),
                  writes=bmix[2:4])
            S.dma("sp", lambda e, tsl=tsl: e.dma_start(out=mixT[:, 4:8, :],
                                                       in_=ocT.rearrange("(kc p) t -> p kc t", p=128)[:, :, tsl]),
                  writes=bmix[4:8])
            S.op("act", lambda e: e.activation(sqa[:].rearrange("p a t -> p (a t)"), oa[:].rearrange("p a t -> p (a t)"),
                                               AF.Square), reads=[boa], writes=[bsqa])
            for c in range(2):
                ps, bps = C.next_psum()
                S.op("pe", lambda e, ps=ps, c=c: e.matmul(ps[:, :TB], blk_t[:], sqa[:, c, :], start=True, stop=True),
                     reads=[b_blk, bsqa], writes=[bps])
                S.op("act", lambda e, ps=ps: e.activation(sd[:], ps[:, :TB], AF.Sqrt, bias=EPS), reads=[bps], writes=[bsd])
                S.op("dve", lambda e: e.reciprocal(rstd[:], sd[:]), reads=[bsd], writes=[brstd])
                S.op("act", lambda e, c=c: e.activation(sg[:], gg[:, c, :], AF.Silu), reads=[bgg], writes=[bsg])
                S.op("dve", lambda e, c=c: e.tensor_tensor(tmpa[:], oa[:, c, :], rstd[:], ALU.mult),
                     reads=[boa, brstd], writes=[btmpa])
                S.op("dve", lambda e, c=c: e.tensor_tensor(mixT[:, c, :], tmpa[:], sg[:], ALU.mult),
                     reads=[btmpa, bsg], writes=[bmix[c]])

            def evac_res(j, ps, bps):
                S.op("dve", lambda e, j=j, ps=ps: e.tensor_tensor(xb[:, j, :], xb[:, j, :], ps[:, :TB], ALU.add),
                     reads=[bps, bx[j]], writes=[bx[j]])

            def evac_up(j, ps, bps):
                i = cnts["usq"] % 2
                cnts["usq"] += 1
                S.op("act", lambda e, i=i, ps=ps: e.activation(usq[i][:], ps[:, :TB], AF.Square), reads=[bps],
                     writes=[busq[i]])
                S.op("dve", lambda e, i=i, ps=ps, j=j: e.scalar_tensor_tensor(actT[:, j, :], ps[:, :TB], 0.0, usq[i][:],
                                                                            ALU.is_gt, ALU.mult),
                     reads=[bps, busq[i]], writes=[bact[j]])

            matmul_phase(ws_out, b_wout, 8, 1024, mixT, bmix, evac_res)
            norm_to_h(nw_tiles["mlp"])
            matmul_phase(ws_up, b_wup, 8, 4096, hT, bh, evac_up)
            matmul_phase(ws_down, b_wdown, 32, 1024, actT, bact, evac_res)
            if not do_final:
                S.dma("act", lambda e, tsl=tsl: e.dma_start(out=xoT.rearrange("(kc p) t -> p kc t", p=128)[:, :, tsl],
                                                           in_=xb[:]), reads=bx, is_output=True)
        if do_final:
            def out_fn(kc, sc, bnw):
                i = cnts["yst"] % 2
                cnts["yst"] += 1
                S.op("dve", lambda e, kc=kc, sc=sc, i=i: e.scalar_tensor_tensor(yst[i][:], xb[:, kc, :], sc, rstd[:],
                                                                              ALU.mult, ALU.mult),
                     reads=[bx[kc], brstd, bnw], writes=[byst[i]])
                S.dma("act", lambda e, kc=kc, i=i, tsl=tsl: e.dma_start(out=yT[kc * 128:(kc + 1) * 128, tsl], in_=yst[i][:]),
                      reads=[byst[i]], is_output=True)
            rmsnorm(nw_tiles["fin"], out_fn)
        if do_proj:
            norm_to_h(nw_tiles["attn"])

            def evac_proj(j, ps, bps):
                i = cnts["ost"] % 3
                cnts["ost"] += 1
                if 4 <= j < 8:
                    k = cnts["ostf"] % 2
                    cnts["ostf"] += 1
                    S.op("act", lambda e, k=k, ps=ps: e.activation(ostf[k][:], ps[:, :TB], AF.Copy), reads=[bps],
                         writes=[bostf[k]])
                    S.op("dve", lambda e, k=k, i=i: e.tensor_copy(ost[i][:], ostf[k][:]), reads=[bostf[k]],
                         writes=[bost[i]])
                    S.dma("act", lambda e, k=k, j=j, tsl=tsl: e.dma_start(out=zT[(j - 4) * 128:(j - 3) * 128, tsl],
                                                                         in_=ostf[k][:]),
                          reads=[bostf[k]], is_output=True)
                else:
                    S.op("act", lambda e, i=i, ps=ps: e.activation(ost[i][:], ps[:, :TB], AF.Copy), reads=[bps],
                         writes=[bost[i]])
                S.dma("act", lambda e, i=i, j=j, tsl=tsl: e.dma_start(out=projT[j * 128:(j + 1) * 128, tsl], in_=ost[i][:]),
                      reads=[bost[i]], is_output=True)
            matmul_phase(ws_in, b_win, 8, 3072, hT, bh, evac_proj)
    return C.done()


PATTERN_D = (1, 4, 16)
QPAD = 2048
KPAD = 1024
SEG = 4096
TWO_PI = 6.283185307179586
CW1 = 6.28125
CW2 = TWO_PI - CW1
MAGIC = 12582912.0
PI_SAFE = 3.1415925


def attn_tile_base():
    base, b = {}, 0
    for d in PATTERN_D:
        base[d] = b
        b += d * (S_LEN // d // 128 + 1)
    return base, b


def build_attn():
    C = Ctx()
    nc, S = C.nc, C.S
    base, NT = attn_tile_base()
    qT = C.din("qT", [64, S_LEN], BF16)
    kT = C.din("kT", [64, S_LEN], BF16)
    vaug = C.din("vaug", [128, NT, 65], BF16)
    posb = C.din("posb", [64, S_LEN], I32)
    invf = C.din("invf", [64, 1], F32)
    psw_d = C.din("psw", [64, 64], BF16)
    mask_d = C.din("maskab", [128, 256], BF16)
    ident_d = C.din("ident", [128, 128], BF16)
    sel_d = C.din("sel", [65, 64], F32)
    oT = C.dout("oT", [64, S_LEN], BF16)
    C.init_psum(8)

    qTp = C.sb("qTp", [64, QPAD + S_LEN + QPAD], BF16); bq = Buf()
    kTp = C.sb("kTp", [64, KPAD + S_LEN + KPAD], BF16); bk = Buf()
    V = C.sb("V", [128, NT, 65], BF16); bV = Buf()
    acc = C.sb("acc", [65, SEG], F32); bacc = Buf()
    invf_t = C.sb("invf_t", [64, 1], F32); binvf = Buf()
    psw = C.sb("psw_t", [64, 64], BF16); bpsw = Buf()
    maskab = C.sb("mask_t", [128, 256], BF16); bmask = Buf()
    ident = C.sb("ident_t", [128, 128], BF16); bident = Buf()
    sel = C.sb("sel_t", [65, 64], F32); bsel = Buf()
    for t, b, a in ((invf_t, binvf, invf), (psw, bpsw, psw_d), (maskab, bmask, mask_d), (ident, bident, ident_d),
                    (sel, bsel, sel_d)):
        S.dma("sp", lambda e, t=t, a=a: e.dma_start(out=t[:], in_=a[:, :]), writes=[b])
    for i0 in range(0, NT, 81):
        S.dma("sp", lambda e, i0=i0: e.dma_start(out=V[:, i0:i0 + 81, :], in_=vaug[:, i0:i0 + 81, :]), parts=[bV])
    bVo = Buf()
    S.op("pool", lambda e: e.memset(V[:, :, 64:65], 1.0), reads=[bV], writes=[bVo])
    for d in PATTERN_D:
        nt = S_LEN // d // 128 + 1
        S.op("pool", lambda e, d=d, nt=nt: e.memset(V[0:64, base[d]:base[d] + d * nt:nt, 64:65], 0.0), writes=[bVo])
        S.op("pool", lambda e, d=d, nt=nt: e.memset(V[64:128, base[d] + nt - 1:base[d] + d * nt:nt, 64:65], 0.0), writes=[bVo])
    S.op("pool", lambda e: e.memset(qTp[:, 0:QPAD], 0.0), parts=[bq])
    S.op("pool", lambda e: e.memset(qTp[:, QPAD + S_LEN:], 0.0), parts=[bq])
    S.op("pool", lambda e: e.memset(kTp[:, 0:KPAD], 0.0), parts=[bk])
    S.op("pool", lambda e: e.memset(kTp[:, KPAD + S_LEN:], 0.0), parts=[bk])

    CH = 1024
    pos_i = C.sb("pos_i", [64, CH], I32); bpos = Buf()
    ang = C.sb("ang", [64, CH], F32); bang = Buf()
    nn_ = C.sb("nn", [64, CH], F32); bnn = Buf()
    yy = C.sb("yy", [64, CH], F32); byy = Buf()
    sin_t = C.sb("sin_t", [64, CH], F32); bsin = Buf()
    cos_t = C.sb("cos_t", [64, CH], F32); bcos = Buf()
    raw = [C.sb("raw%d" % i, [64, CH], BF16) for i in range(2)]; braw = [Buf() for _ in range(2)]
    t1 = C.sb("t1", [64, CH], F32); bt1 = Buf()
    t2 = C.sb("t2", [64, CH], F32); bt2 = Buf()
    for ch in range(S_LEN // CH):
        csl = slice(ch * CH, (ch + 1) * CH)
        S.dma("sp", lambda e, csl=csl: e.dma_start(out=pos_i[:], in_=posb[:, csl]), writes=[bpos])
        S.op("dve", lambda e: e.tensor_copy(ang[:], pos_i[:]), reads=[bpos], writes=[bang])
        S.op("dve", lambda e: e.tensor_scalar(ang[:], ang[:], invf_t[:, 0:1], None, ALU.mult), reads=[bang, binvf],
             writes=[bang])
        S.op("dve", lambda e: e.tensor_scalar(nn_[:], ang[:], 1.0 / TWO_PI, MAGIC, ALU.mult, ALU.add), reads=[bang],
             writes=[bnn])
        S.op("dve", lambda e: e.tensor_scalar(nn_[:], nn_[:], -MAGIC, None, ALU.add), reads=[bnn], writes=[bnn])
        S.op("dve", lambda e: e.scalar_tensor_tensor(yy[:], nn_[:], -CW1, ang[:], ALU.mult, ALU.add), reads=[bnn, bang],
             writes=[byy])
        S.op("dve", lambda e: e.scalar_tensor_tensor(yy[:], nn_[:], -CW2, yy[:], ALU.mult, ALU.add), reads=[bnn, byy],
             writes=[byy])
        S.op("dve", lambda e: e.tensor_scalar(yy[:], yy[:], PI_SAFE, -PI_SAFE, ALU.min, ALU.max), reads=[byy],
             writes=[byy])
        S.op("act", lambda e: e.activation(sin_t[:], yy[:], AF.Sin), reads=[byy], writes=[bsin])
        S.op("dve", lambda e: e.scalar_tensor_tensor(nn_[:], yy[:], -1.0, yy[:], ALU.mult, ALU.min), reads=[byy],
             writes=[bnn])
        S.op("dve", lambda e: e.tensor_scalar(yy[:], nn_[:], 1.5707963, None, ALU.add), reads=[bnn], writes=[byy])
        S.op("act", lambda e: e.activation(cos_t[:], yy[:], AF.Sin), reads=[byy], writes=[bcos])
        for which, (src, dstt, bdst, pad) in enumerate(((qT, qTp, bq, QPAD), (kT, kTp, bk, KPAD))):
            rw, brw = raw[which], braw[which]
            S.dma("sp", lambda e, rw=rw, src=src, csl=csl: e.dma_start(out=rw[:], in_=src[:, csl]), writes=[brw])
            pss = []
            for b4 in range(CH // 512):
                ps, bps = C.next_psum()
                S.op("pe", lambda e, ps=ps, rw=rw, b4=b4: e.matmul(ps[0:64, :], psw[:], rw[:, b4 * 512:(b4 + 1) * 512],
                                                                 start=True, stop=True),
                     reads=[bpsw, brw], writes=[bps])
                pss.append((ps, bps))
            S.op("dve", lambda e, rw=rw: e.tensor_tensor(t1[:], rw[:], cos_t[:], ALU.mult), reads=[brw, bcos],
                 writes=[bt1])
            for b4 in range(CH // 512):
                ps, bps = pss[b4]
                S.op("dve", lambda e, ps=ps, b4=b4: e.tensor_tensor(t2[:, b4 * 512:(b4 + 1) * 512], ps[0:64, :],
                                                                  sin_t[:, b4 * 512:(b4 + 1) * 512], ALU.mult),
                     reads=[bps, bsin], writes=[bt2] if b4 == 0 else (), parts=[bt2] if b4 else ())
            S.op("pool", lambda e, dstt=dstt, pad=pad, ch=ch: e.tensor_tensor(
                dstt[:, pad + ch * CH:pad + (ch + 1) * CH], t1[:], t2[:], ALU.add), reads=[bt1, bt2], parts=[bdst])

    NP_ = 6
    pending = []
    Pt = [C.sb("P%d" % i, [128, 512], BF16) for i in range(NP_)]; bP = [Buf() for _ in range(NP_)]
    p_rr = [0]
    rec = C.sb("rec", [64, 512], F32); brec = Buf()
    ost = [C.sb("ost%d" % i, [64, SEG], BF16) for i in range(2)]; bost = [Buf() for _ in range(2)]
    for seg in range(S_LEN // SEG):
        for d in PATTERN_D:
            L = S_LEN // d
            nt = L // 128 + 1
            nq = SEG // d // 128
            m0 = nq * seg
            for r in range(d):
                def kcols(mp):
                    st_ = KPAD + r + d * (128 * mp - 64)
                    return slice(st_, st_ + 127 * d + 1, d)

                def qcols(mp):
                    st_ = QPAD + r + d * 128 * (mp - 1)
                    return slice(st_, st_ + 255 * d + 1, d)
                npair = (nq + 2) // 2
                ptiles = {}
                grp = {"out": None}
                for u in range(npair):
                    ps, bps = C.next_psum()
                    slot = p_rr[0] % NP_
                    p_rr[0] += 1
                    ntile = 0
                    for s_ in range(2):
                        rel = 2 * u + s_
                        if rel > nq:
                            break
                        mp = m0 + rel
                        S.op("pe", lambda e, ps=ps, s_=s_, kc=kcols(mp), qc=qcols(mp): e.matmul(
                            ps[:, s_ * 256:(s_ + 1) * 256], kTp[:, kc], qTp[:, qc], start=True, stop=False),
                            reads=[bk, bq], writes=[bps] if s_ == 0 else (), parts=[bps] if s_ else (), inc=False)
                        S.op("pe", lambda e, ps=ps, s_=s_: e.matmul(ps[:, s_ * 256:(s_ + 1) * 256], ident[:], maskab[:],
                                                                   start=False, stop=True),
                             reads=[bident, bmask], parts=[bps], inc=True)
                        ptiles[rel] = (slot, s_)
                        ntile += 1
                    w = ntile * 256
                    S.op("act", lambda e, ps=ps, slot=slot, w=w: e.activation(Pt[slot][:, :w], ps[:, :w], AF.Exp, scale=0.125),
                         reads=[bps], writes=[bP[slot]])
                    for fn in pending:
                        fn()
                    del pending[:]
                    for q in (2 * u - 1, 2 * u):
                        if q < 0 or q >= nq or (q + 1) not in ptiles:
                            continue

                        def pv(q=q, grp=grp, pa=ptiles[q], pb=ptiles[q + 1], d=d, r=r, nq=nq, ia=base[d] + r * nt + m0 + q):
                            wq = q % 4
                            if wq == 0:
                                grp["out"] = C.next_psum()
                            ops, bops = grp["out"]
                            sa, ha = pa
                            sb_, hb = pb
                            S.op("pe", lambda e: e.matmul(
                                ops[0:65, wq * 128:(wq + 1) * 128], V[:, ia, :], Pt[sa][:, ha * 256 + 128:ha * 256 + 256],
                                start=True, stop=False),
                                reads=[bV, bVo, bP[sa]], writes=[bops] if wq == 0 else (), parts=[bops] if wq else (),
                                inc=False)
                            S.op("pe", lambda e: e.matmul(
                                ops[0:65, wq * 128:(wq + 1) * 128], V[:, ia + 1, :], Pt[sb_][:, hb * 256:hb * 256 + 128],
                                start=False, stop=True),
                                reads=[bV, bVo, bP[sb_]], parts=[bops], inc=True)
                            if wq == 3 or q == nq - 1:
                                nqt = wq + 1
                                q0 = q - wq
                                t0 = r + d * 128 * q0
                                asl = slice(t0, t0 + (nqt * 128 - 1) * d + 1, d)
                                if d == 1:
                                    S.op("dve", lambda e: e.tensor_copy(acc[:, asl], ops[0:65, :nqt * 128]),
                                         reads=[bops], parts=[bacc])
                                else:
                                    S.op("dve", lambda e: e.tensor_tensor(acc[:, asl], acc[:, asl], ops[0:65, :nqt * 128],
                                                                          ALU.add),
                                         reads=[bops, bacc], writes=[bacc])
                        pending.append(pv)
        for fn in pending:
            fn()
        del pending[:]
        oi = seg % 2
        for b8 in range(SEG // 512):
            ps, bps = C.next_psum()
            S.op("pe", lambda e, ps=ps, b8=b8: e.matmul(ps[0:64, :], sel[:], acc[:, b8 * 512:(b8 + 1) * 512], start=True,
                                                      stop=True), reads=[bsel, bacc], writes=[bps])
            S.op("dve", lambda e, ps=ps: e.reciprocal(rec[:], ps[0:64, :]), reads=[bps], writes=[brec])
            S.op("dve", lambda e, b8=b8, oi=oi: e.tensor_tensor(ost[oi][:, b8 * 512:(b8 + 1) * 512],
                                                              acc[0:64, b8 * 512:(b8 + 1) * 512], rec[:], ALU.mult),
                 reads=[bacc, brec], writes=[bost[oi]] if b8 == 0 else (), parts=[bost[oi]] if b8 else ())
        S.dma("pool", lambda e, oi=oi, seg=seg: e.dma_start(out=oT[:, seg * SEG:(seg + 1) * SEG], in_=ost[oi][:]),
              reads=[bost[oi]], is_output=True)
    return C.done()


def attn_consts():
    inv = (np.float32(10000.0) ** (-(np.arange(0, 64, 2, dtype=np.float32) / np.float32(64.0)))).astype(np.float32)
    invf = np.concatenate([-inv, inv]).reshape(64, 1).astype(np.float32)
    psw = np.zeros((64, 64), np.float32)
    for c in range(64):
        psw[(c + 32) % 64, c] = 1.0
    p = np.arange(128)[:, None]
    c = np.arange(128)[None, :]
    NEG = -30000.0
    maskab = np.concatenate([np.where(c >= p, 0.0, NEG), np.where(c <= p, 0.0, NEG)], axis=1).astype(np.float32)
    sel = np.zeros((65, 64), np.float32)
    sel[64, :] = 1.0
    return {"invf": invf, "psw": psw.astype(NPBF), "maskab": maskab.astype(NPBF),
            "ident": np.eye(128, dtype=np.float32).astype(NPBF), "sel": sel}


def attn_v_layout(v):
    base, NT = attn_tile_base()
    out = np.zeros((128, NT, 65), dtype=v.dtype)
    p = np.arange(128)
    for d in PATTERN_D:
        L = S_LEN // d
        nt = L // 128 + 1
        for r in range(d):
            for mp in range(nt):
                j = 128 * mp - 64 + p
                ok = (j >= 0) & (j < L)
                tok = r + d * j[ok]
                out[p[ok], base[d] + r * nt + mp, :64] = v[tok]
    return out


def build_fft():
    C = Ctx()
    nc, S = C.nc, C.S
    uT = C.din("uT", [64, S_LEN], BF16)
    cs64_d = C.din("cs64", [64, 128], BF16)
    r1a_d = C.din("r1a", [128, 128], BF16)
    r1b_d = C.din("r1b", [128, 128], BF16)
    c128_d = C.din("c128", [128, 128], BF16)
    s128_d = C.din("s128", [128, 128], BF16)
    tr_d = C.din("tw_r", [128, 64], F32)
    ti_d = C.din("tw_i", [128, 64], F32)
    yh = C.dout("yh", [128, 64, 64], BF16)
    C.init_psum(8)
    u = C.sb("u", [64, S_LEN], BF16); bu = Buf()
    consts = {}
    for nm, a, shp, dt in (("cs64", cs64_d, [64, 128], BF16), ("r1a", r1a_d, [128, 128], BF16),
                           ("r1b", r1b_d, [128, 128], BF16), ("c128", c128_d, [128, 128], BF16),
                           ("s128", s128_d, [128, 128], BF16), ("tr", tr_d, [128, 64], F32), ("ti", ti_d, [128, 64], F32)):
        t = C.sb("k_" + nm, shp, dt); b = Buf()
        S.dma("sp", lambda e, t=t, a=a: e.dma_start(out=t[:], in_=a[:, :]), writes=[b])
        consts[nm] = (t, b)
    for i in range(4):
        S.dma("sp", lambda e, i=i: e.dma_start(out=u[:, i * 4096:(i + 1) * 4096], in_=uT[:, i * 4096:(i + 1) * 4096]),
              parts=[bu])
    Vall = C.sb("Vall", [128, 128, 2, 64], BF16); bVall = Buf()
    Pp = C.sb("Pp", [128, 2, 64, 64], BF16); bPp = Buf()
    ysb = C.sb("ysb", [128, 64, 64], BF16); bys = Buf()
    tmp = [C.sb("ftmp%d" % i, [128, 4, 64], F32) for i in range(4)]; btmp = [Buf() for _ in range(4)]
    cs64, bcs = consts["cs64"]
    for g4 in range(32):
        ps, bps = C.next_psum()
        for i in range(4):
            s1 = g4 * 4 + i
            S.op("pe", lambda e, ps=ps, i=i, s1=s1: e.matmul(ps[:, i * 128:(i + 1) * 128],
                                                           u[:, s1:s1 + 127 * 128 + 1:128], cs64[:], start=True, stop=True),
                 reads=[bu, bcs], writes=[bps] if i == 0 else (), parts=[bps] if i else (), inc=(i == 3))
        dst = Vall[:, g4 * 4:(g4 + 1) * 4, :, :].rearrange("p a r c -> p (a r c)")
        if g4 % 2 == 0:
            S.op("act", lambda e, ps=ps, dst=dst: e.activation(dst, ps[:, :], AF.Copy), reads=[bps], parts=[bVall])
        else:
            S.op("dve", lambda e, ps=ps, dst=dst: e.tensor_copy(dst, ps[:, :]), reads=[bps], parts=[bVall])
    r1a, br1a = consts["r1a"]
    r1b, br1b = consts["r1b"]
    tr, btr = consts["tr"]
    ti, bti = consts["ti"]
    trb = tr[:, None, :].to_broadcast([128, 4, 64])
    tib = ti[:, None, :].to_broadcast([128, 4, 64])
    for g4 in range(16):
        ps, bps = C.next_psum()
        for i in range(4):
            cp = g4 * 4 + i
            S.op("pe", lambda e, ps=ps, i=i, cp=cp: e.matmul(ps[:, i * 128:(i + 1) * 128], Vall[:, :, 0, cp], r1a[:],
                                                           start=True, stop=False),
                 reads=[bVall, br1a], writes=[bps] if i == 0 else (), parts=[bps] if i else (), inc=False)
            S.op("pe", lambda e, ps=ps, i=i, cp=cp: e.matmul(ps[:, i * 128:(i + 1) * 128], Vall[:, :, 1, cp], r1b[:],
                                                           start=False, stop=True),
                 reads=[bVall, br1b], parts=[bps], inc=(i == 3))
        pv = ps[:, :].rearrange("p (c r k) -> p c r k", c=4, r=2)
        pr, pi = pv[:, :, 0, :], pv[:, :, 1, :]
        S.op("dve", lambda e, pr=pr: e.tensor_tensor(tmp[0][:], pr, trb, ALU.mult), reads=[bps, btr], writes=[btmp[0]])
        S.op("dve", lambda e, pi=pi: e.tensor_tensor(tmp[1][:], pi, tib, ALU.mult), reads=[bps, bti], writes=[btmp[1]])
        S.op("dve", lambda e, pr=pr: e.tensor_tensor(tmp[2][:], pr, tib, ALU.mult), reads=[bps, bti], writes=[btmp[2]])
        S.op("dve", lambda e, pi=pi: e.tensor_tensor(tmp[3][:], pi, trb, ALU.mult), reads=[bps, btr], writes=[btmp[3]])
        csl = slice(g4 * 4, (g4 + 1) * 4)
        S.op("pool", lambda e, csl=csl: e.tensor_tensor(Pp[:, 0, csl, :], tmp[0][:], tmp[1][:], ALU.subtract),
             reads=[btmp[0], btmp[1]], parts=[bPp])
        S.op("pool", lambda e, csl=csl: e.tensor_tensor(Pp[:, 1, csl, :], tmp[2][:], tmp[3][:], ALU.add),
             reads=[btmp[2], btmp[3]], parts=[bPp])
    c128, bc128 = consts["c128"]
    s128, bs128 = consts["s128"]
    for b8 in range(8):
        ps, bps = C.next_psum()
        csl = slice(b8 * 8, (b8 + 1) * 8)
        S.op("pe", lambda e, ps=ps, csl=csl: e.matmul(ps[:, :], c128[:], Pp[:, 0, csl, :].rearrange("p c k -> p (c k)"),
                                                    start=True, stop=False), reads=[bPp, bc128], writes=[bps], inc=False)
        S.op("pe", lambda e, ps=ps, csl=csl: e.matmul(ps[:, :], s128[:], Pp[:, 1, csl, :].rearrange("p c k -> p (c k)"),
                                                    start=False, stop=True), reads=[bPp, bs128], parts=[bps])
        S.op("act", lambda e, ps=ps, csl=csl: e.activation(ysb[:, :, csl].rearrange("p k c -> p c k"),
                                                        ps[:, :].rearrange("p (c k) -> p c k", c=8), AF.Copy,
                                                        scale=1.0 / 1024.0), reads=[bps], parts=[bys])
    S.dma("pool", lambda e: e.dma_start(out=yh[:, :, :], in_=ysb[:]), reads=[bys], is_output=True)
    return C.done()


def fft_consts(hh):
    f64 = np.float64
    c = np.arange(64)
    a64 = 2 * np.pi * np.outer(c, c) / 64
    cs64 = np.concatenate([np.cos(a64), -np.sin(a64)], axis=1)
    n = np.arange(128)
    a128 = 2 * np.pi * np.outer(n, n) / 128
    C128, S128 = np.cos(a128), np.sin(a128)
    k2 = np.arange(64) + 64 * hh
    Ch, Sh = C128[:, k2], S128[:, k2]
    r1a = np.concatenate([Ch, -Sh], axis=1)
    r1b = np.concatenate([Sh, Ch], axis=1)
    th = 2 * np.pi * np.outer(n, k2) / S_LEN
    return {"cs64": cs64.astype(np.float32).astype(NPBF), "r1a": r1a.astype(np.float32).astype(NPBF),
            "r1b": r1b.astype(np.float32).astype(NPBF), "c128": C128.astype(np.float32).astype(NPBF),
            "s128": S128.astype(np.float32).astype(NPBF), "tw_r": np.cos(th).astype(np.float32),
            "tw_i": (-np.sin(th)).astype(np.float32)}


def build_hgrn():
    C = Ctx()
    nc, S = C.nc, C.S
    T = S_LEN
    TBK = 1024
    NBK = T // TBK
    NCH = T // 64
    NPAIR = T // 128
    din = {}
    for dr in ("f", "b"):
        din["q" + dr] = C.din("qT_" + dr, [64, T], BF16)
        din["z" + dr] = C.din("zT_" + dr, [64, T], F32)
        din["v" + dr] = C.din("vtok_" + dr, [128, NPAIR, 32], BF16)
        din["o" + dr] = C.dout("oT_" + dr, [32, T], F32)
    lbraw_d = C.din("lbraw", [64, 2, 4], F32)
    lbmask_d = C.din("lbmask", [64, 2, 4], F32)
    maskT_d = C.din("maskT", [128, 128], BF16)
    identf_d = C.din("identf", [64, 64], F32)
    C.init_psum(8)
    lbraw = C.sb("lbraw_t", [64, 2, 4], F32); blbraw = Buf()
    lbmask = C.sb("lbmask_t", [64, 2, 4], F32); blbmask = Buf()
    maskT = C.sb("maskT_t", [128, 128], BF16); bmaskT = Buf()
    identf = C.sb("identf_t", [64, 64], F32); bidentf = Buf()
    for t, b, a in ((lbraw, blbraw, lbraw_d), (lbmask, blbmask, lbmask_d)):
        S.dma("sp", lambda e, t=t, a=a: e.dma_start(out=t[:], in_=a[:, :, :]), writes=[b])
    for t, b, a in ((maskT, bmaskT, maskT_d), (identf, bidentf, identf_d)):
        S.dma("sp", lambda e, t=t, a=a: e.dma_start(out=t[:], in_=a[:, :]), writes=[b])
    lbe = C.sb("lbe", [64, 2, 4], F32); blbe = Buf()
    lbs = C.sb("lbs", [64, 2], F32); blbs = Buf()
    lbn = C.sb("lbn", [64, 2], F32); blbn = Buf()
    lb = C.sb("lb", [64, 2], F32); blb = Buf()
    oml = C.sb("oml", [64, 2], F32); boml = Buf()
    S.op("act", lambda e: e.activation(lbe[:], lbraw[:], AF.Exp), reads=[blbraw], writes=[blbe])
    S.op("dve", lambda e: e.reduce_sum(lbs[:], lbe[:], mybir.AxisListType.X), reads=[blbe], writes=[blbs])
    S.op("dve", lambda e: e.tensor_tensor(lbe[:], lbe[:], lbmask[:], ALU.mult), reads=[blbe, blbmask], writes=[blbe])
    S.op("dve", lambda e: e.reduce_sum(lbn[:], lbe[:], mybir.AxisListType.X), reads=[blbe], writes=[blbn])
    S.op("dve", lambda e: e.reciprocal(lbs[:], lbs[:]), reads=[blbs], writes=[blbs])
    S.op("dve", lambda e: e.tensor_tensor(lb[:], lbn[:], lbs[:], ALU.mult), reads=[blbn, blbs], writes=[blb])
    S.op("dve", lambda e: e.tensor_scalar(oml[:], lb[:], -1.0, 1.0, ALU.mult, ALU.add), reads=[blb], writes=[boml])
    rmask = C.sb("rmask", [64, TBK], F32); brmask = Buf()
    S.op("pool", lambda e: e.memset(rmask[:], 1.0), writes=[brmask])
    S.op("pool", lambda e: e.memset(rmask[:, 0:TBK:64], 0.0), writes=[brmask])
    Qt = C.sb("Qt", [64, T], BF16); bQt = Buf()
    Kt = C.sb("Kt", [64, T], BF16); bKt = Buf()
    Ktok = C.sb("Ktok", [128, NPAIR, 64], BF16); bKtok = Buf()
    Vtok = C.sb("Vtok", [128, NPAIR, 32], BF16); bVtok = Buf()
    U = C.sb("U", [64, NCH, 32], F32); bU = Buf()
    Sd = C.sb("Sd", [64, NCH, 32], BF16); bSd = Buf()
    Dd = C.sb("Dd", [64, NCH], F32); bDd = Buf()
    a1 = C.sb("a1", [64, TBK], F32); ba1 = Buf()
    a2 = C.sb("a2", [64, TBK], F32); ba2 = Buf()
    a3 = C.sb("a3", [64, TBK], F32); ba3 = Buf()
    a4 = C.sb("a4", [64, TBK], F32); ba4 = Buf()
    kf = C.sb("kf", [64, TBK], F32); bkf = Buf()
    qraw = C.sb("qraw", [64, TBK], BF16); bqraw = Buf()
    qs = C.sb("qs", [64, TBK], F32); bqs = Buf()
    Am = [C.sb("Am%d" % i, [128, 512], BF16) for i in range(3)]; bAm = [Buf() for _ in range(3)]
    hpend = []
    osb = [C.sb("osb%d" % i, [32, 512], F32) for i in range(2)]; bosb = [Buf() for _ in range(2)]
    nck = TBK // 64
    am_i = [0]
    for di, dr in enumerate(("f", "b")):
        qd, zd, vd, od = din["q" + dr], din["z" + dr], din["v" + dr], din["o" + dr]
        lbc, omlc = lb[:, di:di + 1], oml[:, di:di + 1]
        S.dma("sp", lambda e, vd=vd: e.dma_start(out=Vtok[:], in_=vd[:, :, :]), writes=[bVtok])
        for bk in range(NBK):
            tsl = slice(bk * TBK, (bk + 1) * TBK)
            S.dma("sp", lambda e, zd=zd, tsl=tsl: e.dma_start(out=a1[:], in_=zd[:, tsl]), writes=[ba1])
            S.dma("sp", lambda e, qd=qd, tsl=tsl: e.dma_start(out=qraw[:], in_=qd[:, tsl]), writes=[bqraw])
            S.op("act", lambda e: e.activation(a1[:], a1[:], AF.Sigmoid), reads=[ba1], writes=[ba1])
            S.op("dve", lambda e, omlc=omlc, lbc=lbc: e.tensor_scalar(a1[:], a1[:], omlc, lbc, ALU.mult, ALU.add),
                 reads=[ba1, blb, boml], writes=[ba1])
            S.op("act", lambda e: e.activation(a2[:], a1[:], AF.Ln), reads=[ba1], writes=[ba2])
            S.op("pool", lambda e: e.tensor_scalar(a3[:], a1[:], -1.0, 1.0, ALU.mult, ALU.add), reads=[ba1],
                 writes=[ba3])
            S.op("dve", lambda e: e.tensor_tensor_scan(a4[:], rmask[:], a2[:], 0.0, ALU.mult, ALU.add),
                 reads=[brmask, ba2], writes=[ba4])
            blast = a4[:, 63:TBK:64]
            S.op("act", lambda e, bk=bk, blast=blast: e.activation(Dd[:, bk * nck:(bk + 1) * nck], blast, AF.Exp),
                 reads=[ba4], parts=[bDd])
            S.op("dve", lambda e, blast=blast: e.tensor_tensor(
                a2[:].rearrange("p (c t) -> p c t", t=64), a4[:].rearrange("p (c t) -> p c t", t=64),
                blast[:, :, None].to_broadcast([64, nck, 64]), ALU.subtract), reads=[ba4], writes=[ba2])
            S.op("act", lambda e: e.activation(a1[:], a2[:], AF.Exp), reads=[ba2], writes=[ba1])
            S.op("act", lambda e: e.activation(a4[:], a2[:], AF.Exp, scale=-1.0), reads=[ba2], writes=[ba4])
            S.op("act", lambda e: e.activation(qs[:], qraw[:], AF.Silu), reads=[bqraw], writes=[bqs])
            S.op("dve", lambda e, tsl=tsl: e.tensor_tensor(Qt[:, tsl], qs[:], a1[:], ALU.mult), reads=[bqs, ba1],
                 parts=[bQt])
            S.op("dve", lambda e: e.tensor_tensor(kf[:], a3[:], a4[:], ALU.mult), reads=[ba3, ba4], writes=[bkf])
            S.op("pool", lambda e, tsl=tsl: e.tensor_copy(Kt[:, tsl], kf[:]), reads=[bkf], parts=[bKt])
            ps, bps = C.next_psum()
            for i in range(TBK // 128):
                S.op("pe", lambda e, ps=ps, i=i: e.transpose(ps[:, i * 64:(i + 1) * 64], kf[:, i * 128:(i + 1) * 128],
                                                            identf[:]),
                     reads=[bkf, bidentf], writes=[bps] if i == 0 else (), parts=[bps] if i else (),
                     inc=(i == TBK // 128 - 1))
            pr0 = bk * (TBK // 128)
            S.op("act", lambda e, ps=ps, pr0=pr0: e.activation(
                Ktok[:, pr0:pr0 + TBK // 128, :].rearrange("p a k -> p (a k)"), ps[:, :TBK // 2], AF.Copy),
                reads=[bps], parts=[bKtok])
        for g in range(NCH // 32):
            banks = [C.next_psum(), C.next_psum()]
            for i in range(16):
                for half in range(2):
                    ps, bps = banks[half]
                    pr = g * 16 + i
                    psl = slice(half * 64, half * 64 + 64)
                    S.op("pe", lambda e, ps=ps, i=i, pr=pr, psl=psl: e.matmul(ps[0:64, i * 32:(i + 1) * 32], Ktok[psl, pr, :],
                                                                            Vtok[psl, pr, :], start=True, stop=True),
                         reads=[bKtok, bVtok], writes=[bps] if i == 0 else (), parts=[bps] if i else (), inc=(i == 15))
            for half in range(2):
                ps, bps = banks[half]
                S.op("dve", lambda e, ps=ps, g=g, half=half: e.tensor_copy(
                    U[:, g * 32 + half:g * 32 + 32:2, :], ps[0:64, :].rearrange("p (c v) -> p c v", v=32)),
                    reads=[bps], parts=[bU])
        S.op("pool", lambda e: e.memset(Sd[:, 0, :], 0.0), reads=[bSd], parts=[bSd])
        for v in range(32):
            S.op("dve", lambda e, v=v: e.tensor_tensor_scan(Sd[:, 1:NCH, v], U[:, 0:NCH - 1, v], Dd[:, 1:NCH], 0.0,
                                                           ALU.add, ALU.mult),
                 reads=[bU, bDd], parts=[bSd])
        for g in range(NPAIR // 4):
            ps, bps = C.next_psum()
            for i in range(4):
                pr = g * 4 + i
                tsl = slice(pr * 128, (pr + 1) * 128)
                S.op("pe", lambda e, ps=ps, i=i, tsl=tsl: e.matmul(ps[:, i * 128:(i + 1) * 128], Kt[:, tsl], Qt[:, tsl],
                                                                 start=True, stop=True),
                     reads=[bKt, bQt], writes=[bps] if i == 0 else (), parts=[bps] if i else (), inc=(i == 3))
            ai = am_i[0] % 3
            am_i[0] += 1
            S.op("dve", lambda e, ps=ps, ai=ai: e.tensor_tensor(
                Am[ai][:].rearrange("p (a t) -> p a t", a=4), ps[:, :].rearrange("p (a t) -> p a t", a=4),
                maskT[:, None, :].to_broadcast([128, 4, 128]), ALU.mult), reads=[bps, bmaskT], writes=[bAm[ai]])
            for fn in hpend:
                fn()
            del hpend[:]

            def outpart(g=g, ai=ai, od=od):
                po, bpo = C.next_psum()
                for i in range(4):
                    pr = g * 4 + i
                    S.op("pe", lambda e, po=po, i=i, pr=pr: e.matmul(po[0:32, i * 128:(i + 1) * 128], Vtok[:, pr, :],
                                                                   Am[ai][:, i * 128:(i + 1) * 128], start=True,
                                                                   stop=False),
                         reads=[bVtok, bAm[ai]], writes=[bpo] if i == 0 else (), parts=[bpo] if i else (), inc=False)
                    for h in range(2):
                        c = pr * 2 + h
                        tq = slice(pr * 128 + h * 64, pr * 128 + h * 64 + 64)
                        S.op("pe", lambda e, po=po, i=i, h=h, c=c, tq=tq: e.matmul(
                            po[0:32, i * 128 + h * 64:i * 128 + h * 64 + 64], Sd[:, c, :], Qt[:, tq], start=False,
                            stop=(h == 1)), reads=[bSd, bQt], parts=[bpo], inc=(i == 3 and h == 1))
                oi = g % 2
                S.op("act", lambda e, po=po: e.activation(osb[oi][:], po[0:32, :], AF.Copy), reads=[bpo],
                     writes=[bosb[oi]])
                S.dma("pool", lambda e: e.dma_start(out=od[:, g * 512:(g + 1) * 512], in_=osb[oi][:]),
                      reads=[bosb[oi]], is_output=True)
            hpend.append(outpart)
        for fn in hpend:
            fn()
        del hpend[:]
    return C.done()


def hgrn_consts(layer):
    s = np.arange(128)[:, None]
    t = np.arange(128)[None, :]
    maskT = ((s // 64 == t // 64) & (s <= t)).astype(np.float32)
    lbmask = np.zeros((64, 2, 4), np.float32)
    lbmask[:, :, 1:layer + 1] = 1.0
    return {"maskT": maskT.astype(NPBF), "identf": np.eye(64, dtype=np.float32), "lbmask": lbmask}


_PROGS = {}


def _prog(key, fn):
    if key not in _PROGS:
        _PROGS[key] = fn()
    return _PROGS[key]


def _run(nc, in_maps):
    return run_bass_kernel_spmd(nc, in_maps, core_ids=list(range(NCORE))).results


def _nw_layout(w):
    return np.ascontiguousarray(np.asarray(w, np.float32).reshape(8, 128).T)


def kernel(x, positions, attn_norm_w, w_in, hgrn_lower_bounds, w_out, mlp_norm_w, w_up, w_down, final_norm_w):
    x = np.asarray(x, np.float32)[0]
    pos = np.asarray(positions)[0].astype(np.int32)
    depth = w_in.shape[0]
    T = TOK
    cat = np.concatenate
    xT_sh = [np.ascontiguousarray(x[c * T:(c + 1) * T].T) for c in range(NCORE)]
    acon = attn_consts()
    posb = np.ascontiguousarray(np.broadcast_to(pos[None, :], (64, S_LEN)))
    fcon = [fft_consts(hh) for hh in range(2)]
    nc = _prog("proj", lambda: build_dense(False, True, False))
    res = _run(nc, [{"xT": xT_sh[c], "w_in": np.asarray(w_in[0], np.float32), "attn_nw": _nw_layout(attn_norm_w[0])}
                    for c in range(NCORE)])
    out = None
    for layer in range(depth):
        projT = cat([r["projT"] for r in res], axis=1)
        zT = cat([r["zT"] for r in res], axis=1)
        if layer > 0:
            xT_sh = [r["xoT"] for r in res]
        nc = _prog("attn", build_attn)
        ins = []
        for h in range(8):
            v = np.ascontiguousarray(projT[2560 + 64 * h:2560 + 64 * h + 64].T)
            ins.append({"qT": np.ascontiguousarray(projT[1536 + 64 * h:1536 + 64 * h + 64]),
                        "kT": np.ascontiguousarray(projT[2048 + 64 * h:2048 + 64 * h + 64]),
                        "vaug": attn_v_layout(v), "posb": posb, **acon})
        ra = _run(nc, ins)
        ocT = cat([r["oT"] for r in ra], axis=0)
        nc = _prog("fft", build_fft)
        ins = []
        for c in range(8):
            g, hh = c // 2, c % 2
            ins.append({"uT": np.ascontiguousarray(projT[1280 + 64 * g:1280 + 64 * g + 64]), **fcon[hh]})
        rf = _run(nc, ins)
        ob = np.zeros((S_LEN, 256), dtype=projT.dtype)
        for c in range(8):
            g, hh = c // 2, c % 2
            yh = rf[c]["yh"]
            ob.reshape(128, 2, 64, 256)[:, hh, :, 64 * g:64 * g + 64] = yh
        obT = np.ascontiguousarray(ob.T)
        nc = _prog("hgrn", build_hgrn)
        hcon = hgrn_consts(layer)
        ins = []
        lbr = np.asarray(hgrn_lower_bounds, np.float32)
        for c in range(8):
            h, vh = c // 2, c % 2
            q = projT[64 * h:64 * h + 64]
            iv = projT[256 + 64 * h + 32 * vh:256 + 64 * h + 32 * vh + 32]
            zf = zT[64 * h:64 * h + 64]
            zb = zT[256 + 64 * h:256 + 64 * h + 64]

            def vtok(vT):
                return np.ascontiguousarray(vT.T.reshape(S_LEN // 128, 128, 32).transpose(1, 0, 2))
            ins.append({"qT_f": np.ascontiguousarray(q), "zT_f": np.ascontiguousarray(zf), "vtok_f": vtok(iv),
                        "qT_b": np.ascontiguousarray(q[:, ::-1]), "zT_b": np.ascontiguousarray(zb[:, ::-1]),
                        "vtok_b": vtok(iv[:, ::-1]),
                        "lbraw": np.ascontiguousarray(lbr[:, :, 64 * h:64 * h + 64].transpose(2, 1, 0)), **hcon})
        rh = _run(nc, ins)
        oafT = cat([r["oT_f"] for r in rh], axis=0)
        oabT = np.ascontiguousarray(cat([r["oT_b"] for r in rh], axis=0)[:, ::-1])
        gT = projT[1024:1280]
        last = layer == depth - 1
        nc = _prog("tail_final" if last else "tail_proj", lambda: build_dense(True, not last, last))
        ins = []
        for c in range(NCORE):
            sl = slice(c * T, (c + 1) * T)
            d = {"xT": xT_sh[c], "oafT": np.ascontiguousarray(oafT[:, sl]), "oabT": np.ascontiguousarray(oabT[:, sl]),
                 "gT": np.ascontiguousarray(gT[:, sl]), "obT": np.ascontiguousarray(obT[:, sl]),
                 "ocT": np.ascontiguousarray(ocT[:, sl]), "w_out": np.asarray(w_out[layer], np.float32),
                 "w_up": np.asarray(w_up[layer], np.float32), "w_down": np.asarray(w_down[layer], np.float32),
                 "mlp_nw": _nw_layout(mlp_norm_w[layer])}
            if last:
                d["fin_nw"] = _nw_layout(final_norm_w)
            else:
                d["w_in"] = np.asarray(w_in[layer + 1], np.float32)
                d["attn_nw"] = _nw_layout(attn_norm_w[layer + 1])
            ins.append(d)
        res = _run(nc, ins)
        if last:
            out = cat([r["yT"].T for r in res], axis=0)
    return np.ascontiguousarray(out, dtype=np.float32)[None]
```

```python
import numpy as np
import ml_dtypes
from contextlib import ExitStack
import concourse.bass as bass
import concourse.mybir as mybir
from concourse.bass_utils import run_bass_kernel_spmd

F32 = mybir.dt.float32
BF16 = mybir.dt.bfloat16
I32 = mybir.dt.int32
AF = mybir.ActivationFunctionType
ALU = mybir.AluOpType
NPBF = ml_dtypes.bfloat16

S_LEN = 16384
D = 1024
NCORE = 8
TOK = S_LEN // NCORE
EPS = 1e-6


class Buf:
    __slots__ = ("name", "w", "r", "excl")

    def __init__(self, name="", excl=False):
        self.name = name
        self.w = []
        self.r = {}
        self.excl = excl


class Sched:
    ENGS = ("pe", "act", "dve", "pool", "sp")

    def __init__(self, nc, n_dma_sems=32, strict_same=("act", "dve", "pool")):
        self.nc = nc
        self.prog = {e: [] for e in self.ENGS}
        self.cnt = {e: 0 for e in self.ENGS}
        self.known = {e: {} for e in self.ENGS}
        self.strict_same = set(strict_same)
        self.n_dma_sems = n_dma_sems
        self.dma_val = [0] * n_dma_sems
        self.q_range = {"sp": (0, n_dma_sems // 2), "pool": (n_dma_sems // 2, n_dma_sems * 3 // 4),
                        "act": (n_dma_sems * 3 // 4, n_dma_sems)}
        self.dma_rr = {q: r[0] for q, r in self.q_range.items()}
        self.out_events = []

    def _need(self, eng, ev):
        kind, key, val = ev
        if kind == "eng" and key == eng and eng not in self.strict_same:
            return
        k = (kind, key)
        if self.known[eng].get(k, 0) >= val:
            return
        self.known[eng][k] = val
        self.prog[eng].append(("wait", k, val))

    def _deps(self, eng, reads, writes, parts):
        for b in reads:
            for ev in b.w:
                self._need(eng, ev)
            if b.excl:
                for k, ev in b.r.items():
                    if k != eng:
                        self._need(eng, ev)
        for b in writes:
            for ev in b.w:
                self._need(eng, ev)
            for ev in b.r.values():
                self._need(eng, ev)
        for b in parts:
            for ev in b.r.values():
                self._need(eng, ev)

    def _mark(self, rkey, ev, reads, writes, parts):
        for b in reads:
            b.r[rkey] = ev
        for b in writes:
            b.w = [ev]
            b.r = {}
        for b in parts:
            b.w.append(ev)

    def op(self, eng, fn, reads=(), writes=(), parts=(), inc=True):
        self._deps(eng, reads, writes, parts)
        if inc:
            self.cnt[eng] += 1
            ev = ("eng", eng, self.cnt[eng])
        else:
            ev = ("eng", eng, self.cnt[eng] + 1)
        self.prog[eng].append(("op", fn, inc))
        self._mark(eng, ev, reads, writes, parts)
        return ev

    def dma(self, q, fn, reads=(), writes=(), parts=(), is_output=False):
        self._deps(q, reads, writes, parts)
        i = self.dma_rr[q]
        lo, hi = self.q_range[q]
        self.dma_rr[q] = lo + (i + 1 - lo) % (hi - lo)
        if self.dma_val[i] > 0:
            self._need(q, ("dma", i, self.dma_val[i]))
        self.dma_val[i] += 16
        ev = ("dma", i, self.dma_val[i])
        self.prog[q].append(("dma", fn, i))
        self._mark(("dma", i), ev, reads, writes, parts)
        if is_output:
            self.out_events.append(ev)
        return ev

    def finish(self, q="sp"):
        for ev in reversed(self.out_events):
            self._need(q, ev)
        self.out_events = []

    def emit(self):
        nc = self.nc
        with ExitStack() as st:
            esem = {e: st.enter_context(nc.semaphore("s_" + e)) for e in self.ENGS}
            dsem = [st.enter_context(nc.semaphore("d%d" % i)) for i in range(self.n_dma_sems)]
            block = st.enter_context(nc.Block())

            def run(eng_name):
                def body(eng):
                    for item in self.prog[eng_name]:
                        if item[0] == "wait":
                            (kind, key), val = item[1], item[2]
                            eng.wait_ge(esem[key] if kind == "eng" else dsem[key], val)
                        elif item[0] == "op":
                            ins = item[1](eng)
                            if item[2]:
                                ins.then_inc(esem[eng_name], 1)
                        else:
                            item[1](eng).then_inc(dsem[item[2]], 16)
                return body

            block.tensor(run("pe"))
            block.scalar(run("act"))
            block.vector(run("dve"))
            block.gpsimd(run("pool"))
            block.sync(run("sp"))


class Ctx:
    def __init__(self):
        self.nc = bass.Bass("TRN2", target_bir_lowering=False)
        self.S = Sched(self.nc)
        self.st = ExitStack()
        self.psum = []
        self.pb = []
        self.ps_rr = 0

    def sb(self, name, shape, dt):
        return self.st.enter_context(self.nc.sbuf_tensor(name, list(shape), dt))

    def din(self, name, shape, dt):
        return self.nc.dram_tensor(name, list(shape), dt, kind="ExternalInput").ap()

    def dout(self, name, shape, dt):
        return self.nc.dram_tensor(name, list(shape), dt, kind="ExternalOutput").ap()

    def dscratch(self, name, shape, dt):
        return self.nc.dram_tensor(name, list(shape), dt).ap()

    def init_psum(self, n=8):
        for i in range(n):
            self.psum.append(self.st.enter_context(self.nc.psum_tensor("ps%d" % i, [128, 512], F32)))
            self.pb.append(Buf("ps%d" % i, excl=True))

    def next_psum(self):
        i = self.ps_rr
        self.ps_rr = (self.ps_rr + 1) % len(self.psum)
        return self.psum[i], self.pb[i]

    def done(self):
        self.S.finish("sp")
        self.S.emit()
        self.st.close()
        return self.nc


def build_dense(do_tail, do_proj, do_final):
    C = Ctx()
    nc, S = C.nc, C.S
    T, TB = TOK, 512
    NB = T // TB
    xT = C.din("xT", [D, T], F32)
    if do_tail:
        oafT = C.din("oafT", [256, T], F32)
        oabT = C.din("oabT", [256, T], F32)
        gT = C.din("gT", [256, T], BF16)
        obT = C.din("obT", [256, T], BF16)
        ocT = C.din("ocT", [512, T], BF16)
        w_out = C.din("w_out", [1024, 1024], F32)
        w_up = C.din("w_up", [1024, 4096], F32)
        w_down = C.din("w_down", [4096, 1024], F32)
        mlp_nw = C.din("mlp_nw", [128, 8], F32)
        ws_out = C.dscratch("ws_out", [2, 128, 8, 512], BF16)
        ws_up = C.dscratch("ws_up", [8, 128, 8, 512], BF16)
        ws_down = C.dscratch("ws_down", [2, 128, 32, 512], BF16)
        if not do_final:
            xoT = C.dout("xoT", [D, T], F32)
    if do_proj:
        w_in = C.din("w_in", [1024, 3072], F32)
        attn_nw = C.din("attn_nw", [128, 8], F32)
        ws_in = C.dscratch("ws_in", [6, 128, 8, 512], BF16)
        projT = C.dout("projT", [3072, T], BF16)
        zT = C.dout("zT", [512, T], F32)
    if do_final:
        fin_nw = C.din("fin_nw", [128, 8], F32)
        yT = C.dout("yT", [D, T], F32)

    C.init_psum(8)
    ones_t = C.sb("ones_t", [128, 128], BF16); b_ones = Buf()
    blk_t = C.sb("blk_t", [128, 128], BF16); b_blk = Buf()
    S.op("pool", lambda e: e.memset(ones_t[:], 1.0 / 1024.0), writes=[b_ones])
    S.op("pool", lambda e: e.memset(blk_t[:], 0.0), writes=[b_blk])
    S.op("pool", lambda e: e.memset(blk_t[0:64, 0:64], 1.0 / 64.0), writes=[b_blk])
    S.op("pool", lambda e: e.memset(blk_t[64:128, 64:128], 1.0 / 64.0), writes=[b_blk])
    nw_tiles = {}
    for nm, ap_ in (("mlp", mlp_nw if do_tail else None), ("attn", attn_nw if do_proj else None),
                    ("fin", fin_nw if do_final else None)):
        if ap_ is None:
            continue
        t = C.sb("nw_" + nm, [128, 8], F32); b = Buf()
        S.dma("sp", lambda e, t=t, a=ap_: e.dma_start(out=t[:], in_=a[:, :]), writes=[b])
        nw_tiles[nm] = (t, b)

    stg = [C.sb("stg%d" % i, [128, 4, 512], F32) for i in range(2)]
    stg16 = [C.sb("stg16_%d" % i, [128, 4, 512], BF16) for i in range(2)]
    b_stg = [Buf() for _ in range(2)]
    b_stg16 = [Buf() for _ in range(2)]
    conv_i = [0]
    conv_engs = ("dve", "pool", "act")
    conv_plan = []
    conv_done = [0]

    def plan_weight(W, Ws, K, N):
        KC = K // 128
        bWp = [Buf() for _ in range(N // 512)]
        Wv = W.rearrange("(kc p) n -> p kc n", p=128)
        for p in range(N // 512):
            conv_plan.append((Wv, Ws, KC, p, bWp[p]))
        return bWp

    def convert_upto(n):
        while conv_done[0] < min(n, len(conv_plan)):
            Wv, Ws, KC, p, bp = conv_plan[conv_done[0]]
            conv_done[0] += 1
            for q in range(KC // 4):
                i = conv_i[0] % 2
                eng = conv_engs[conv_i[0] % 3]
                conv_i[0] += 1
                S.dma("sp", lambda e, i=i, p=p, q=q, Wv=Wv: e.dma_start(
                    out=stg[i][:], in_=Wv[:, q * 4:(q + 1) * 4, p * 512:(p + 1) * 512]), writes=[b_stg[i]])
                src = stg[i][:].rearrange("p a n -> p (a n)")
                dst = stg16[i][:].rearrange("p a n -> p (a n)")
                if eng == "act":
                    S.op("act", lambda e, src=src, dst=dst: e.activation(dst, src, AF.Copy), reads=[b_stg[i]],
                         writes=[b_stg16[i]])
                else:
                    S.op(eng, lambda e, src=src, dst=dst: e.tensor_copy(dst, src), reads=[b_stg[i]], writes=[b_stg16[i]])
                S.dma("pool", lambda e, i=i, p=p, q=q, Ws=Ws: e.dma_start(out=Ws[p, :, q * 4:(q + 1) * 4, :], in_=stg16[i][:]),
                      reads=[b_stg16[i]], parts=[bp])

    if do_tail:
        b_wout = plan_weight(w_out, ws_out, 1024, 1024)
        b_wup = plan_weight(w_up, ws_up, 1024, 4096)
        b_wdown = plan_weight(w_down, ws_down, 4096, 1024)
    if do_proj:
        b_win = plan_weight(w_in, ws_in, 1024, 3072)
    panel_seq = [0]

    xb = C.sb("xb", [128, 8, TB], F32); bx = [Buf() for _ in range(8)]
    hT = C.sb("hT", [128, 8, TB], BF16); bh = [Buf() for _ in range(8)]
    sq = C.sb("sq", [128, 8, TB], BF16); bsq = Buf()
    rstd = C.sb("rstd", [128, TB], F32); brstd = Buf()
    sd = C.sb("sd", [128, TB], F32); bsd = Buf()
    pan = [C.sb("pan%d" % i, [128, 32, 512], BF16) for i in range(2)]
    bpan = [Buf() for _ in range(2)]
    if do_tail:
        mixT = C.sb("mixT", [128, 8, TB], BF16); bmix = [Buf() for _ in range(8)]
        actT = C.sb("actT", [128, 32, TB], BF16); bact = [Buf() for _ in range(32)]
        oa = C.sb("oa", [128, 2, TB], F32); boa = Buf()
        oa2 = C.sb("oa2", [128, 2, TB], F32); boa2 = Buf()
        gg = C.sb("gg", [128, 2, TB], BF16); bgg = Buf()
        sqa = C.sb("sqa", [128, 2, TB], BF16); bsqa = Buf()
        sg = C.sb("sg", [128, TB], F32); bsg = Buf()
        tmpa = C.sb("tmpa", [128, TB], F32); btmpa = Buf()
        usq = [C.sb("usq%d" % i, [128, TB], F32) for i in range(2)]; busq = [Buf() for _ in range(2)]
    if do_proj:
        ost = [C.sb("ost%d" % i, [128, TB], BF16) for i in range(3)]; bost = [Buf() for _ in range(3)]
        ostf = [C.sb("ostf%d" % i, [128, TB], F32) for i in range(2)]; bostf = [Buf() for _ in range(2)]
    if do_final:
        yst = [C.sb("yst%d" % i, [128, TB], F32) for i in range(2)]; byst = [Buf() for _ in range(2)]

    pan_i = [0]
    cnts = {"usq": 0, "ost": 0, "ostf": 0, "yst": 0}

    def rmsnorm(nw, out_fn):
        nwt, bnw = nw
        S.op("act", lambda e: e.activation(sq[:].rearrange("p a t -> p (a t)"), xb[:].rearrange("p a t -> p (a t)"),
                                           AF.Square), reads=bx, writes=[bsq])
        ps, bps = C.next_psum()
        for kc in range(8):
            S.op("pe", lambda e, kc=kc, ps=ps: e.matmul(ps[:, :TB], ones_t[:], sq[:, kc, :], start=(kc == 0),
                                                      stop=(kc == 7)),
                 reads=[b_ones, bsq], writes=[bps] if kc == 0 else (), parts=[bps] if kc > 0 else (), inc=(kc == 7))
        S.op("act", lambda e, ps=ps: e.activation(sd[:], ps[:, :TB], AF.Sqrt, bias=EPS), reads=[bps], writes=[bsd])
        S.op("dve", lambda e: e.reciprocal(rstd[:], sd[:]), reads=[bsd], writes=[brstd])
        for kc in range(8):
            out_fn(kc, nwt[:, kc:kc + 1], bnw)

    def norm_to_h(nw):
        def out_fn(kc, sc, bnw):
            S.op("dve", lambda e, kc=kc, sc=sc: e.scalar_tensor_tensor(hT[:, kc, :], xb[:, kc, :], sc, rstd[:],
                                                                      ALU.mult, ALU.mult),
                 reads=[bx[kc], brstd, bnw], writes=[bh[kc]])
        rmsnorm(nw, out_fn)

    def matmul_phase(Ws, bW, KC, N, in_tile, in_bufs, evac):
        npan = N // 512
        jobs = list(range(npan))

        def load(p):
            panel_seq[0] += 1
            convert_upto(panel_seq[0] + 3)
            i = pan_i[0] % 2
            pan_i[0] += 1
            S.dma("sp", lambda e, i=i, p=p: e.dma_start(out=pan[i][:, :KC, :], in_=Ws[p, :, :, :]),
                  reads=[bW[p]], writes=[bpan[i]])
            return i
        cur = load(0)
        for p in jobs:
            nxt = load(p + 1) if p + 1 < npan else None
            for jj in range(4):
                j = p * 4 + jj
                ps, bps = C.next_psum()
                for kc in range(KC):
                    S.op("pe", lambda e, ps=ps, cur=cur, kc=kc, jj=jj: e.matmul(
                        ps[:, :TB], pan[cur][:, kc, jj * 128:(jj + 1) * 128], in_tile[:, kc, :],
                        start=(kc == 0), stop=(kc == KC - 1)),
                        reads=[bpan[cur], in_bufs[kc]], writes=[bps] if kc == 0 else (),
                        parts=[bps] if kc > 0 else (), inc=(kc == KC - 1))
                evac(j, ps, bps)
            cur = nxt

    for tb in range(NB):
        tsl = slice(tb * TB, (tb + 1) * TB)
        S.dma("sp", lambda e, tsl=tsl: e.dma_start(out=xb[:], in_=xT.rearrange("(kc p) t -> p kc t", p=128)[:, :, tsl]),
              writes=bx)
        if do_tail:
            S.dma("sp", lambda e, tsl=tsl: e.dma_start(out=oa[:], in_=oafT.rearrange("(kc p) t -> p kc t", p=128)[:, :, tsl]),
                  writes=[boa])
            S.dma("sp", lambda e, tsl=tsl: e.dma_start(out=oa2[:], in_=oabT.rearrange("(kc p) t -> p kc t", p=128)[:, :, tsl]),
                  writes=[boa2])
            S.op("dve", lambda e: e.tensor_tensor(oa[:], oa[:], oa2[:], ALU.add), reads=[boa, boa2], writes=[boa])
            S.dma("sp", lambda e, tsl=tsl: e.dma_start(out=gg[:], in_=gT.rearrange("(kc p) t -> p kc t", p=128)[:, :, tsl]),
                  writes=[bgg])
            S.dma("sp", lambda e, tsl=tsl: e.dma_start(out=mixT[:, 2:4, :],
                                                       in_=obT.rearrange("(kc p) t -> p kc t", p=128)[:, :, tsl]),
                  writes=bmix[2:4])
            S.dma("sp", lambda e, tsl=tsl: e.dma_start(out=mixT[:, 4:8, :],
                                                       in_=ocT.rearrange("(kc p) t -> p kc t", p=128)[:, :, tsl]),
                  writes=bmix[4:8])
            S.op("act", lambda e: e.activation(sqa[:].rearrange("p a t -> p (a t)"), oa[:].rearrange("p a t -> p (a t)"),
                                               AF.Square), reads=[boa], writes=[bsqa])
            for c in range(2):
                ps, bps = C.next_psum()
                S.op("pe", lambda e, ps=ps, c=c: e.matmul(ps[:, :TB], blk_t[:], sqa[:, c, :], start=True, stop=True),
                     reads=[b_blk, bsqa], writes=[bps])
                S.op("act", lambda e, ps=ps: e.activation(sd[:], ps[:, :TB], AF.Sqrt, bias=EPS), reads=[bps], writes=[bsd])
                S.op("dve", lambda e: e.reciprocal(rstd[:], sd[:]), reads=[bsd], writes=[brstd])
                S.op("act", lambda e, c=c: e.activation(sg[:], gg[:, c, :], AF.Silu), reads=[bgg], writes=[bsg])
                S.op("dve", lambda e, c=c: e.tensor_tensor(tmpa[:], oa[:, c, :], rstd[:], ALU.mult),
                     reads=[boa, brstd], writes=[btmpa])
                S.op("dve", lambda e, c=c: e.tensor_tensor(mixT[:, c, :], tmpa[:], sg[:], ALU.mult),
                     reads=[btmpa, bsg], writes=[bmix[c]])

            def evac_res(j, ps, bps):
                S.op("dve", lambda e, j=j, ps=ps: e.tensor_tensor(xb[:, j, :], xb[:, j, :], ps[:, :TB], ALU.add),
                     reads=[bps, bx[j]], writes=[bx[j]])

            def evac_up(j, ps, bps):
                i = cnts["usq"] % 2
                cnts["usq"] += 1
                S.op("act", lambda e, i=i, ps=ps: e.activation(usq[i][:], ps[:, :TB], AF.Square), reads=[bps],
                     writes=[busq[i]])
                S.op("dve", lambda e, i=i, ps=ps, j=j: e.scalar_tensor_tensor(actT[:, j, :], ps[:, :TB], 0.0, usq[i][:],
                                                                            ALU.is_gt, ALU.mult),
                     reads=[bps, busq[i]], writes=[bact[j]])

            matmul_phase(ws_out, b_wout, 8, 1024, mixT, bmix, evac_res)
            norm_to_h(nw_tiles["mlp"])
            matmul_phase(ws_up, b_wup, 8, 4096, hT, bh, evac_up)
            matmul_phase(ws_down, b_wdown, 32, 1024, actT, bact, evac_res)
            if not do_final:
                S.dma("act", lambda e, tsl=tsl: e.dma_start(out=xoT.rearrange("(kc p) t -> p kc t", p=128)[:, :, tsl],
                                                           in_=xb[:]), reads=bx, is_output=True)
        if do_final:
            def out_fn(kc, sc, bnw):
                i = cnts["yst"] % 2
                cnts["yst"] += 1
                S.op("dve", lambda e, kc=kc, sc=sc, i=i: e.scalar_tensor_tensor(yst[i][:], xb[:, kc, :], sc, rstd[:],
                                                                              ALU.mult, ALU.mult),
                     reads=[bx[kc], brstd, bnw], writes=[byst[i]])
                S.dma("act", lambda e, kc=kc, i=i, tsl=tsl: e.dma_start(out=yT[kc * 128:(kc + 1) * 128, tsl], in_=yst[i][:]),
                      reads=[byst[i]], is_output=True)
            rmsnorm(nw_tiles["fin"], out_fn)
        if do_proj:
            norm_to_h(nw_tiles["attn"])

            def evac_proj(j, ps, bps):
                i = cnts["ost"] % 3
                cnts["ost"] += 1
                if 4 <= j < 8:
                    k = cnts["ostf"] % 2
                    cnts["ostf"] += 1
                    S.op("act", lambda e, k=k, ps=ps: e.activation(ostf[k][:], ps[:, :TB], AF.Copy), reads=[bps],
                         writes=[bostf[k]])
                    S.op("dve", lambda e, k=k, i=i: e.tensor_copy(ost[i][:], ostf[k][:]), reads=[bostf[k]],
                         writes=[bost[i]])
                    S.dma("act", lambda e, k=k, j=j, tsl=tsl: e.dma_start(out=zT[(j - 4) * 128:(j - 3) * 128, tsl],
                                                                         in_=ostf[k][:]),
                          reads=[bostf[k]], is_output=True)
                else:
                    S.op("act", lambda e, i=i, ps=ps: e.activation(ost[i][:], ps[:, :TB], AF.Copy), reads=[bps],
                         writes=[bost[i]])
                S.dma("act", lambda e, i=i, j=j, tsl=tsl: e.dma_start(out=projT[j * 128:(j + 1) * 128, tsl], in_=ost[i][:]),
                      reads=[bost[i]], is_output=True)
            matmul_phase(ws_in, b_win, 8, 3072, hT, bh, evac_proj)
    return C.done()


PATTERN_D = (1, 4, 16)
QPAD = 2048
KPAD = 1024
SEG = 4096
TWO_PI = 6.283185307179586
CW1 = 6.28125
CW2 = TWO_PI - CW1
MAGIC = 12582912.0
PI_SAFE = 3.1415925


def attn_tile_base():
    base, b = {}, 0
    for d in PATTERN_D:
        base[d] = b
        b += d * (S_LEN // d // 128 + 1)
    return base, b


def build_attn():
    C = Ctx()
    nc, S = C.nc, C.S
    base, NT = attn_tile_base()
    qT = C.din("qT", [64, S_LEN], BF16)
    kT = C.din("kT", [64, S_LEN], BF16)
    vaug = C.din("vaug", [128, NT, 65], BF16)
    posb = C.din("posb", [64, S_LEN], I32)
    invf = C.din("invf", [64, 1], F32)
    psw_d = C.din("psw", [64, 64], BF16)
    mask_d = C.din("maskab", [128, 256], BF16)
    ident_d = C.din("ident", [128, 128], BF16)
    sel_d = C.din("sel", [65, 64], F32)
    oT = C.dout("oT", [64, S_LEN], BF16)
    C.init_psum(8)

    qTp = C.sb("qTp", [64, QPAD + S_LEN + QPAD], BF16); bq = Buf()
    kTp = C.sb("kTp", [64, KPAD + S_LEN + KPAD], BF16); bk = Buf()
    V = C.sb("V", [128, NT, 65], BF16); bV = Buf()
    acc = C.sb("acc", [65, SEG], F32); bacc = Buf()
    invf_t = C.sb("invf_t", [64, 1], F32); binvf = Buf()
    psw = C.sb("psw_t", [64, 64], BF16); bpsw = Buf()
    maskab = C.sb("mask_t", [128, 256], BF16); bmask = Buf()
    ident = C.sb("ident_t", [128, 128], BF16); bident = Buf()
    sel = C.sb("sel_t", [65, 64], F32); bsel = Buf()
    for t, b, a in ((invf_t, binvf, invf), (psw, bpsw, psw_d), (maskab, bmask, mask_d), (ident, bident, ident_d),
                    (sel, bsel, sel_d)):
        S.dma("sp", lambda e, t=t, a=a: e.dma_start(out=t[:], in_=a[:, :]), writes=[b])
    for i0 in range(0, NT, 81):
        S.dma("sp", lambda e, i0=i0: e.dma_start(out=V[:, i0:i0 + 81, :], in_=vaug[:, i0:i0 + 81, :]), parts=[bV])
    bVo = Buf()
    S.op("pool", lambda e: e.memset(V[:, :, 64:65], 1.0), reads=[bV], writes=[bVo])
    for d in PATTERN_D:
        nt = S_LEN // d // 128 + 1
        S.op("pool", lambda e, d=d, nt=nt: e.memset(V[0:64, base[d]:base[d] + d * nt:nt, 64:65], 0.0), writes=[bVo])
        S.op("pool", lambda e, d=d, nt=nt: e.memset(V[64:128, base[d] + nt - 1:base[d] + d * nt:nt, 64:65], 0.0), writes=[bVo])
    S.op("pool", lambda e: e.memset(qTp[:, 0:QPAD], 0.0), parts=[bq])
    S.op("pool", lambda e: e.memset(qTp[:, QPAD + S_LEN:], 0.0), parts=[bq])
    S.op("pool", lambda e: e.memset(kTp[:, 0:KPAD], 0.0), parts=[bk])
    S.op("pool", lambda e: e.memset(kTp[:, KPAD + S_LEN:], 0.0), parts=[bk])

    CH = 1024
    pos_i = C.sb("pos_i", [64, CH], I32); bpos = Buf()
    ang = C.sb("ang", [64, CH], F32); bang = Buf()
    nn_ = C.sb("nn", [64, CH], F32); bnn = Buf()
    yy = C.sb("yy", [64, CH], F32); byy = Buf()
    sin_t = C.sb("sin_t", [64, CH], F32); bsin = Buf()
    cos_t = C.sb("cos_t", [64, CH], F32); bcos = Buf()
    raw = [C.sb("raw%d" % i, [64, CH], BF16) for i in range(2)]; braw = [Buf() for _ in range(2)]
    t1 = C.sb("t1", [64, CH], F32); bt1 = Buf()
    t2 = C.sb("t2", [64, CH], F32); bt2 = Buf()
    for ch in range(S_LEN // CH):
        csl = slice(ch * CH, (ch + 1) * CH)
        S.dma("sp", lambda e, csl=csl: e.dma_start(out=pos_i[:], in_=posb[:, csl]), writes=[bpos])
        S.op("dve", lambda e: e.tensor_copy(ang[:], pos_i[:]), reads=[bpos], writes=[bang])
        S.op("dve", lambda e: e.tensor_scalar(ang[:], ang[:], invf_t[:, 0:1], None, ALU.mult), reads=[bang, binvf],
             writes=[bang])
        S.op("dve", lambda e: e.tensor_scalar(nn_[:], ang[:], 1.0 / TWO_PI, MAGIC, ALU.mult, ALU.add), reads=[bang],
             writes=[bnn])
        S.op("dve", lambda e: e.tensor_scalar(nn_[:], nn_[:], -MAGIC, None, ALU.add), reads=[bnn], writes=[bnn])
        S.op("dve", lambda e: e.scalar_tensor_tensor(yy[:], nn_[:], -CW1, ang[:], ALU.mult, ALU.add), reads=[bnn, bang],
             writes=[byy])
        S.op("dve", lambda e: e.scalar_tensor_tensor(yy[:], nn_[:], -CW2, yy[:], ALU.mult, ALU.add), reads=[bnn, byy],
             writes=[byy])
        S.op("dve", lambda e: e.tensor_scalar(yy[:], yy[:], PI_SAFE, -PI_SAFE, ALU.min, ALU.max), reads=[byy],
             writes=[byy])
        S.op("act", lambda e: e.activation(sin_t[:], yy[:], AF.Sin), reads=[byy], writes=[bsin])
        S.op("dve", lambda e: e.scalar_tensor_tensor(nn_[:], yy[:], -1.0, yy[:], ALU.mult, ALU.min), reads=[byy],
             writes=[bnn])
        S.op("dve", lambda e: e.tensor_scalar(yy[:], nn_[:], 1.5707963, None, ALU.add), reads=[bnn], writes=[byy])
        S.op("act", lambda e: e.activation(cos_t[:], yy[:], AF.Sin), reads=[byy], writes=[bcos])
        for which, (src, dstt, bdst, pad) in enumerate(((qT, qTp, bq, QPAD), (kT, kTp, bk, KPAD))):
            rw, brw = raw[which], braw[which]
            S.dma("sp", lambda e, rw=rw, src=src, csl=csl: e.dma_start(out=rw[:], in_=src[:, csl]), writes=[brw])
            pss = []
            for b4 in range(CH // 512):
                ps, bps = C.next_psum()
                S.op("pe", lambda e, ps=ps, rw=rw, b4=b4: e.matmul(ps[0:64, :], psw[:], rw[:, b4 * 512:(b4 + 1) * 512],
                                                                 start=True, stop=True),
                     reads=[bpsw, brw], writes=[bps])
                pss.append((ps, bps))
            S.op("dve", lambda e, rw=rw: e.tensor_tensor(t1[:], rw[:], cos_t[:], ALU.mult), reads=[brw, bcos],
                 writes=[bt1])
            for b4 in range(CH // 512):
                ps, bps = pss[b4]
                S.op("dve", lambda e, ps=ps, b4=b4: e.tensor_tensor(t2[:, b4 * 512:(b4 + 1) * 512], ps[0:64, :],
                                                                  sin_t[:, b4 * 512:(b4 + 1) * 512], ALU.mult),
                     reads=[bps, bsin], writes=[bt2] if b4 == 0 else (), parts=[bt2] if b4 else ())
            S.op("pool", lambda e, dstt=dstt, pad=pad, ch=ch: e.tensor_tensor(
                dstt[:, pad + ch * CH:pad + (ch + 1) * CH], t1[:], t2[:], ALU.add), reads=[bt1, bt2], parts=[bdst])

    NP_ = 6
    pending = []
    Pt = [C.sb("P%d" % i, [128, 512], BF16) for i in range(NP_)]; bP = [Buf() for _ in range(NP_)]
    p_rr = [0]
    rec = C.sb("rec", [64, 512], F32); brec = Buf()
    ost = [C.sb("ost%d" % i, [64, SEG], BF16) for i in range(2)]; bost = [Buf() for _ in range(2)]
    for seg in range(S_LEN // SEG):
        for d in PATTERN_D:
            L = S_LEN // d
            nt = L // 128 + 1
            nq = SEG // d // 128
            m0 = nq * seg
            for r in range(d):
                def kcols(mp):
                    st_ = KPAD + r + d * (128 * mp - 64)
                    return slice(st_, st_ + 127 * d + 1, d)

                def qcols(mp):
                    st_ = QPAD + r + d * 128 * (mp - 1)
                    return slice(st_, st_ + 255 * d + 1, d)
                npair = (nq + 2) // 2
                ptiles = {}
                grp = {"out": None}
                for u in range(npair):
                    ps, bps = C.next_psum()
                    slot = p_rr[0] % NP_
                    p_rr[0] += 1
                    ntile = 0
                    for s_ in range(2):
                        rel = 2 * u + s_
                        if rel > nq:
                            break
                        mp = m0 + rel
                        S.op("pe", lambda e, ps=ps, s_=s_, kc=kcols(mp), qc=qcols(mp): e.matmul(
                            ps[:, s_ * 256:(s_ + 1) * 256], kTp[:, kc], qTp[:, qc], start=True, stop=False),
                            reads=[bk, bq], writes=[bps] if s_ == 0 else (), parts=[bps] if s_ else (), inc=False)
                        S.op("pe", lambda e, ps=ps, s_=s_: e.matmul(ps[:, s_ * 256:(s_ + 1) * 256], ident[:], maskab[:],
                                                                   start=False, stop=True),
                             reads=[bident, bmask], parts=[bps], inc=True)
                        ptiles[rel] = (slot, s_)
                        ntile += 1
                    w = ntile * 256
                    S.op("act", lambda e, ps=ps, slot=slot, w=w: e.activation(Pt[slot][:, :w], ps[:, :w], AF.Exp, scale=0.125),
                         reads=[bps], writes=[bP[slot]])
                    for fn in pending:
                        fn()
                    del pending[:]
                    for q in (2 * u - 1, 2 * u):
                        if q < 0 or q >= nq or (q + 1) not in ptiles:
                            continue

                        def pv(q=q, grp=grp, pa=ptiles[q], pb=ptiles[q + 1], d=d, r=r, nq=nq, ia=base[d] + r * nt + m0 + q):
                            wq = q % 4
                            if wq == 0:
                                grp["out"] = C.next_psum()
                            ops, bops = grp["out"]
                            sa, ha = pa
                            sb_, hb = pb
                            S.op("pe", lambda e: e.matmul(
                                ops[0:65, wq * 128:(wq + 1) * 128], V[:, ia, :], Pt[sa][:, ha * 256 + 128:ha * 256 + 256],
                                start=True, stop=False),
                                reads=[bV, bVo, bP[sa]], writes=[bops] if wq == 0 else (), parts=[bops] if wq else (),
                                inc=False)
                            S.op("pe", lambda e: e.matmul(
                                ops[0:65, wq * 128:(wq + 1) * 128], V[:, ia + 1, :], Pt[sb_][:, hb * 256:hb * 256 + 128],
                                start=False, stop=True),
                                reads=[bV, bVo, bP[sb_]], parts=[bops], inc=True)
                            if wq == 3 or q == nq - 1:
                                nqt = wq + 1
                                q0 = q - wq
                                t0 = r + d * 128 * q0
                                asl = slice(t0, t0 + (nqt * 128 - 1) * d + 1, d)
                                if d == 1:
                                    S.op("dve", lambda e: e.tensor_copy(acc[:, asl], ops[0:65, :nqt * 128]),
                                         reads=[bops], parts=[bacc])
                                else:
                                    S.op("dve", lambda e: e.tensor_tensor(acc[:, asl], acc[:, asl], ops[0:65, :nqt * 128],
                                                                          ALU.add),
                                         reads=[bops, bacc], writes=[bacc])
                        pending.append(pv)
        for fn in pending:
            fn()
        del pending[:]
        oi = seg % 2
        for b8 in range(SEG // 512):
            ps, bps = C.next_psum()
            S.op("pe", lambda e, ps=ps, b8=b8: e.matmul(ps[0:64, :], sel[:], acc[:, b8 * 512:(b8 + 1) * 512], start=True,
                                                      stop=True), reads=[bsel, bacc], writes=[bps])
            S.op("dve", lambda e, ps=ps: e.reciprocal(rec[:], ps[0:64, :]), reads=[bps], writes=[brec])
            S.op("dve", lambda e, b8=b8, oi=oi: e.tensor_tensor(ost[oi][:, b8 * 512:(b8 + 1) * 512],
                                                              acc[0:64, b8 * 512:(b8 + 1) * 512], rec[:], ALU.mult),
                 reads=[bacc, brec], writes=[bost[oi]] if b8 == 0 else (), parts=[bost[oi]] if b8 else ())
        S.dma("pool", lambda e, oi=oi, seg=seg: e.dma_start(out=oT[:, seg * SEG:(seg + 1) * SEG], in_=ost[oi][:]),
              reads=[bost[oi]], is_output=True)
    return C.done()


def attn_consts():
    inv = (np.float32(10000.0) ** (-(np.arange(0, 64, 2, dtype=np.float32) / np.float32(64.0)))).astype(np.float32)
    invf = np.concatenate([-inv, inv]).reshape(64, 1).astype(np.float32)
    psw = np.zeros((64, 64), np.float32)
    for c in range(64):
        psw[(c + 32) % 64, c] = 1.0
    p = np.arange(128)[:, None]
    c = np.arange(128)[None, :]
    NEG = -30000.0
    maskab = np.concatenate([np.where(c >= p, 0.0, NEG), np.where(c <= p, 0.0, NEG)], axis=1).astype(np.float32)
    sel = np.zeros((65, 64), np.float32)
    sel[64, :] = 1.0
    return {"invf": invf, "psw": psw.astype(NPBF), "maskab": maskab.astype(NPBF),
            "ident": np.eye(128, dtype=np.float32).astype(NPBF), "sel": sel}


def attn_v_layout(v):
    base, NT = attn_tile_base()
    out = np.zeros((128, NT, 65), dtype=v.dtype)
    p = np.arange(128)
    for d in PATTERN_D:
        L = S_LEN // d
        nt = L // 128 + 1
        for r in range(d):
            for mp in range(nt):
                j = 128 * mp - 64 + p
                ok = (j >= 0) & (j < L)
                tok = r + d * j[ok]
                out[p[ok], base[d] + r * nt + mp, :64] = v[tok]
    return out


def build_fft():
    C = Ctx()
    nc, S = C.nc, C.S
    uT = C.din("uT", [64, S_LEN], BF16)
    cs64_d = C.din("cs64", [64, 128], BF16)
    r1a_d = C.din("r1a", [128, 128], BF16)
    r1b_d = C.din("r1b", [128, 128], BF16)
    c128_d = C.din("c128", [128, 128], BF16)
    s128_d = C.din("s128", [128, 128], BF16)
    tr_d = C.din("tw_r", [128, 64], F32)
    ti_d = C.din("tw_i", [128, 64], F32)
    yh = C.dout("yh", [128, 64, 64], BF16)
    C.init_psum(8)
    u = C.sb("u", [64, S_LEN], BF16); bu = Buf()
    consts = {}
    for nm, a, shp, dt in (("cs64", cs64_d, [64, 128], BF16), ("r1a", r1a_d, [128, 128], BF16),
                           ("r1b", r1b_d, [128, 128], BF16), ("c128", c128_d, [128, 128], BF16),
                           ("s128", s128_d, [128, 128], BF16), ("tr", tr_d, [128, 64], F32), ("ti", ti_d, [128, 64], F32)):
        t = C.sb("k_" + nm, shp, dt); b = Buf()
        S.dma("sp", lambda e, t=t, a=a: e.dma_start(out=t[:], in_=a[:, :]), writes=[b])
        consts[nm] = (t, b)
    for i in range(4):
        S.dma("sp", lambda e, i=i: e.dma_start(out=u[:, i * 4096:(i + 1) * 4096], in_=uT[:, i * 4096:(i + 1) * 4096]),
              parts=[bu])
    Vall = C.sb("Vall", [128, 128, 2, 64], BF16); bVall = Buf()
    Pp = C.sb("Pp", [128, 2, 64, 64], BF16); bPp = Buf()
    ysb = C.sb("ysb", [128, 64, 64], BF16); bys = Buf()
    tmp = [C.sb("ftmp%d" % i, [128, 4, 64], F32) for i in range(4)]; btmp = [Buf() for _ in range(4)]
    cs64, bcs = consts["cs64"]
    for g4 in range(32):
        ps, bps = C.next_psum()
        for i in range(4):
            s1 = g4 * 4 + i
            S.op("pe", lambda e, ps=ps, i=i, s1=s1: e.matmul(ps[:, i * 128:(i + 1) * 128],
                                                           u[:, s1:s1 + 127 * 128 + 1:128], cs64[:], start=True, stop=True),
                 reads=[bu, bcs], writes=[bps] if i == 0 else (), parts=[bps] if i else (), inc=(i == 3))
        dst = Vall[:, g4 * 4:(g4 + 1) * 4, :, :].rearrange("p a r c -> p (a r c)")
        if g4 % 2 == 0:
            S.op("act", lambda e, ps=ps, dst=dst: e.activation(dst, ps[:, :], AF.Copy), reads=[bps], parts=[bVall])
        else:
            S.op("dve", lambda e, ps=ps, dst=dst: e.tensor_copy(dst, ps[:, :]), reads=[bps], parts=[bVall])
    r1a, br1a = consts["r1a"]
    r1b, br1b = consts["r1b"]
    tr, btr = consts["tr"]
    ti, bti = consts["ti"]
    trb = tr[:, None, :].to_broadcast([128, 4, 64])
    tib = ti[:, None, :].to_broadcast([128, 4, 64])
    for g4 in range(16):
        ps, bps = C.next_psum()
        for i in range(4):
            cp = g4 * 4 + i
            S.op("pe", lambda e, ps=ps, i=i, cp=cp: e.matmul(ps[:, i * 128:(i + 1) * 128], Vall[:, :, 0, cp], r1a[:],
                                                           start=True, stop=False),
                 reads=[bVall, br1a], writes=[bps] if i == 0 else (), parts=[bps] if i else (), inc=False)
            S.op("pe", lambda e, ps=ps, i=i, cp=cp: e.matmul(ps[:, i * 128:(i + 1) * 128], Vall[:, :, 1, cp], r1b[:],
                                                           start=False, stop=True),
                 reads=[bVall, br1b], parts=[bps], inc=(i == 3))
        pv = ps[:, :].rearrange("p (c r k) -> p c r k", c=4, r=2)
        pr, pi = pv[:, :, 0, :], pv[:, :, 1, :]
        S.op("dve", lambda e, pr=pr: e.tensor_tensor(tmp[0][:], pr, trb, ALU.mult), reads=[bps, btr], writes=[btmp[0]])
        S.op("dve", lambda e, pi=pi: e.tensor_tensor(tmp[1][:], pi, tib, ALU.mult), reads=[bps, bti], writes=[btmp[1]])
        S.op("dve", lambda e, pr=pr: e.tensor_tensor(tmp[2][:], pr, tib, ALU.mult), reads=[bps, bti], writes=[btmp[2]])
        S.op("dve", lambda e, pi=pi: e.tensor_tensor(tmp[3][:], pi, trb, ALU.mult), reads=[bps, btr], writes=[btmp[3]])
        csl = slice(g4 * 4, (g4 + 1) * 4)
        S.op("pool", lambda e, csl=csl: e.tensor_tensor(Pp[:, 0, csl, :], tmp[0][:], tmp[1][:], ALU.subtract),
             reads=[btmp[0], btmp[1]], parts=[bPp])
        S.op("pool", lambda e, csl=csl: e.tensor_tensor(Pp[:, 1, csl, :], tmp[2][:], tmp[3][:], ALU.add),
             reads=[btmp[2], btmp[3]], parts=[bPp])
    c128, bc128 = consts["c128"]
    s128, bs128 = consts["s128"]
    for b8 in range(8):
        ps, bps = C.next_psum()
        csl = slice(b8 * 8, (b8 + 1) * 8)
        S.op("pe", lambda e, ps=ps, csl=csl: e.matmul(ps[:, :], c128[:], Pp[:, 0, csl, :].rearrange("p c k -> p (c k)"),
                                                    start=True, stop=False), reads=[bPp, bc128], writes=[bps], inc=False)
        S.op("pe", lambda e, ps=ps, csl=csl: e.matmul(ps[:, :], s128[:], Pp[:, 1, csl, :].rearrange("p c k -> p (c k)"),
                                                    start=False, stop=True), reads=[bPp, bs128], parts=[bps])
        S.op("act", lambda e, ps=ps, csl=csl: e.activation(ysb[:, :, csl].rearrange("p k c -> p c k"),
                                                        ps[:, :].rearrange("p (c k) -> p c k", c=8), AF.Copy,
                                                        scale=1.0 / 1024.0), reads=[bps], parts=[bys])
    S.dma("pool", lambda e: e.dma_start(out=yh[:, :, :], in_=ysb[:]), reads=[bys], is_output=True)
    return C.done()


def fft_consts(hh):
    f64 = np.float64
    c = np.arange(64)
    a64 = 2 * np.pi * np.outer(c, c) / 64
    cs64 = np.concatenate([np.cos(a64), -np.sin(a64)], axis=1)
    n = np.arange(128)
    a128 = 2 * np.pi * np.outer(n, n) / 128
    C128, S128 = np.cos(a128), np.sin(a128)
    k2 = np.arange(64) + 64 * hh
    Ch, Sh = C128[:, k2], S128[:, k2]
    r1a = np.concatenate([Ch, -Sh], axis=1)
    r1b = np.concatenate([Sh, Ch], axis=1)
    th = 2 * np.pi * np.outer(n, k2) / S_LEN
    return {"cs64": cs64.astype(np.float32).astype(NPBF), "r1a": r1a.astype(np.float32).astype(NPBF),
            "r1b": r1b.astype(np.float32).astype(NPBF), "c128": C128.astype(np.float32).astype(NPBF),
            "s128": S128.astype(np.float32).astype(NPBF), "tw_r": np.cos(th).astype(np.float32),
            "tw_i": (-np.sin(th)).astype(np.float32)}


def build_hgrn():
    C = Ctx()
    nc, S = C.nc, C.S
    T = S_LEN
    TBK = 1024
    NBK = T // TBK
    NCH = T // 64
    NPAIR = T // 128
    din = {}
    for dr in ("f", "b"):
        din["q" + dr] = C.din("qT_" + dr, [64, T], BF16)
        din["z" + dr] = C.din("zT_" + dr, [64, T], F32)
        din["v" + dr] = C.din("vtok_" + dr, [128, NPAIR, 32], BF16)
        din["o" + dr] = C.dout("oT_" + dr, [32, T], F32)
    lbraw_d = C.din("lbraw", [64, 2, 4], F32)
    lbmask_d = C.din("lbmask", [64, 2, 4], F32)
    maskT_d = C.din("maskT", [128, 128], BF16)
    identf_d = C.din("identf", [64, 64], F32)
    C.init_psum(8)
    lbraw = C.sb("lbraw_t", [64, 2, 4], F32); blbraw = Buf()
    lbmask = C.sb("lbmask_t", [64, 2, 4], F32); blbmask = Buf()
    maskT = C.sb("maskT_t", [128, 128], BF16); bmaskT = Buf()
    identf = C.sb("identf_t", [64, 64], F32); bidentf = Buf()
    for t, b, a in ((lbraw, blbraw, lbraw_d), (lbmask, blbmask, lbmask_d)):
        S.dma("sp", lambda e, t=t, a=a: e.dma_start(out=t[:], in_=a[:, :, :]), writes=[b])
    for t, b, a in ((maskT, bmaskT, maskT_d), (identf, bidentf, identf_d)):
        S.dma("sp", lambda e, t=t, a=a: e.dma_start(out=t[:], in_=a[:, :]), writes=[b])
    lbe = C.sb("lbe", [64, 2, 4], F32); blbe = Buf()
    lbs = C.sb("lbs", [64, 2], F32); blbs = Buf()
    lbn = C.sb("lbn", [64, 2], F32); blbn = Buf()
    lb = C.sb("lb", [64, 2], F32); blb = Buf()
    oml = C.sb("oml", [64, 2], F32); boml = Buf()
    S.op("act", lambda e: e.activation(lbe[:], lbraw[:], AF.Exp), reads=[blbraw], writes=[blbe])
    S.op("dve", lambda e: e.reduce_sum(lbs[:], lbe[:], mybir.AxisListType.X), reads=[blbe], writes=[blbs])
    S.op("dve", lambda e: e.tensor_tensor(lbe[:], lbe[:], lbmask[:], ALU.mult), reads=[blbe, blbmask], writes=[blbe])
    S.op("dve", lambda e: e.reduce_sum(lbn[:], lbe[:], mybir.AxisListType.X), reads=[blbe], writes=[blbn])
    S.op("dve", lambda e: e.reciprocal(lbs[:], lbs[:]), reads=[blbs], writes=[blbs])
    S.op("dve", lambda e: e.tensor_tensor(lb[:], lbn[:], lbs[:], ALU.mult), reads=[blbn, blbs], writes=[blb])
    S.op("dve", lambda e: e.tensor_scalar(oml[:], lb[:], -1.0, 1.0, ALU.mult, ALU.add), reads=[blb], writes=[boml])
    rmask = C.sb("rmask", [64, TBK], F32); brmask = Buf()
    S.op("pool", lambda e: e.memset(rmask[:], 1.0), writes=[brmask])
    S.op("pool", lambda e: e.memset(rmask[:, 0:TBK:64], 0.0), writes=[brmask])
    Qt = C.sb("Qt", [64, T], BF16); bQt = Buf()
    Kt = C.sb("Kt", [64, T], BF16); bKt = Buf()
    Ktok = C.sb("Ktok", [128, NPAIR, 64], BF16); bKtok = Buf()
    Vtok = C.sb("Vtok", [128, NPAIR, 32], BF16); bVtok = Buf()
    U = C.sb("U", [64, NCH, 32], F32); bU = Buf()
    Sd = C.sb("Sd", [64, NCH, 32], BF16); bSd = Buf()
    Dd = C.sb("Dd", [64, NCH], F32); bDd = Buf()
    a1s = [C.sb("a1_%d" % i, [64, TBK], F32) for i in range(2)]; ba1s = [Buf() for _ in range(2)]
    a2s = [C.sb("a2_%d" % i, [64, TBK], F32) for i in range(2)]; ba2s = [Buf() for _ in range(2)]
    a3s = [C.sb("a3_%d" % i, [64, TBK], F32) for i in range(2)]; ba3s = [Buf() for _ in range(2)]
    a4s = [C.sb("a4_%d" % i, [64, TBK], F32) for i in range(2)]; ba4s = [Buf() for _ in range(2)]
    kfs = [C.sb("kf_%d" % i, [64, TBK], F32) for i in range(2)]; bkfs = [Buf() for _ in range(2)]
    qraws = [C.sb("qraw_%d" % i, [64, TBK], BF16) for i in range(2)]; bqraws = [Buf() for _ in range(2)]
    qss = [C.sb("qs_%d" % i, [64, TBK], F32) for i in range(2)]; bqss = [Buf() for _ in range(2)]
    Am = [C.sb("Am%d" % i, [128, 512], BF16) for i in range(3)]; bAm = [Buf() for _ in range(3)]
    hpend = []
    osb = [C.sb("osb%d" % i, [32, 512], F32) for i in range(2)]; bosb = [Buf() for _ in range(2)]
    nck = TBK // 64
    am_i = [0]
    for di, dr in enumerate(("f", "b")):
        qd, zd, vd, od = din["q" + dr], din["z" + dr], din["v" + dr], din["o" + dr]
        lbc, omlc = lb[:, di:di + 1], oml[:, di:di + 1]
        S.dma("sp", lambda e, vd=vd: e.dma_start(out=Vtok[:], in_=vd[:, :, :]), writes=[bVtok])
        def _blk(bk, a1, ba1, a2, ba2, a3, ba3, a4, ba4, kf, bkf, qraw, bqraw, qs, bqs, zd=zd, qd=qd, lbc=lbc, omlc=omlc):
            tsl = slice(bk * TBK, (bk + 1) * TBK)
            S.dma("sp", lambda e, zd=zd, tsl=tsl: e.dma_start(out=a1[:], in_=zd[:, tsl]), writes=[ba1])
            S.dma("sp", lambda e, qd=qd, tsl=tsl: e.dma_start(out=qraw[:], in_=qd[:, tsl]), writes=[bqraw])
            S.op("act", lambda e: e.activation(a1[:], a1[:], AF.Sigmoid), reads=[ba1], writes=[ba1])
            S.op("dve", lambda e, omlc=omlc, lbc=lbc: e.tensor_scalar(a1[:], a1[:], omlc, lbc, ALU.mult, ALU.add),
                 reads=[ba1, blb, boml], writes=[ba1])
            S.op("act", lambda e: e.activation(a2[:], a1[:], AF.Ln), reads=[ba1], writes=[ba2])
            S.op("pool", lambda e: e.tensor_scalar(a3[:], a1[:], -1.0, 1.0, ALU.mult, ALU.add), reads=[ba1],
                 writes=[ba3])
            S.op("dve", lambda e: e.tensor_tensor_scan(a4[:], rmask[:], a2[:], 0.0, ALU.mult, ALU.add),
                 reads=[brmask, ba2], writes=[ba4])
            blast = a4[:, 63:TBK:64]
            S.op("act", lambda e, bk=bk, blast=blast: e.activation(Dd[:, bk * nck:(bk + 1) * nck], blast, AF.Exp),
                 reads=[ba4], parts=[bDd])
            S.op("dve", lambda e, blast=blast: e.tensor_tensor(
                a2[:].rearrange("p (c t) -> p c t", t=64), a4[:].rearrange("p (c t) -> p c t", t=64),
                blast[:, :, None].to_broadcast([64, nck, 64]), ALU.subtract), reads=[ba4], writes=[ba2])
            S.op("act", lambda e: e.activation(a1[:], a2[:], AF.Exp), reads=[ba2], writes=[ba1])
            S.op("act", lambda e: e.activation(a4[:], a2[:], AF.Exp, scale=-1.0), reads=[ba2], writes=[ba4])
            S.op("act", lambda e: e.activation(qs[:], qraw[:], AF.Silu), reads=[bqraw], writes=[bqs])
            S.op("dve", lambda e, tsl=tsl: e.tensor_tensor(Qt[:, tsl], qs[:], a1[:], ALU.mult), reads=[bqs, ba1],
                 parts=[bQt])
            S.op("dve", lambda e: e.tensor_tensor(kf[:], a3[:], a4[:], ALU.mult), reads=[ba3, ba4], writes=[bkf])
            S.op("pool", lambda e, tsl=tsl: e.tensor_copy(Kt[:, tsl], kf[:]), reads=[bkf], parts=[bKt])
            ps, bps = C.next_psum()
            for i in range(TBK // 128):
                S.op("pe", lambda e, ps=ps, i=i: e.transpose(ps[:, i * 64:(i + 1) * 64], kf[:, i * 128:(i + 1) * 128],
                                                            identf[:]),
                     reads=[bkf, bidentf], writes=[bps] if i == 0 else (), parts=[bps] if i else (),
                     inc=(i == TBK // 128 - 1))
            pr0 = bk * (TBK // 128)
            S.op("act", lambda e, ps=ps, pr0=pr0: e.activation(
                Ktok[:, pr0:pr0 + TBK // 128, :].rearrange("p a k -> p (a k)"), ps[:, :TBK // 2], AF.Copy),
                reads=[bps], parts=[bKtok])

        for bk in range(NBK):
            _i = bk % 2
            _blk(bk, a1s[_i], ba1s[_i], a2s[_i], ba2s[_i], a3s[_i], ba3s[_i], a4s[_i], ba4s[_i], kfs[_i], bkfs[_i],
                 qraws[_i], bqraws[_i], qss[_i], bqss[_i])
        for g in range(NCH // 32):
            banks = [C.next_psum(), C.next_psum()]
            for i in range(16):
                for half in range(2):
                    ps, bps = banks[half]
                    pr = g * 16 + i
                    psl = slice(half * 64, half * 64 + 64)
                    S.op("pe", lambda e, ps=ps, i=i, pr=pr, psl=psl: e.matmul(ps[0:64, i * 32:(i + 1) * 32], Ktok[psl, pr, :],
                                                                            Vtok[psl, pr, :], start=True, stop=True),
                         reads=[bKtok, bVtok], writes=[bps] if i == 0 else (), parts=[bps] if i else (), inc=(i == 15))
            for half in range(2):
                ps, bps = banks[half]
                S.op("dve", lambda e, ps=ps, g=g, half=half: e.tensor_copy(
                    U[:, g * 32 + half:g * 32 + 32:2, :], ps[0:64, :].rearrange("p (c v) -> p c v", v=32)),
                    reads=[bps], parts=[bU])
        S.op("pool", lambda e: e.memset(Sd[:, 0, :], 0.0), reads=[bSd], parts=[bSd])
        for v in range(32):
            S.op("dve", lambda e, v=v: e.tensor_tensor_scan(Sd[:, 1:NCH, v], U[:, 0:NCH - 1, v], Dd[:, 1:NCH], 0.0,
                                                           ALU.add, ALU.mult),
                 reads=[bU, bDd], parts=[bSd])
        for g in range(NPAIR // 4):
            ps, bps = C.next_psum()
            for i in range(4):
                pr = g * 4 + i
                tsl = slice(pr * 128, (pr + 1) * 128)
                S.op("pe", lambda e, ps=ps, i=i, tsl=tsl: e.matmul(ps[:, i * 128:(i + 1) * 128], Kt[:, tsl], Qt[:, tsl],
                                                                 start=True, stop=True),
                     reads=[bKt, bQt], writes=[bps] if i == 0 else (), parts=[bps] if i else (), inc=(i == 3))
            ai = am_i[0] % 3
            am_i[0] += 1
            S.op("dve", lambda e, ps=ps, ai=ai: e.tensor_tensor(
                Am[ai][:].rearrange("p (a t) -> p a t", a=4), ps[:, :].rearrange("p (a t) -> p a t", a=4),
                maskT[:, None, :].to_broadcast([128, 4, 128]), ALU.mult), reads=[bps, bmaskT], writes=[bAm[ai]])
            for fn in hpend:
                fn()
            del hpend[:]

            def outpart(g=g, ai=ai, od=od):
                po, bpo = C.next_psum()
                for i in range(4):
                    pr = g * 4 + i
                    S.op("pe", lambda e, po=po, i=i, pr=pr: e.matmul(po[0:32, i * 128:(i + 1) * 128], Vtok[:, pr, :],
                                                                   Am[ai][:, i * 128:(i + 1) * 128], start=True,
                                                                   stop=False),
                         reads=[bVtok, bAm[ai]], writes=[bpo] if i == 0 else (), parts=[bpo] if i else (), inc=False)
                    for h in range(2):
                        c = pr * 2 + h
                        tq = slice(pr * 128 + h * 64, pr * 128 + h * 64 + 64)
                        S.op("pe", lambda e, po=po, i=i, h=h, c=c, tq=tq: e.matmul(
                            po[0:32, i * 128 + h * 64:i * 128 + h * 64 + 64], Sd[:, c, :], Qt[:, tq], start=False,
                            stop=(h == 1)), reads=[bSd, bQt], parts=[bpo], inc=(i == 3 and h == 1))
                oi = g % 2
                S.op("act", lambda e, po=po: e.activation(osb[oi][:], po[0:32, :], AF.Copy), reads=[bpo],
                     writes=[bosb[oi]])
                S.dma("pool", lambda e: e.dma_start(out=od[:, g * 512:(g + 1) * 512], in_=osb[oi][:]),
                      reads=[bosb[oi]], is_output=True)
            hpend.append(outpart)
        for fn in hpend:
            fn()
        del hpend[:]
    return C.done()


def hgrn_consts(layer):
    s = np.arange(128)[:, None]
    t = np.arange(128)[None, :]
    maskT = ((s // 64 == t // 64) & (s <= t)).astype(np.float32)
    lbmask = np.zeros((64, 2, 4), np.float32)
    lbmask[:, :, 1:layer + 1] = 1.0
    return {"maskT": maskT.astype(NPBF), "identf": np.eye(64, dtype=np.float32), "lbmask": lbmask}


_PROGS = {}


def _prog(key, fn):
    if key not in _PROGS:
        _PROGS[key] = fn()
    return _PROGS[key]


def _run(nc, in_maps):
    return run_bass_kernel_spmd(nc, in_maps, core_ids=list(range(NCORE))).results


def _nw_layout(w):
    return np.ascontiguousarray(np.asarray(w, np.float32).reshape(8, 128).T)


def kernel(x, positions, attn_norm_w, w_in, hgrn_lower_bounds, w_out, mlp_norm_w, w_up, w_down, final_norm_w):
    x = np.asarray(x, np.float32)[0]
    pos = np.asarray(positions)[0].astype(np.int32)
    depth = w_in.shape[0]
    T = TOK
    cat = np.concatenate
    xT_sh = [np.ascontiguousarray(x[c * T:(c + 1) * T].T) for c in range(NCORE)]
    acon = attn_consts()
    posb = np.ascontiguousarray(np.broadcast_to(pos[None, :], (64, S_LEN)))
    fcon = [fft_consts(hh) for hh in range(2)]
    nc = _prog("proj", lambda: build_dense(False, True, False))
    res = _run(nc, [{"xT": xT_sh[c], "w_in": np.asarray(w_in[0], np.float32), "attn_nw": _nw_layout(attn_norm_w[0])}
                    for c in range(NCORE)])
    out = None
    for layer in range(depth):
        projT = cat([r["projT"] for r in res], axis=1)
        zT = cat([r["zT"] for r in res], axis=1)
        if layer > 0:
            xT_sh = [r["xoT"] for r in res]
        nc = _prog("attn", build_attn)
        ins = []
        for h in range(8):
            v = np.ascontiguousarray(projT[2560 + 64 * h:2560 + 64 * h + 64].T)
            ins.append({"qT": np.ascontiguousarray(projT[1536 + 64 * h:1536 + 64 * h + 64]),
                        "kT": np.ascontiguousarray(projT[2048 + 64 * h:2048 + 64 * h + 64]),
                        "vaug": attn_v_layout(v), "posb": posb, **acon})
        ra = _run(nc, ins)
        ocT = cat([r["oT"] for r in ra], axis=0)
        nc = _prog("fft", build_fft)
        ins = []
        for c in range(8):
            g, hh = c // 2, c % 2
            ins.append({"uT": np.ascontiguousarray(projT[1280 + 64 * g:1280 + 64 * g + 64]), **fcon[hh]})
        rf = _run(nc, ins)
        ob = np.zeros((S_LEN, 256), dtype=projT.dtype)
        for c in range(8):
            g, hh = c // 2, c % 2
            yh = rf[c]["yh"]
            ob.reshape(128, 2, 64, 256)[:, hh, :, 64 * g:64 * g + 64] = yh
        obT = np.ascontiguousarray(ob.T)
        nc = _prog("hgrn", build_hgrn)
        hcon = hgrn_consts(layer)
        ins = []
        lbr = np.asarray(hgrn_lower_bounds, np.float32)
        for c in range(8):
            h, vh = c // 2, c % 2
            q = projT[64 * h:64 * h + 64]
            iv = projT[256 + 64 * h + 32 * vh:256 + 64 * h + 32 * vh + 32]
            zf = zT[64 * h:64 * h + 64]
            zb = zT[256 + 64 * h:256 + 64 * h + 64]

            def vtok(vT):
                return np.ascontiguousarray(vT.T.reshape(S_LEN // 128, 128, 32).transpose(1, 0, 2))
            ins.append({"qT_f": np.ascontiguousarray(q), "zT_f": np.ascontiguousarray(zf), "vtok_f": vtok(iv),
                        "qT_b": np.ascontiguousarray(q[:, ::-1]), "zT_b": np.ascontiguousarray(zb[:, ::-1]),
                        "vtok_b": vtok(iv[:, ::-1]),
                        "lbraw": np.ascontiguousarray(lbr[:, :, 64 * h:64 * h + 64].transpose(2, 1, 0)), **hcon})
        rh = _run(nc, ins)
        oafT = cat([r["oT_f"] for r in rh], axis=0)
        oabT = np.ascontiguousarray(cat([r["oT_b"] for r in rh], axis=0)[:, ::-1])
        gT = projT[1024:1280]
        last = layer == depth - 1
        nc = _prog("tail_final" if last else "tail_proj", lambda: build_dense(True, not last, last))
        ins = []
        for c in range(NCORE):
            sl = slice(c * T, (c + 1) * T)
            d = {"xT": xT_sh[c], "oafT": np.ascontiguousarray(oafT[:, sl]), "oabT": np.ascontiguousarray(oabT[:, sl]),
                 "gT": np.ascontiguousarray(gT[:, sl]), "obT": np.ascontiguousarray(obT[:, sl]),
                 "ocT": np.ascontiguousarray(ocT[:, sl]), "w_out": np.asarray(w_out[layer], np.float32),
                 "w_up": np.asarray(w_up[layer], np.float32), "w_down": np.asarray(w_down[layer], np.float32),
                 "mlp_nw": _nw_layout(mlp_norm_w[layer])}
            if last:
                d["fin_nw"] = _nw_layout(final_norm_w)
            else:
                d["w_in"] = np.asarray(w_in[layer + 1], np.float32)
                d["attn_nw"] = _nw_layout(attn_norm_w[layer + 1])
            ins.append(d)
        res = _run(nc, ins)
        if last:
            out = cat([r["yT"].T for r in res], axis=0)
    return np.ascontiguousarray(out, dtype=np.float32)[None]
```
